# Optimizing a Trainium2 kernel written in Bass

```python
import math
import jax, jax.numpy as jnp
from jax import lax
import numpy as np

D_MODEL = 1024
BATCH = 16
SEQ = 2048
DEPTH = 2
DEC_BATCH = 32
DEC_SEQ = 4
PAST_LEN = 16384
PAGE_SIZE = 128

N_MIXERS = 2
N_ATTN_LAYERS = (DEPTH + 1) // 2
N_LRU_LAYERS = DEPTH // 2

N_HEADS = 8
N_KV_HEADS = 2
HEAD_DIM = D_MODEL // N_HEADS
ATTN_WIDTH = N_HEADS * HEAD_DIM
KV_WIDTH = N_KV_HEADS * HEAD_DIM
N_IDX_HEADS = 16
IDX_DIM = 64
TOPK_MAX = 256
ROPE_THETA = 500000.0
ROT_FRAC = 4
Q_BLOCK = 128
ATTN_SPLITS = (ATTN_WIDTH, KV_WIDTH, KV_WIDTH, N_IDX_HEADS * IDX_DIM, IDX_DIM, N_IDX_HEADS, ATTN_WIDTH)
IN_A = sum(ATTN_SPLITS)

LRU_WIDTH = D_MODEL
N_LRU_BLOCKS = 4
LRU_BLOCK = LRU_WIDTH // N_LRU_BLOCKS
CONV_W = 4
LRU_C = 8.0

EPS = 1e-6
POOL_SLACK = 1.25

kernel_name = "hybrid_dsa_rglru_adaln_step"

F32 = jnp.float32


def rmsnorm(x, g):
    xf = x.astype(F32)
    y = xf * lax.rsqrt(jnp.mean(xf * xf, axis=-1, keepdims=True) + EPS)
    return (y * g.astype(F32)).astype(x.dtype)


def modulate(x, c, g, aw, ab):
    m = jax.nn.silu(c) @ aw + ab
    shift, scale, gate = jnp.split(m, 3, axis=-1)
    h = rmsnorm(x, g) * (1 + scale[:, None]) + shift[:, None]
    return h, gate[:, None]


def rope(x, pos):
    d = x.shape[-1]
    r = d // ROT_FRAC
    half = r // 2
    inv = jnp.exp(-math.log(ROPE_THETA) * jnp.arange(half, dtype=F32) * 2.0 / r)
    ang = pos.astype(F32)[:, None] * inv[None, :]
    cos = jnp.cos(ang)[:, None, :]
    sin = jnp.sin(ang)[:, None, :]
    xf = x.astype(F32)
    x1 = xf[..., :half]
    x2 = xf[..., half:r]
    out = jnp.concatenate([x1 * cos - x2 * sin, x2 * cos + x1 * sin, xf[..., r:]], axis=-1)
    return out.astype(x.dtype)


def attn_project(h, pos, w_in, q_norm, k_norm):
    B, T, _ = h.shape
    z = h @ w_in
    offs = [int(o) for o in np.cumsum(ATTN_SPLITS)[:-1]]
    q, k, v, qi, ki, wi, g = jnp.split(z, offs, axis=-1)
    q = rope(rmsnorm(q.reshape(B, T, N_HEADS, HEAD_DIM), q_norm), pos)
    k = rope(rmsnorm(k.reshape(B, T, N_KV_HEADS, HEAD_DIM), k_norm), pos)
    v = v.reshape(B, T, N_KV_HEADS, HEAD_DIM)
    qi = rope(qi.reshape(B, T, N_IDX_HEADS, IDX_DIM), pos)
    ki = rope(ki.reshape(B, T, 1, IDX_DIM), pos)[:, :, 0]
    wi = wi * (N_IDX_HEADS ** -0.5 * IDX_DIM ** -0.5)
    return q, k, v, qi, ki, wi, g


def indexer_select(qi, wi, ki, q_pos, k_sel):
    dots = jnp.einsum('thd,ld->thl', qi.astype(F32), ki.astype(F32))
    score = jnp.einsum('th,thl->tl', wi.astype(F32), jax.nn.relu(dots))
    admissible = jnp.arange(ki.shape[0])[None, :] <= q_pos[:, None]
    score = jnp.where(admissible, score, -jnp.inf)
    _, idx = lax.top_k(score, k_sel)
    valid = idx <= q_pos[:, None]
    return idx, valid


def sparse_attend(q, k_sel, v_sel, valid):
    T = q.shape[0]
    qg = q.reshape(T, N_KV_HEADS, N_HEADS // N_KV_HEADS, HEAD_DIM).astype(F32)
    s = jnp.einsum('tkgd,tskd->tkgs', qg, k_sel.astype(F32)) * (HEAD_DIM ** -0.5)
    s = jnp.where(valid[:, None, None, :], s, -jnp.inf)
    p = jax.nn.softmax(s, axis=-1)
    o = jnp.einsum('tkgs,tskd->tkgd', p, v_sel.astype(F32))
    return o.reshape(T, ATTN_WIDTH).astype(q.dtype)


def attn_prompt(q, k, v, qi, ki, wi):
    B, S = q.shape[:2]
    k_sel = min(TOPK_MAX, S // 4)
    nb = S // Q_BLOCK
    pos_blocks = jnp.arange(S).reshape(nb, Q_BLOCK)

    def per_seq(args):
        q_b, k_b, v_b, qi_b, ki_b, wi_b = args

        def per_block(blk):
            qb, qib, wib, pos = blk
            idx, valid = indexer_select(qib, wib, ki_b, pos, k_sel)
            return sparse_attend(qb, k_b[idx], v_b[idx], valid)

        blocks = (q_b.reshape(nb, Q_BLOCK, N_HEADS, HEAD_DIM),
                  qi_b.reshape(nb, Q_BLOCK, N_IDX_HEADS, IDX_DIM),
                  wi_b.reshape(nb, Q_BLOCK, N_IDX_HEADS),
                  pos_blocks)
        return lax.map(per_block, blocks).reshape(S, ATTN_WIDTH)

    return lax.map(per_seq, (q, k, v, qi, ki, wi))


def attn_sample(q, k_new, v_new, qi, ki_new, wi, cache_k, cache_v, cache_idx_k, page_table, li):
    Bd, T = q.shape[:2]
    past = page_table.shape[1] * PAGE_SIZE
    L = past + T
    k_sel = min(TOPK_MAX, L // 4)
    ki_past = cache_idx_k[li, page_table].reshape(Bd, past, IDX_DIM)
    ki_all = jnp.concatenate([ki_past, ki_new.astype(ki_past.dtype)], axis=1)
    pos = past + jnp.arange(T)
    idx, valid = jax.vmap(lambda a, b, c_: indexer_select(a, b, c_, pos, k_sel))(qi, wi, ki_all)
    bidx = jnp.arange(Bd)[:, None, None]
    pidx = jnp.minimum(idx, past - 1)
    phys = page_table[bidx, pidx // PAGE_SIZE]
    slot = pidx % PAGE_SIZE
    nidx = jnp.clip(idx - past, 0, T - 1)
    in_past = (idx < past)[..., None, None]
    ks = jnp.where(in_past, cache_k[li, phys, slot], k_new[bidx, nidx])
    vs = jnp.where(in_past, cache_v[li, phys, slot], v_new[bidx, nidx])
    return jax.vmap(sparse_attend)(q, ks, vs, valid)


def causal_conv(xb, buf, w, b):
    xp = jnp.concatenate([buf.astype(xb.dtype), xb], axis=1)
    out = lax.conv_general_dilated(xp, w[:, None, :].astype(xb.dtype), window_strides=(1,), padding='VALID',
                                   dimension_numbers=('NWC', 'WIO', 'NWC'), feature_group_count=xb.shape[-1])
    return out + b, xp[:, -(CONV_W - 1):]


def rglru(xc, h0, w_a, b_a, w_x, b_x, lam):
    B, T, W = xc.shape
    xf = xc.astype(F32)
    xblk = xf.reshape(B, T, N_LRU_BLOCKS, LRU_BLOCK)
    r = jax.nn.sigmoid(jnp.einsum('btni,nij->btnj', xblk, w_a.astype(F32)).reshape(B, T, W) + b_a.astype(F32))
    i = jax.nn.sigmoid(jnp.einsum('btni,nij->btnj', xblk, w_x.astype(F32)).reshape(B, T, W) + b_x.astype(F32))
    log_a = LRU_C * r * jax.nn.log_sigmoid(lam.astype(F32))
    a = jnp.exp(log_a)
    bterm = jnp.sqrt(-jnp.expm1(2.0 * log_a)) * (i * xf)
    bterm = bterm.at[:, 0].add(a[:, 0] * h0.astype(F32))

    def comb(left, right):
        a_l, b_l = left
        a_r, b_r = right
        return a_l * a_r, a_r * b_l + b_r

    _, hs = lax.associative_scan(comb, (a, bterm), axis=1)
    return hs, hs[:, -1]


def lru_branch(h, buf, h0, w_in, conv_w, conv_b, w_a, b_a, w_x, b_x, lam, w_out):
    xb, g = jnp.split(h @ w_in, 2, axis=-1)
    xc, new_buf = causal_conv(xb, buf, conv_w, conv_b)
    hs, h_last = rglru(xc, h0, w_a, b_a, w_x, b_x, lam)
    out = (hs.astype(h.dtype) * jax.nn.silu(g)) @ w_out
    return out, new_buf, h_last.astype(h.dtype)


def setup_inputs(seed: int = 0) -> dict:
    key = jax.random.key(seed)
    ks = jax.random.split(key, 32)
    n_pages = PAST_LEN // PAGE_SIZE
    n_pool = int(math.ceil(POOL_SLACK * DEC_BATCH * n_pages))
    nrm = lambda k, shape, s=1.0: jax.random.normal(k, shape, F32) * s
    perm = jax.random.permutation(ks[0], n_pool)[: DEC_BATCH * n_pages]
    u = jax.random.uniform(ks[1], (N_LRU_LAYERS, LRU_WIDTH), F32, 0.9, 0.999)
    return {
        "x_prompt": nrm(ks[2], (BATCH, SEQ, D_MODEL)),
        "x_sample": nrm(ks[3], (DEC_BATCH, DEC_SEQ, D_MODEL)),
        "cache_k": nrm(ks[4], (N_ATTN_LAYERS, n_pool, PAGE_SIZE, N_KV_HEADS, HEAD_DIM)),
        "cache_v": nrm(ks[5], (N_ATTN_LAYERS, n_pool, PAGE_SIZE, N_KV_HEADS, HEAD_DIM)),
        "cache_idx_k": nrm(ks[6], (N_ATTN_LAYERS, n_pool, PAGE_SIZE, IDX_DIM)),
        "state_conv": nrm(ks[7], (N_LRU_LAYERS, DEC_BATCH, CONV_W - 1, LRU_WIDTH)),
        "state_h": nrm(ks[8], (N_LRU_LAYERS, DEC_BATCH, LRU_WIDTH), 0.5),
        "page_table": perm.reshape(DEC_BATCH, n_pages).astype(jnp.int32),
        "c_prompt": nrm(ks[9], (BATCH, D_MODEL)),
        "c_sample": nrm(ks[10], (DEC_BATCH, D_MODEL)),
        "norm_g": 1.0 + nrm(ks[11], (DEPTH, D_MODEL), 0.02),
        "ada_w": nrm(ks[12], (DEPTH, D_MODEL, 3 * D_MODEL), D_MODEL ** -0.5),
        "ada_b": nrm(ks[13], (DEPTH, 3 * D_MODEL), 0.02),
        "attn_w_in": nrm(ks[14], (N_ATTN_LAYERS, D_MODEL, IN_A), D_MODEL ** -0.5),
        "attn_q_norm": 1.0 + nrm(ks[15], (N_ATTN_LAYERS, HEAD_DIM), 0.02),
        "attn_k_norm": 1.0 + nrm(ks[16], (N_ATTN_LAYERS, HEAD_DIM), 0.02),
        "attn_w_out": nrm(ks[17], (N_ATTN_LAYERS, ATTN_WIDTH, D_MODEL), ATTN_WIDTH ** -0.5),
        "lru_w_in": nrm(ks[18], (N_LRU_LAYERS, D_MODEL, 2 * LRU_WIDTH), D_MODEL ** -0.5),
        "lru_conv_w": nrm(ks[19], (N_LRU_LAYERS, CONV_W, LRU_WIDTH), CONV_W ** -0.5),
        "lru_conv_b": nrm(ks[20], (N_LRU_LAYERS, LRU_WIDTH), 0.02),
        "lru_w_a": nrm(ks[21], (N_LRU_LAYERS, N_LRU_BLOCKS, LRU_BLOCK, LRU_BLOCK), LRU_BLOCK ** -0.5),
        "lru_b_a": nrm(ks[22], (N_LRU_LAYERS, LRU_WIDTH), 0.02),
        "lru_w_x": nrm(ks[23], (N_LRU_LAYERS, N_LRU_BLOCKS, LRU_BLOCK, LRU_BLOCK), LRU_BLOCK ** -0.5),
        "lru_b_x": nrm(ks[24], (N_LRU_LAYERS, LRU_WIDTH), 0.02),
        "lru_lam": jnp.log(u) - jnp.log1p(-u),
        "lru_w_out": nrm(ks[25], (N_LRU_LAYERS, LRU_WIDTH, D_MODEL), LRU_WIDTH ** -0.5),
    }


def reference(x_prompt, x_sample, cache_k, cache_v, cache_idx_k, state_conv, state_h, page_table,
              c_prompt, c_sample, norm_g, ada_w, ada_b, attn_w_in, attn_q_norm, attn_k_norm, attn_w_out,
              lru_w_in, lru_conv_w, lru_conv_b, lru_w_a, lru_b_a, lru_w_x, lru_b_x, lru_lam, lru_w_out):
    xp, xs = x_prompt, x_sample
    B, S, _ = xp.shape
    Bd, T, _ = xs.shape
    past = page_table.shape[1] * PAGE_SIZE
    pos_p = jnp.arange(S)
    pos_s = past + jnp.arange(T)
    kp_l, vp_l, ikp_l, ksm_l, vsm_l, iks_l = [], [], [], [], [], []
    cp_l, hp_l, cs_l, hs_l = [], [], [], []
    for layer in range(DEPTH):
        li = layer // N_MIXERS
        hp, gp = modulate(xp, c_prompt, norm_g[layer], ada_w[layer], ada_b[layer])
        hs, gs = modulate(xs, c_sample, norm_g[layer], ada_w[layer], ada_b[layer])
        if layer % N_MIXERS == 0:
            q, k, v, qi, ki, wi, g = attn_project(hp, pos_p, attn_w_in[li], attn_q_norm[li], attn_k_norm[li])
            o = attn_prompt(q, k, v, qi, ki, wi)
            xp = xp + gp * ((o * jax.nn.silu(g)) @ attn_w_out[li])
            kp_l.append(k); vp_l.append(v); ikp_l.append(ki)
            q2, k2, v2, qi2, ki2, wi2, g2 = attn_project(hs, pos_s, attn_w_in[li], attn_q_norm[li], attn_k_norm[li])
            o2 = attn_sample(q2, k2, v2, qi2, ki2, wi2, cache_k, cache_v, cache_idx_k, page_table, li)
            xs = xs + gs * ((o2 * jax.nn.silu(g2)) @ attn_w_out[li])
            ksm_l.append(k2); vsm_l.append(v2); iks_l.append(ki2)
        else:
            w = (lru_w_in[li], lru_conv_w[li], lru_conv_b[li], lru_w_a[li], lru_b_a[li],
                 lru_w_x[li], lru_b_x[li], lru_lam[li], lru_w_out[li])
            buf0 = jnp.zeros((B, CONV_W - 1, LRU_WIDTH), xp.dtype)
            h00 = jnp.zeros((B, LRU_WIDTH), F32)
            o, nbuf, hl = lru_branch(hp, buf0, h00, *w)
            xp = xp + gp * o
            cp_l.append(nbuf); hp_l.append(hl)
            o2, nbuf2, hl2 = lru_branch(hs, state_conv[li], state_h[li], *w)
            xs = xs + gs * o2
            cs_l.append(nbuf2); hs_l.append(hl2)
    k_prompt = jnp.stack(kp_l)
    v_prompt = jnp.stack(vp_l)
    ik_prompt = jnp.stack(ikp_l)
    k_sample = jnp.stack(ksm_l)
    v_sample = jnp.stack(vsm_l)
    ik_sample = jnp.stack(iks_l)
    conv_prompt = jnp.stack(cp_l)
    h_prompt = jnp.stack(hp_l)
    conv_sample = jnp.stack(cs_l)
    h_sample = jnp.stack(hs_l)
    return (xp, xs, k_prompt, v_prompt, ik_prompt, k_sample, v_sample, ik_sample,
            conv_prompt, h_prompt, conv_sample, h_sample)
```

```python
import math
import os
import numpy as np
import concourse.bass as bass
import concourse.mybir as mybir
from concourse.bass_utils import run_bass_kernel_spmd

F32 = mybir.dt.float32
BF16 = mybir.dt.bfloat16
I32 = mybir.dt.int32
ALU = mybir.AluOpType
AF = mybir.ActivationFunctionType
AX = mybir.AxisListType

NCORES = 8
D = 1024
SEQ = 2048
NSEQ = 2
NT = SEQ // 128
INA = 3664
OQ, OK_, OV, OQI, OKI, OWI, OG = 0, 1024, 1280, 1536, 2560, 2624, 2640
EPS = 1e-6
NIT = 12
TOPK = 256
NEG = -30000.0


class FW:
    def __init__(self, nc):
        self.nc = nc
        self.eng = {'pe': nc.tensor, 'act': nc.scalar, 'dve': nc.vector, 'pool': nc.gpsimd, 'sp': nc.sync}
        self.csem = {e: nc.alloc_semaphore("cs_" + e) for e in ('pe', 'act', 'dve', 'pool')}
        self.ccnt = {e: 0 for e in self.csem}
        self.known = {e: {} for e in self.eng}
        self.lastw = {}
        self.readers = {}
        self.dpool = {e: [[nc.alloc_semaphore("ds_%s%d" % (e, i)), 0] for i in range(n)] for e, n in (('sp', 14), ('pool', 14), ('act', 4))}
        self.dnext = {'sp': 0, 'pool': 0, 'act': 0}
        self.nwaits = 0
        self.nops = {e: 0 for e in self.eng}
        self.pend = {e: [] for e in self.csem}
        self.bank_of = {}
        self.bank_last = {}
        self.last_ins = {}
        self.last_sig = {e: True for e in self.csem}

    def _flush(self, e):
        if self.pend[e]:
            if not self.last_sig[e]:
                self.ccnt[e] += 1
                self.last_ins[e].then_inc(self.csem[e], 1)
                self.last_sig[e] = True
            for t in self.pend[e]:
                t[1] = self.ccnt[e]
            self.pend[e] = []

    def _need(self, e, deps):
        best = {}
        for t in deps:
            if t[1] is None:
                self._flush(t[2])
            s, v = t[0], t[1]
            k = s.name
            if k not in best or best[k][1] < v:
                best[k] = (s, v)
        for k, (s, v) in best.items():
            if self.known[e].get(k, 0) >= v:
                continue
            self.eng[e].wait_ge(s, v)
            self.nwaits += 1
            self.known[e][k] = v

    def _deps(self, reads, writes, e):
        deps = []
        for k in reads:
            if k in self.lastw:
                deps.append(self.lastw[k])
        for k in writes:
            if k in self.lastw:
                deps.append(self.lastw[k])
            deps.extend(self.readers.get(k, []))
        for k in set(reads) | set(writes):
            b = self.bank_of.get(k)
            if b is not None:
                for e2, tok in self.bank_last.get(b, {}).items():
                    if e2 != e:
                        deps.append(tok)
        if e == 'pe':
            deps = [d for d in deps if d[2] != 'pe']
        return deps

    def _commit(self, reads, writes, tok):
        for k in set(reads) | set(writes):
            b = self.bank_of.get(k)
            if b is not None:
                self.bank_last.setdefault(b, {})[tok[2]] = tok
        for k in reads:
            self.readers.setdefault(k, []).append(tok)
        for k in writes:
            self.lastw[k] = tok
            self.readers[k] = []

    def op(self, e, fn, reads=(), writes=(), sig=None):
        if e == 'pool' and os.environ.get('K_NOPOOL'):
            e = 'dve'
        if sig is None:
            sig = (e != 'pe')
        self._need(e, self._deps(reads, writes, e))
        ins = fn(self.eng[e])
        self.nops[e] += 1
        self.last_ins[e] = ins
        if sig:
            self.ccnt[e] += 1
            ins.then_inc(self.csem[e], 1)
            self.last_sig[e] = True
            for t in self.pend[e]:
                t[1] = self.ccnt[e]
            self.pend[e] = []
            tok = [self.csem[e], self.ccnt[e], e]
        else:
            self.last_sig[e] = False
            tok = [self.csem[e], None, e]
            self.pend[e].append(tok)
        self._commit(reads, writes, tok)
        return ins

    def dma(self, e, out, in_, reads=(), writes=(), indirect=None, **kw):
        ent = self.dpool[e][self.dnext[e]]
        self.dnext[e] = (self.dnext[e] + 1) % len(self.dpool[e])
        deps = self._deps(reads, writes, e)
        if ent[1] > 0:
            deps.append([ent[0], ent[1], 'dma'])
        self._need(e, deps)
        if indirect is None:
            ins = self.eng[e].dma_start(out=out, in_=in_, **kw)
        else:
            ins = self.eng[e].indirect_dma_start(out=out, in_=in_, **indirect)
        self.nops[e] += 1
        ent[1] += 16
        ins.then_inc(ent[0], 16)
        self._commit(reads, writes, [ent[0], ent[1], 'dma'])
        return ins

    def barrier(self):
        for ee in self.csem:
            self._flush(ee)
        deps = list(self.lastw.values())
        for rl in self.readers.values():
            deps.extend(rl)
        for b in self.bank_last.values():
            deps.extend(b.values())
        for e in self.eng:
            self._need(e, deps)
        self.lastw = {}
        self.readers = {}
        self.bank_last = {}

    def finish(self, e='sp'):
        for ee in self.csem:
            self._flush(ee)
        deps = list(self.lastw.values())
        for rl in self.readers.values():
            deps.extend(rl)
        self._need(e, deps)


def host_consts():
    c = {}
    theta = 500000.0
    inv = np.exp(-math.log(theta) * np.arange(16, dtype=np.float32) * 2.0 / 32).astype(np.float32)
    pos = np.arange(SEQ, dtype=np.float32)
    ang = (pos[:, None] * inv[None, :]).astype(np.float32)
    cq = np.cos(ang).astype(np.float32).reshape(NT, 128, 16).transpose(1, 0, 2)
    sq = np.sin(ang).astype(np.float32).reshape(NT, 128, 16).transpose(1, 0, 2)
    inv8 = np.exp(-math.log(theta) * np.arange(8, dtype=np.float32) * 2.0 / 16).astype(np.float32)
    ang8 = (pos[:, None] * inv8[None, :]).astype(np.float32)
    ci = np.cos(ang8).astype(np.float32).reshape(NT, 128, 8).transpose(1, 0, 2)
    si = np.sin(ang8).astype(np.float32).reshape(NT, 128, 8).transpose(1, 0, 2)
    c["rope_q"] = np.ascontiguousarray(np.concatenate([cq, sq], axis=2)).reshape(128, NT * 32)
    c["rope_i"] = np.ascontiguousarray(np.concatenate([ci, si], axis=2)).reshape(128, NT * 16)
    ident = np.eye(128, dtype=np.float32)
    c["ident"] = ident
    p = np.arange(128)
    q8 = p % 8
    chunk = (p % 64) // 8
    par = p // 64
    mw = np.zeros((128, 16, 128), np.float32)
    for r in range(16):
        mw[p, r, 8 * r + q8] = 1.0
    c["maskw"] = mw.reshape(128, 2048)
    asel = np.zeros((16, 128), np.float32)
    asel[2 * chunk + par, p] = 1.0
    c["asel"] = asel
    cb = np.where(np.arange(128)[None, :] > np.arange(128)[:, None], -1e30, 0.0).astype(np.float32)
    c["cb"] = cb
    pw = np.zeros((128, 2, NIT + 1), np.float32)
    pw[:, 0, :] = (0.5 ** (np.arange(NIT + 1) + 2))[None, :]
    pw[:, 1, :] = (0.5 ** (np.arange(NIT + 1) + 1))[None, :]
    c["pw"] = pw.reshape(128, 2 * (NIT + 1))
    sel = np.zeros((6, 3, 128), np.float32)
    sel[0, 0, :] = 1.0
    sel[1, 1, :] = 1.0
    for t in range(16):
        sel[2 + t // 4, 2, t] = 1.0
    c["sel"] = sel.reshape(6, 384)
    NITS = 20
    ps_ = (16384 + (np.arange(16) % 4)).astype(np.float32)
    a16 = (ps_[:, None] * inv[None, :]).astype(np.float32)
    c["rope_qs"] = np.concatenate([np.cos(a16), np.sin(a16)], axis=1).astype(np.float32)
    a8 = (ps_[:, None] * inv8[None, :]).astype(np.float32)
    c["rope_is"] = np.concatenate([np.cos(a8), np.sin(a8)], axis=1).astype(np.float32)
    p64 = np.arange(64)
    par_s, c_s, t_s = p64 // 32, (p64 % 32) // 4, p64 % 4
    asel_s = np.zeros((16, 64), np.float32)
    asel_s[2 * c_s + par_s, p64] = 1.0
    c["asel_s"] = asel_s
    mask_s = np.zeros((64, 16), np.float32)
    for bl in range(4):
        mask_s[p64, bl * 4 + t_s] = 1.0
    c["mask_s"] = mask_s
    p128 = np.arange(128)
    c["g128"] = (p128[:, None] % 16 == p128[None, :] % 16).astype(np.float32)
    rep = (np.arange(16)[:, None] == p128[None, :] % 16).astype(np.float32)
    c["rep"] = rep
    c["repT"] = np.ascontiguousarray(rep.T)
    mE = np.full((128, 4), -1e30, np.float32)
    for p_ in range(16):
        for j in range(4):
            if j <= p_ % 4:
                mE[p_, j] = 0.0
    c["maskE"] = mE
    c["cbase"] = ((p128 // 16) * 16).astype(np.float32).reshape(128, 1)
    selj = np.zeros((16, 4, 16), np.float32)
    for bl in range(4):
        for j in range(4):
            for t in range(4):
                selj[bl * 4 + j, j, bl * 4 + t] = 1.0
    c["selj"] = selj.reshape(16, 64)
    c["iota"] = np.tile(np.arange(128, dtype=np.float32)[None, :], (128, 1))
    pws = np.zeros((128, 2, NITS + 1), np.float32)
    pws[:, 0, :] = (0.5 ** (np.arange(NITS + 1) + 2))[None, :]
    pws[:, 1, :] = (0.5 ** (np.arange(NITS + 1) + 1))[None, :]
    c["pws"] = pws.reshape(128, 2 * (NITS + 1))
    return c


CONST_SHAPES = {"rope_q": [128, NT * 32], "rope_i": [128, NT * 16], "ident": [128, 128], "maskw": [128, 2048],
                "asel": [16, 128], "rope_qs": [16, 32], "rope_is": [16, 16], "asel_s": [16, 64], "mask_s": [64, 16], "g128": [128, 128], "rep": [16, 128], "repT": [128, 16], "maskE": [128, 4], "cbase": [128, 1], "selj": [16, 64], "iota": [128, 128], "pws": [128, 42], "cb": [128, 128], "pw": [128, 2 * (NIT + 1)], "sel": [6, 384]}

STAGE = int(os.environ.get("K_STAGE", "9"))
NSB_LIMIT = int(os.environ.get("K_NSB", "8"))
NT1 = int(os.environ.get("K_NT1", str(NT)))
CUT = int(os.environ.get("K_CUT", "99"))


def build_program():
    nc = bass.Bass("TRN2", target_bir_lowering=False)
    fw = FW(nc)

    def din(name, shape, dt=F32):
        return nc.dram_tensor(name, shape, dt, kind="ExternalInput").ap()

    def dout(name, shape, dt=F32):
        return nc.dram_tensor(name, shape, dt, kind="ExternalOutput").ap()

    xp = din("xp", [NSEQ * SEQ, D])
    c6 = din("c6", [6, D])
    norm_g = din("norm_g", [2, D])
    ada_w = din("ada_w", [2, D, 3 * D])
    ada_b = din("ada_b", [2, 3 * D])
    attn_w_in = din("attn_w_in", [D, INA])
    q_norm = din("q_norm", [1, 128])
    k_norm = din("k_norm", [1, 128])
    attn_w_out = din("attn_w_out", [D, D])
    lru_w_in = din("lru_w_in", [D, 2 * D])
    conv_w = din("conv_w", [4, D])
    conv_b = din("conv_b", [1, D])
    w_a = din("w_a", [D, 256])
    b_a = din("b_a", [1, D])
    w_x = din("w_x", [D, 256])
    b_x = din("b_x", [1, D])
    lam = din("lam", [1, D])
    lru_w_out = din("lru_w_out", [D, D])
    cst = {k: din("c_" + k, v) for k, v in CONST_SHAPES.items()}
    sdin = din if STAGE >= 4 else (lambda *a, **k: None)
    sdout = dout if STAGE >= 4 else (lambda *a, **k: None)
    d_xs = sdin("xs", [16, D])
    d_cik = sdin("cik", [5120 * 8, 1024])
    d_ck = sdin("ck", [5120 * 128, 256])
    d_cv = sdin("cv", [5120 * 128, 256])
    d_ptT = sdin("ptT", [128, 4], I32)
    d_pt128 = sdin("pt128", [128, 128], I32)
    d_sconv = sdin("sconv", [12, D])
    d_sh = sdin("sh", [4, D])
    y_s = sdout("y_s", [16, D])
    k_s = sdout("k_s", [16, 256])
    v_s = sdout("v_s", [16, 256])
    ik_s = sdout("ik_s", [16, 64])
    conv_s = sdout("conv_s", [12, D])
    h_s = sdout("h_s", [4, D])

    y_p = dout("y_p", [NSEQ * SEQ, D])
    k_p = dout("k_p", [NSEQ * SEQ, 256])
    v_p = dout("v_p", [NSEQ * SEQ, 256])
    ik_p = dout("ik_p", [NSEQ * SEQ, 64])
    conv_p = dout("conv_p", [NSEQ * 3, D])
    h_p = dout("h_p", [NSEQ, D])

    s_win = nc.dram_tensor("s_win", [D, INA], BF16, kind="Internal").ap()
    s_wout0 = nc.dram_tensor("s_wout0", [D, D], BF16, kind="Internal").ap()
    s_lwin = nc.dram_tensor("s_lwin", [D, 2 * D], BF16, kind="Internal").ap()
    s_lwout = nc.dram_tensor("s_lwout", [D, D], BF16, kind="Internal").ap()

    def sb(name, shape, dt=F32):
        return nc.alloc_sbuf_tensor(name, shape, dt)

    def ps(name, shape, dt=F32):
        return nc.alloc_psum_tensor(name, shape, dt)

    identf = sb("identf", [128, 128])
    identb = sb("identb", [128, 128], BF16)
    ident4 = sb("ident4", [128, 4, 128], BF16)
    onesb = sb("onesb", [128, 128], BF16)
    aselb = sb("aselb", [16, 128], BF16)
    cbias = sb("cbias", [128, 128])
    pw = sb("pw", [128, 2, NIT + 1])
    rope_q = sb("rope_q", [128, NT, 32])
    rope_i = sb("rope_i", [128, NT, 16])
    self_ = sb("sel", [6, 3, 128])
    qn_bc = sb("qn_bc", [128, 128])
    kn_bc = sb("kn_bc", [128, 128])
    epsc = sb("epsc", [128, 1])
    onec = sb("onec", [128, 1])
    stage = sb("stage", [128, 2048])
    scores = stage

    fw.op('pool', lambda g: g.memset(epsc[:], EPS), writes=['epsc'])
    fw.op('pool', lambda g: g.memset(onec[:], 1.0), writes=['onec'])
    c255 = sb("c255", [128, 1])
    cneg = sb("cneg", [128, 1])
    fw.op('pool', lambda g: g.memset(c255[:], TOPK - 0.5), writes=['c255'])
    fw.op('pool', lambda g: g.memset(cneg[:], NEG), writes=['cneg'])
    fw.op('pool', lambda g: g.memset(onesb[:], 1.0), writes=['onesb'])
    fw.dma('sp', identf[:], cst["ident"], writes=['identf'])
    fw.op('dve', lambda v: v.tensor_copy(identb[:], identf[:]), reads=['identf'], writes=['identb'])
    for i in range(4):
        fw.op('dve', lambda v, i=i: v.tensor_copy(ident4[:, i, :], identf[:]), reads=['identf'], writes=['ident4_%d' % i])
    fw.dma('sp', stage[0:16, 0:128], cst["asel"], writes=['stage'])
    fw.op('dve', lambda v: v.tensor_copy(aselb[:], stage[0:16, 0:128]), reads=['stage'], writes=['aselb'])
    fw.dma('sp', cbias[:], cst["cb"], writes=['cbias'])
    fw.dma('sp', pw[:].rearrange("p a b -> p (a b)"), cst["pw"], writes=['pw'])
    fw.dma('sp', rope_q[:].rearrange("p t c -> p (t c)"), cst["rope_q"], writes=['rope_q'])
    fw.dma('sp', rope_i[:].rearrange("p t c -> p (t c)"), cst["rope_i"], writes=['rope_i'])
    fw.dma('sp', self_[:].rearrange("p a b -> p (a b)"), cst["sel"], writes=['sel'])
    fw.dma('sp', qn_bc[:], q_norm[0, :].partition_broadcast(128), writes=['qn_bc'])
    fw.dma('sp', kn_bc[:], k_norm[0, :].partition_broadcast(128), writes=['kn_bc'])

    def cast_w(dst, src, ncols, key):
        step = 512
        for r0 in range(0, D, 256):
            fw.dma('pool', dst[r0:r0 + 256, :], src[r0:r0 + 256, :], writes=[key + "_%d" % r0])
        return [key + "_%d" % r0 for r0 in range(0, D, 256)]

    k_win = cast_w(s_win, attn_w_in, INA, 'swin')
    k_wout0 = cast_w(s_wout0, attn_w_out, D, 'swout0')
    k_lwin = cast_w(s_lwin, lru_w_in, 2 * D, 'slwin')
    k_lwout = cast_w(s_lwout, lru_w_out, D, 'slwout')

    pA = [ps("pA0", [128, 1024]), ps("pA1", [128, 1024])]
    pT = ps("pT", [128, 8, 128], BF16)
    pF = ps("pF", [128, 2, 256])
    pS = ps("pS", [128, 512])
    pN = ps("pN", [128, 512])
    fw.bank_of.update({'pA0a': 0, 'pA0b': 1, 'pA1a': 2, 'pA1b': 3, 'pT': 4, 'pF0': 5, 'pF1': 5, 'pS': 6, 'pN': 7})
    pD = [pA[1][:, 0:512], pA[1][:, 512:1024]]
    pFv = [pA[1][:, 0:256], pA[1][:, 512:768]]
    pFk = ['pA1a', 'pA1b']
    pDk = ['pA1a', 'pA1b']
    pAk = [['pA0a', 'pA0b'], ['pA1a', 'pA1b']]

    NRING = 4
    ring = [sb("ring%d" % i, [128, 8, 512], BF16) for i in range(NRING)]
    rstate = {'n': 0}

    def slab(src2d, col0, ncols, srckeys, eng='sp', cast=False):
        i = rstate['n'] % NRING
        rstate['n'] += 1
        key = 'ring%d' % i
        fw.dma(eng, ring[i][:, :, 0:ncols], src2d[:, col0:col0 + ncols].rearrange("(kc p) n -> p kc n", p=128),
               reads=srckeys, writes=[key])
        return ring[i], key

    c6t = sb("c6t", [6, D])
    sc6 = sb("sc6", [6, D], BF16)
    scT = sb("scT", [128, 8, 6], BF16)
    adab = sb("adab", [6, 512])
    mst = sb("mst", [6, 512])
    mgate = [sb("mgate0", [6, D]), sb("mgate1", [6, D])]
    Sfm = [sb("Sfm0", [128, 8, 6]), sb("Sfm1", [128, 8, 6])]
    Bfm = [sb("Bfm0", [128, 8, 6]), sb("Bfm1", [128, 8, 6])]
    gfm = sb("gfm", [128, 2, 8])
    pTf = pN

    fw.dma('sp', c6t[:], c6, writes=['c6t'])
    fw.op('act', lambda a: a.activation(sc6[:], c6t[:], AF.Silu), reads=['c6t'], writes=['sc6'])
    for c in range(8):
        fw.op('pe', lambda t, c=c: t.transpose(pT[0:128, c, 0:6], sc6[0:6, c * 128:(c + 1) * 128], identb[0:6, 0:6]),
              reads=['sc6', 'identb'], writes=['pT'])
    fw.op('dve', lambda v: v.tensor_copy(scT[:], pT[:, :, 0:6]), reads=['pT'], writes=['scT'])
    for l in range(2):
        fw.dma('sp', gfm[:, l, :], norm_g[l, :].rearrange("(c p) -> p c", p=128), writes=['gfm%d' % l],
               allow_slow_non_contiguous=True)
    for l in range(2):
        for j in range(6):
            fw.dma('sp', adab[:], ada_b[l, j * 512:(j + 1) * 512].partition_broadcast(6), reads=[], writes=['adab'])
            rt, rk = slab(ada_w[l], j * 512, 512, [], eng='pool')
            for kc in range(8):
                fw.op('pe', lambda t, kc=kc, rt=rt: t.matmul(pN[0:6, :], scT[:, kc, :], rt[:, kc, :], start=(kc == 0), stop=(kc == 7)),
                      reads=['scT', rk], writes=['pN'])
            if j < 4:
                fw.op('dve', lambda v, j=j: v.tensor_tensor(mst[:], pN[0:6, :], adab[:], ALU.add),
                      reads=['pN', 'adab'], writes=['mst'])
                dst = Bfm[l] if j < 2 else Sfm[l]
                for cc in range(4):
                    c = (j % 2) * 4 + cc
                    fw.op('pe', lambda t, cc=cc: t.transpose(pS[:, cc * 8:cc * 8 + 6], mst[0:6, cc * 128:(cc + 1) * 128], identf[0:6, 0:6]),
                          reads=['mst', 'identf'], writes=['pS'])
                fw.op('dve', lambda v, j=j, dst=dst: v.tensor_copy(dst[:, (j % 2) * 4:(j % 2) * 4 + 4, :],
                                                                 pS[:, 0:32].rearrange("p (c r) -> p c r", r=8)[:, :, 0:6]),
                      reads=['pS'], writes=['mod%d_%d' % (l, j)])
            else:
                fw.op('dve', lambda v, j=j, l=l: v.tensor_tensor(mgate[l][:, (j - 4) * 512:(j - 3) * 512], pN[0:6, :],
                                                                adab[:], ALU.add),
                      reads=['pN', 'adab'], writes=['mgate%d_%d' % (l, j)])
        fw.op('dve', lambda v, l=l: v.tensor_scalar_add(Sfm[l][:], Sfm[l][:], 1.0),
              reads=['mod%d_2' % l, 'mod%d_3' % l], writes=['Afm%d' % l])
        fw.op('dve', lambda v, l=l: v.tensor_tensor(Sfm[l][:], Sfm[l][:], gfm[:, l, :].unsqueeze(2).to_broadcast([128, 8, 6]), ALU.mult),
              reads=['Afm%d' % l, 'gfm%d' % l], writes=['Afm%d' % l])
    modkeys = [['Afm%d' % l, 'mod%d_0' % l, 'mod%d_1' % l] for l in range(2)]


    def make_gate(l, row):
        for h in range(2):
            fw.op('pe', lambda t, h=h: t.matmul(pA[0][:, h * 512:(h + 1) * 512], self_[:, row, :], mgate[l][:, h * 512:(h + 1) * 512],
                                               start=True, stop=True),
                  reads=['sel', 'mgate%d_%d' % (l, 4 + h)], writes=[pAk[0][h]])
            fw.op('act', lambda a, h=h: a.copy(gate_bc[l][:, h * 512:(h + 1) * 512], pA[0][:, h * 512:(h + 1) * 512]),
                  reads=[pAk[0][h]], writes=['gate_bc%d_%d' % (l, h)])

    wki = sb("wki", [128, 8, 64], BF16)
    wwi = sb("wwi", [128, 8, 16], BF16)
    fw.dma('sp', wki[:], s_win[:, OKI:OKI + 64].rearrange("(kc p) n -> p kc n", p=128), reads=k_win, writes=['wki'])
    fw.dma('sp', wwi[:], s_win[:, OWI:OWI + 16].rearrange("(kc p) n -> p kc n", p=128), reads=k_win, writes=['wwi'])

    junk = sb("junk", [128, 1024], BF16)
    xn = sb("xn", [128, D], BF16)
    hT = sb("hT", [128, 8, 256], BF16)
    small = sb("small", [128, 64])
    smk = {'n': 0}
    qf = sb("qf", [128, D])
    qb = sb("qb", [128, D], BF16)
    rt_ = sb("rt_", [128, 4, 128])
    kb = sb("kb", [128, 320], BF16)

    def norm_tile(xap, xkey, okey, n=128):
        fw.op('act', lambda a: a.activation(junk[0:n, 0:D], xap, AF.Square, scale=1.0 / 32.0, accum_out=small[0:n, 0:1]),
              reads=[xkey], writes=['junk', 'sm0'])
        fw.op('act', lambda a: a.activation(small[0:n, 1:2], small[0:n, 0:1], AF.Sqrt, bias=epsc[0:n, 0:1], scale=1.0),
              reads=['sm0', 'epsc'], writes=['sm1'])
        fw.op('dve', lambda v: v.reciprocal(small[0:n, 2:3], small[0:n, 1:2]), reads=['sm1'], writes=['sm2'])
        fw.op('dve', lambda v: v.tensor_scalar(xn[0:n, :], xap, small[0:n, 2:3], None, ALU.mult), reads=[xkey, 'sm2'], writes=[okey])

    def to_hT(l, row, col0, ncols=128, src=None, srckey='xn'):
        src = xn if src is None else src
        for c in range(8):
            fw.op('pe', lambda t, c=c: t.transpose(pT[:, c, 0:ncols], src[0:ncols, c * 128:(c + 1) * 128], identb[0:ncols, 0:ncols]),
                  reads=[srckey, 'identb'], writes=['pT'])
        for c in range(8):
            e = 'act' if c % 2 == 0 else 'dve'
            e = os.environ.get('K_EV', e)
            if e == 'A':
                e = 'act' if c < 4 else 'dve'
            if e == 'B':
                e = 'dve' if c < 4 else 'act'
            if e == 'act':
                fw.op('act', lambda a, c=c: a.activation(hT[:, c, col0:col0 + ncols], pT[:, c, 0:ncols], AF.Identity,
                                                         bias=Bfm[l][:, c, row:row + 1], scale=Sfm[l][:, c, row:row + 1]),
                      reads=['pT'] + modkeys[l], writes=['hT%d' % c] + (['ser'] if os.environ.get('K_SER') else []))
            else:
                fw.op('dve', lambda v, c=c: v.tensor_scalar(hT[:, c, col0:col0 + ncols], pT[:, c, 0:ncols],
                                                          Sfm[l][:, c, row:row + 1], Bfm[l][:, c, row:row + 1], ALU.mult, ALU.add),
                      reads=['pT'] + modkeys[l], writes=['hT%d' % c] + (['ser'] if os.environ.get('K_SER') else []))
    hTk = ['hT%d' % c for c in range(8)]

    def rope_inplace(e, x1, x2, cs, sn, tmp, tkey, xkey, shape):
        cb_ = cs.unsqueeze(1).to_broadcast(shape)
        sb_ = sn.unsqueeze(1).to_broadcast(shape)
        t = [tmp[:, i, 0:shape[1] * shape[2]].rearrange("p (h d) -> p h d", d=shape[2]) for i in range(4)]
        fw.op(e, lambda g: g.tensor_tensor(t[0], x1, cb_, ALU.mult), reads=xkey, writes=[tkey + '0'])
        fw.op(e, lambda g: g.tensor_tensor(t[1], x2, sb_, ALU.mult), reads=xkey, writes=[tkey + '1'])
        fw.op(e, lambda g: g.tensor_tensor(t[2], x2, cb_, ALU.mult), reads=xkey, writes=[tkey + '2'])
        fw.op(e, lambda g: g.tensor_tensor(t[3], x1, sb_, ALU.mult), reads=xkey, writes=[tkey + '3'])
        return t

    def phase1(s):
        rt, rk = slab(s_win, OK_, 512, k_win)
        for i in range(NT1):
            buf = i % 2
            row0 = s * SEQ + i * 128
            xkey = 'xt%d_0' % buf
            fw.dma('sp', xt[buf][:, 0, :], xp[row0:row0 + 128, :], writes=[xkey])
            norm_tile(xt[buf][:, 0, :], xkey, 'xn')
            if CUT <= 1:
                continue
            to_hT(0, s, 0)
            if CUT <= 2:
                continue
            o = pA[0]
            for kc in range(8):
                fw.op('pe', lambda t, kc=kc: t.matmul(o[:, 0:512], hT[:, kc, 0:128], rt[:, kc, :], start=(kc == 0), stop=(kc == 7)),
                      reads=[hTk[kc], rk], writes=['pA0a'])
            for kc in range(8):
                fw.op('pe', lambda t, kc=kc: t.matmul(o[:, 512:576], hT[:, kc, 0:128], wki[:, kc, :], start=(kc == 0), stop=(kc == 7)),
                      reads=[hTk[kc], 'wki'], writes=['pA0b'])
            if CUT <= 3:
                continue
            kfi = qf[:, 0:576]
            kfk = 'qf'
            for h in range(2):
                fw.op('act', lambda a, h=h: a.activation(junk[:, 0:128], o[:, h * 128:(h + 1) * 128], AF.Square,
                                                         scale=1.0 / math.sqrt(128.0), accum_out=small[:, 4 + h:5 + h]),
                      reads=['pA0a'], writes=['junk', 'sm4_%d' % h])
            fw.op('act', lambda a: a.activation(small[:, 6:8], small[:, 4:6], AF.Sqrt, bias=epsc[:, 0:1], scale=1.0),
                  reads=['sm4_0', 'sm4_1', 'epsc'], writes=['sm6'])
            fw.op('dve', lambda v: v.reciprocal(small[:, 8:10], small[:, 6:8]), reads=['sm6'], writes=['sm8'])
            kv3 = kfi[:, 0:256].rearrange("p (h d) -> p h d", d=128)
            fw.op('dve', lambda v: v.tensor_tensor(kv3, o[:, 0:256].rearrange("p (h d) -> p h d", d=128),
                                                   small[:, 8:10].unsqueeze(2).to_broadcast([128, 2, 128]), ALU.mult),
                  reads=['pA0a', 'sm8'], writes=[kfk])
            fw.op('pool', lambda g: g.tensor_tensor(kv3, kv3, kn_bc[:].unsqueeze(1).to_broadcast([128, 2, 128]), ALU.mult),
                  reads=[kfk, 'kn_bc'], writes=[kfk])
            if CUT <= 4:
                continue
            x1 = kv3[:, :, 0:16]
            x2 = kv3[:, :, 16:32]
            t = rope_inplace('pool', x1, x2, rope_q[:, i, 0:16], rope_q[:, i, 16:32], rt_, 'rt', [kfk, 'rope_q'], [128, 2, 16])
            fw.op('pool', lambda g: g.tensor_tensor(x1, t[0], t[1], ALU.subtract), reads=['rt0', 'rt1', 'rt2', 'rt3'], writes=[kfk])
            fw.op('pool', lambda g: g.tensor_tensor(x2, t[2], t[3], ALU.add), reads=['rt2', 'rt3'], writes=[kfk])
            if CUT <= 5:
                continue
            fw.op('act', lambda a: a.copy(kfi[:, 256:512], o[:, 256:512]), reads=['pA0a'], writes=[kfk])
            fw.op('act', lambda a: a.copy(kfi[:, 512:576], o[:, 512:576]), reads=['pA0b'], writes=[kfk])
            y1 = kfi[:, 512:520].unsqueeze(1)
            y2 = kfi[:, 520:528].unsqueeze(1)
            t = rope_inplace('pool', y1, y2, rope_i[:, i, 0:8], rope_i[:, i, 8:16], rt_, 'rt', [kfk, 'rope_i'], [128, 1, 8])
            fw.op('pool', lambda g: g.tensor_tensor(y1, t[0], t[1], ALU.subtract), reads=['rt0', 'rt1', 'rt2', 'rt3'], writes=[kfk])
            fw.op('pool', lambda g: g.tensor_tensor(y2, t[2], t[3], ALU.add), reads=['rt2', 'rt3'], writes=[kfk])
            if CUT <= 6:
                continue
            fw.dma('pool', k_p[row0:row0 + 128, :], kfi[:, 0:256], reads=[kfk], writes=['o_k'])
            fw.dma('pool', v_p[row0:row0 + 128, :], kfi[:, 256:512], reads=[kfk], writes=['o_v'])
            fw.dma('pool', ik_p[row0:row0 + 128, :], kfi[:, 512:576], reads=[kfk], writes=['o_ik'])
            if CUT <= 7:
                continue
            fw.op('act', lambda a: a.copy(Vr[:, i, :], kfi[:, 256:512]), reads=[kfk], writes=['Vr%d' % i])
            fw.op('dve', lambda v: v.tensor_copy(kb[:, 0:256], kfi[:, 0:256]), reads=[kfk], writes=['kb'])
            fw.op('dve', lambda v: v.tensor_copy(kb[:, 256:320], kfi[:, 512:576]), reads=[kfk], writes=['kbi'])
            for h in range(2):
                fw.op('pe', lambda t, h=h: t.transpose(pT[:, h, :], kb[:, h * 128:(h + 1) * 128], identb[:]),
                      reads=['kb', 'identb'], writes=['pT'])
            fw.op('pe', lambda t: t.transpose(pT[0:64, 2, :], kb[:, 256:320], identb[:]), reads=['kbi', 'identb'], writes=['pT'])
            fw.op('act', lambda a: a.copy(KT[:, :, i * 128:(i + 1) * 128], pT[:, 0:2, :]), reads=['pT'], writes=['KT%d' % i])
            fw.op('dve', lambda v: v.tensor_copy(KIT2[0:64, i * 128:(i + 1) * 128], pT[0:64, 2, :]), reads=['pT'], writes=['KIa%d' % i])
            fw.op('dve', lambda v: v.tensor_copy(KIT2[64:128, i * 128:(i + 1) * 128], pT[0:64, 2, :]), reads=['pT'], writes=['KIb%d' % i])

    thr = sb("thr", [128, 8])
    wk = sb("wk", [128, 2, NIT + 1])
    rr = {'r': 0, 'p': 0, 'd': 0, 'f': 0}

    def q_post(tile, t):
        o = pA[0]
        for h in range(8):
            fw.op('act', lambda a, h=h: a.activation(junk[:, 0:128], o[:, h * 128:(h + 1) * 128], AF.Square,
                                                     scale=1.0 / math.sqrt(128.0), accum_out=small[:, 16 + h:17 + h]),
                  reads=[pAk[0][h // 4]], writes=['junk', 'sq%d' % h])
        sqk = ['sq%d' % h for h in range(8)]
        fw.op('act', lambda a: a.activation(small[:, 24:32], small[:, 16:24], AF.Sqrt, bias=epsc[:, 0:1], scale=1.0),
              reads=sqk + ['epsc'], writes=['sq_s'])
        fw.op('dve', lambda v: v.reciprocal(small[:, 32:40], small[:, 24:32]), reads=['sq_s'], writes=['sq_r'])
        q3 = qf[:].rearrange("p (h d) -> p h d", d=128)
        fw.op('dve', lambda v: v.tensor_tensor(q3, o[:].rearrange("p (h d) -> p h d", d=128),
                                               small[:, 32:40].unsqueeze(2).to_broadcast([128, 8, 128]), ALU.mult),
              reads=['pA0a', 'pA0b', 'sq_r'], writes=['qf'])
        fw.op('pool', lambda g: g.tensor_tensor(q3, q3, qn_bc[:].unsqueeze(1).to_broadcast([128, 8, 128]), ALU.mult),
              reads=['qf', 'qn_bc'], writes=['qf'])
        x1 = q3[:, :, 0:16]
        x2 = q3[:, :, 16:32]
        tt = rope_inplace('pool', x1, x2, rope_q[:, tile, 0:16], rope_q[:, tile, 16:32], rt_, 'rt', ['qf', 'rope_q'], [128, 8, 16])
        fw.op('pool', lambda g: g.tensor_tensor(x1, tt[0], tt[1], ALU.subtract), reads=['rt0', 'rt1', 'rt2', 'rt3'], writes=['qf'])
        fw.op('pool', lambda g: g.tensor_tensor(x2, tt[2], tt[3], ALU.add), reads=['rt2', 'rt3'], writes=['qf'])
        fw.op('act', lambda a: a.copy(qb[:], qf[:]), reads=['qf'], writes=['qb'])
        for h in range(8):
            fw.op('pe', lambda t_, h=h: t_.transpose(pT[:, h, :], qb[:, h * 128:(h + 1) * 128], identb[:]),
                  reads=['qb', 'identb'], writes=['pT'])
        fw.op('dve', lambda v: v.tensor_copy(qT[:, t, 0, :], pT[:, 0:4, :].rearrange("p h q -> p (h q)")), reads=['pT'], writes=['qT%d' % t])
        fw.op('dve', lambda v: v.tensor_copy(qT[:, t, 1, :], pT[:, 4:8, :].rearrange("p h q -> p (h q)")), reads=['pT'], writes=['qTb%d' % t])

    def qi_post(tile, t):
        o = pA[1]
        fw.op('act', lambda a: a.copy(qb[:], o[:]), reads=['pA1a', 'pA1b'], writes=['qb'])
        o3 = o[:].rearrange("p (h d) -> p h d", d=64)
        b3 = qb[:].rearrange("p (h d) -> p h d", d=64)
        tt = rope_inplace('dve', o3[:, :, 0:8], o3[:, :, 8:16], rope_i[:, tile, 0:8], rope_i[:, tile, 8:16], rt_, 'rt',
                          ['pA1a', 'pA1b', 'rope_i'], [128, 16, 8])
        fw.op('pool', lambda g: g.tensor_tensor(b3[:, :, 0:8], tt[0], tt[1], ALU.subtract), reads=['rt0', 'rt1', 'rt2', 'rt3'], writes=['qb'])
        fw.op('pool', lambda g: g.tensor_tensor(b3[:, :, 8:16], tt[2], tt[3], ALU.add), reads=['rt2', 'rt3'], writes=['qb'])
        for c in range(8):
            fw.op('pe', lambda t_, c=c: t_.transpose(pT[:, c, :], qb[:, c * 128:(c + 1) * 128], identb[:]),
                  reads=['qb', 'identb'], writes=['pT'])
        for half in (0, 1):
            qv = qiT[half * 64:(half + 1) * 64, t, :, half * 64:(half + 1) * 64].rearrange("p r (c q) -> p r c q", q=8)
            for c0 in (0, 4):
                src = pT[half * 64:(half + 1) * 64, c0:c0 + 4, :].rearrange("p c (r q) -> p r c q", q=8)
                if half == 0:
                    fw.op('dve', lambda v: v.tensor_copy(qv[:, :, c0:c0 + 4, :], src), reads=['pT'], writes=['qiT%d' % t])
                else:
                    fw.op('act', lambda a: a.copy(qv[:, :, c0:c0 + 4, :], src), reads=['pT'], writes=['qiTb%d' % t])

    def proj_tok(o, okeys, srcT, srckeys, t, slabs):
        for h in range(2):
            rt, rk = slabs[h]
            for kc in range(8):
                fw.op('pe', lambda t_, kc=kc, rt=rt, h=h: t_.matmul(o[:, h * 512:(h + 1) * 512], srcT[:, kc, t * 128:(t + 1) * 128], rt[:, kc, :],
                                                                  start=(kc == 0), stop=(kc == 7)),
                      reads=[srckeys[kc], rk], writes=[okeys[h]])

    def indexer(s, j, t):
        scores = scb[t]
        biasm = biasb[t]
        nk = (j + 1) * 128
        nch = (nk + 511) // 512
        qsl = slice(t * 128, (t + 1) * 128)
        fw.op('pe', lambda t_: t_.matmul(pN[:, 0:128], aselb[:], wiT[:, qsl], start=True, stop=True),
              reads=['aselb', 'wiT'], writes=['pN'])
        fw.op('act', lambda a: a.copy(s1sb[:], pN[:, 0:128]), reads=['pN'], writes=['s1sb'])
        fw.op('pool', lambda g: g.tensor_tensor(wall[:], s1sb[:].unsqueeze(1).to_broadcast([128, 16, 128]), maskw[:], ALU.mult),
              reads=['s1sb', 'maskw'], writes=['wall'])
        for ch in range(nch):
            k0 = ch * 512
            w = min(512, nk - k0)
            kik = ['KIa%d' % i for i in range(k0 // 128, (k0 + w) // 128)] + ['KIb%d' % i for i in range(k0 // 128, (k0 + w) // 128)]
            rinfo = {}
            for r in range(17):
                if r < 16:
                    d = rr['d'] % 2
                    rr['d'] += 1
                    fw.op('pe', lambda t_, d=d, r=r: t_.matmul(pD[d][:, 0:w], qiT[:, t, r, :], KIT2[:, k0:k0 + w], start=True, stop=True),
                          reads=['qiT%d' % t, 'qiTb%d' % t] + kik, writes=[pDk[d]])
                    ri = rr['r'] % 2
                    rr['r'] += 1
                    fw.op('act', lambda a, d=d, ri=ri: a.activation(Rb[ri][:, 0:w], pD[d][:, 0:w], AF.Relu), reads=[pDk[d]], writes=['Rb%d' % ri])
                    rinfo[r] = ri
                if r >= 1:
                    r1 = r - 1
                    ri1 = rinfo[r1]
                    fw.op('pe', lambda t_, r1=r1, ri1=ri1: t_.matmul(pS[:, 0:w], wall[:, r1, :], Rb[ri1][:, 0:w], start=(r1 == 0), stop=(r1 == 15)),
                          reads=['wall', 'Rb%d' % ri1], writes=['pS'])
            fw.op('act', lambda a: a.copy(scores[:, k0:k0 + w], pS[:, 0:w]), reads=['pS', 'stage'], writes=['sc%d_%d' % (t, ch)])
        sck = ['sc%d_%d' % (t, ch) for ch in range(nch)]
        fw.op('dve', lambda v: v.tensor_reduce(thr[:, 0:1], scores[:, 0:nk], AX.X, ALU.max), reads=sck, writes=['thr0'])
        fw.op('dve', lambda v: v.tensor_reduce(thr[:, 1:2], scores[:, 0:nk], AX.X, ALU.min), reads=sck, writes=['thr1'])
        fw.op('pool', lambda g: g.tensor_tensor(scores[:, j * 128:nk], scores[:, j * 128:nk], cbias[:], ALU.add),
              reads=sck + ['cbias', 'thr0', 'thr1'], writes=['scd%d' % t])
        fw.op('dve', lambda v: v.tensor_tensor(thr[:, 2:3], thr[:, 0:1], thr[:, 1:2], ALU.subtract), reads=['thr0', 'thr1'], writes=['thr2'])
        fw.op('dve', lambda v: v.tensor_scalar(thr[:, 3:4], thr[:, 2:3], 1.02, 2e-3, ALU.mult, ALU.add), reads=['thr2'], writes=['thr3'])
        fw.op('dve', lambda v: v.tensor_scalar(thr[:, 4:5], thr[:, 2:3], -0.01, -1e-3, ALU.mult, ALU.add), reads=['thr2'], writes=['thr4'])
        fw.op('dve', lambda v: v.tensor_tensor(thr[:, 4:5], thr[:, 4:5], thr[:, 1:2], ALU.add), reads=['thr4', 'thr1'], writes=['thr4'])
        fw.op('dve', lambda v: v.tensor_scalar(wk[:, 0, :], pw[:, 0, :], thr[:, 3:4], None, ALU.mult), reads=['pw', 'thr3'], writes=['wk0'])
        fw.op('dve', lambda v: v.tensor_scalar(wk[:, 1, :], pw[:, 1, :], thr[:, 3:4], None, ALU.mult), reads=['pw', 'thr3'], writes=['wk1'])
        fw.op('dve', lambda v: v.tensor_tensor(thr[:, 5:6], thr[:, 4:5], wk[:, 1, 0:1], ALU.add), reads=['thr4', 'wk1'], writes=['mid'])
        for it in range(NIT):
            fw.op('dve', lambda v: v.tensor_scalar(biasm[:, 0:nk], scores[:, 0:nk], thr[:, 5:6], None, ALU.is_ge, ALU.add, accum_out=thr[:, 6:7]),
                  reads=sck + ['scd%d' % t, 'mid'], writes=['biasm%d' % t, 'cnt'])
            fw.op('dve', lambda v, it=it: v.tensor_scalar(thr[:, 7:8], thr[:, 6:7], c255[:, 0:1], wk[:, 1, it:it + 1], ALU.is_ge, ALU.mult),
                  reads=['cnt', 'wk1', 'c255'], writes=['dlt'])
            fw.op('dve', lambda v, it=it: v.scalar_tensor_tensor(thr[:, 5:6], thr[:, 5:6], wk[:, 0, it:it + 1], thr[:, 7:8], ALU.subtract, ALU.add),
                  reads=['mid', 'wk0', 'dlt'], writes=['mid'])
        fw.op('dve', lambda v: v.tensor_tensor(thr[:, 5:6], thr[:, 5:6], wk[:, 0, NIT - 1:NIT], ALU.subtract), reads=['mid', 'wk0'], writes=['mid'])
        fw.op('dve', lambda v: v.tensor_scalar(biasm[:, 0:nk], scores[:, 0:nk], thr[:, 5:6], cneg[:, 0:1], ALU.is_lt, ALU.mult),
              reads=sck + ['scd%d' % t, 'mid', 'cneg'], writes=['biasm%d' % t])

    def attention(s, j, t):
        biasm = biasb[t]
        qsl = slice(t * 128, (t + 1) * 128)
        nc_ = j + 1
        aOs = [pS[:], pA[0][:, 0:512]]
        aNs = [pN[:], pA[0][:, 512:1024]]
        kOs = ['pS', 'pA0a']
        kNs = ['pN', 'pA0b']
        items = [(kvh, c) for kvh in range(2) for c in range(nc_)]
        info = {}

        def stage1(i):
            kvh, c = items[i]
            d = rr['d'] % 2
            rr['d'] += 1
            fw.op('pe', lambda t_: t_.matmul(pD[d], KT[:, kvh, c * 128:(c + 1) * 128], qT[:, t, kvh, :], start=True, stop=False),
                  reads=['KT%d' % c, 'qT%d' % t, 'qTb%d' % t], writes=[pDk[d]])
            fw.op('pe', lambda t_: t_.matmul(pD[d], biasm[:, c * 128:(c + 1) * 128], ident4[:].rearrange("p a b -> p (a b)"), start=False, stop=True),
                  reads=['biasm%d' % t, 'ident4_0', 'ident4_1', 'ident4_2', 'ident4_3'], writes=[pDk[d]])
            pi = rr['p'] % 2
            rr['p'] += 1
            fw.op('act', lambda a: a.activation(Pb[pi][:], pD[d], AF.Exp, scale=1.0 / math.sqrt(128.0)), reads=[pDk[d]], writes=['Pb%d' % pi])
            info[i] = pi

        def stage2(i):
            kvh, c = items[i]
            pi = info[i]
            aO, aN, kO, kN = aOs[kvh], aNs[kvh], kOs[kvh], kNs[kvh]
            fw.op('pe', lambda t_: t_.matmul(aO, Vr[:, c, kvh * 128:(kvh + 1) * 128], Pb[pi][:], start=(c == 0), stop=(c == nc_ - 1)),
                  reads=['Vr%d' % c, 'Pb%d' % pi], writes=[kO])
            fw.op('pe', lambda t_: t_.matmul(aN, onesb[:], Pb[pi][:], start=(c == 0), stop=(c == nc_ - 1)),
                  reads=['onesb', 'Pb%d' % pi], writes=[kN])
            if c == nc_ - 1:
                fw.op('dve', lambda v: v.reciprocal(rden[:], aN), reads=[kN], writes=['rden'])
                fw.op('dve', lambda v: v.tensor_tensor(otmp[:], aO, rden[:], ALU.mult), reads=[kO, 'rden'], writes=['otmp'])
                fw.op('pool', lambda g: g.tensor_tensor(ogT[:, 4 * kvh:4 * kvh + 4, qsl], otmp[:].rearrange("p (h q) -> p h q", q=128),
                                                        sgT[:, 4 * kvh:4 * kvh + 4, qsl], ALU.mult),
                      reads=['otmp'] + ['sgT%d' % m for m in range(4 * kvh, 4 * kvh + 4)],
                      writes=['ogT%d_%d' % (t, kvh)] + ['ogL%d' % m for m in range(4 * kvh, 4 * kvh + 4)])
        for i in range(len(items) + 1):
            if i < len(items):
                stage1(i)
            if i >= 1:
                stage2(i - 1)

    def residual(buf, t, l, okey_prefix, pa=0):
        xk = 'xt%d_%d' % (buf, t)
        for h in range(2):
            fw.op('dve', lambda v, h=h: v.tensor_tensor(qf[:, h * 512:(h + 1) * 512], pA[pa][:, h * 512:(h + 1) * 512],
                                                       gate_bc[l][:, h * 512:(h + 1) * 512], ALU.mult),
                  reads=[pAk[pa][h], 'gate_bc%d_%d' % (l, h)], writes=['qf'])
        fw.op('pool', lambda g: g.tensor_tensor(xt[buf][:, t, :], xt[buf][:, t, :], qf[:], ALU.add), reads=['qf', xk], writes=[xk])

    lvec = sb("lvec", [128, 8, 8])
    lc8 = sb("lc8", [128, 2, 8])
    wab = sb("wab", [128, 2, 8, 256], BF16)

    def setup_l1():
        for j in range(4):
            fw.dma('sp', lvec[:, j, :], conv_w[j, :].rearrange("(c p) -> p c", p=128), writes=['lvec%d' % j], allow_slow_non_contiguous=True)
        for j, src in enumerate([conv_b, b_a, b_x, lam]):
            fw.dma('sp', lvec[:, 4 + j, :], src[0, :].rearrange("(c p) -> p c", p=128), writes=['lvec%d' % (4 + j)], allow_slow_non_contiguous=True)
        fw.op('act', lambda a: a.activation(lc8[:, 0, :], lvec[:, 7, :], AF.Exp, scale=-1.0), reads=['lvec7'], writes=['lc8'])
        fw.op('act', lambda a: a.activation(lc8[:, 0, :], lc8[:, 0, :], AF.Ln, bias=onec[:, 0:1], scale=1.0), reads=['lc8', 'onec'], writes=['lc8'])
        fw.op('dve', lambda v: v.tensor_scalar_mul(lc8[:, 1, :], lc8[:, 0, :], -16.0), reads=['lc8'], writes=['lc8b'])
        fw.op('dve', lambda v: v.tensor_scalar_mul(lc8[:, 0, :], lc8[:, 0, :], -8.0), reads=['lc8', 'lc8b'], writes=['lc8'])
        for g_, src in enumerate([w_a, w_x]):
            fw.dma('pool', wab[:, g_, :, :], src.rearrange("(n p) c -> p n c", p=128), writes=['wab%d' % g_])

    def layer1(s, sbi, buf):
        tile0 = sbi * 2
        for t in range(2):
            xk = 'xt%d_%d' % (buf, t)
            norm_tile(xt[buf][:, t, :], xk, 'xn')
            to_hT(1, s, t * 128)
        slabs = [slab(s_lwin, i * 512, 512, k_lwin) for i in range(2)]
        for m in range(8):
            f = rr['f'] % 2
            rr['f'] += 1
            rt, rk = slabs[m // 4]
            for kc in range(8):
                fw.op('pe', lambda t_, kc=kc, rt=rt, m=m, f=f: t_.matmul(pFv[f], rt[:, kc, (m % 4) * 128:(m % 4 + 1) * 128], hT[:, kc, :],
                                                                       start=(kc == 0), stop=(kc == 7)),
                      reads=[hTk[kc], rk], writes=[pFk[f]])
            if m % 2 == 0:
                fw.op('act', lambda a, m=m, f=f: a.copy(xbT[:, m, 3:259], pFv[f]), reads=[pFk[f]], writes=['xbT%d' % m])
            else:
                fw.op('dve', lambda v, m=m, f=f: v.tensor_copy(xbT[:, m, 3:259], pFv[f]), reads=[pFk[f]], writes=['xbT%d' % m])
        slabs = [slab(s_lwin, 1024 + i * 512, 512, k_lwin) for i in range(2)]
        for m in range(8):
            f = rr['f'] % 2
            rr['f'] += 1
            rt, rk = slabs[m // 4]
            for kc in range(8):
                fw.op('pe', lambda t_, kc=kc, rt=rt, m=m, f=f: t_.matmul(pFv[f], rt[:, kc, (m % 4) * 128:(m % 4 + 1) * 128], hT[:, kc, :],
                                                                       start=(kc == 0), stop=(kc == 7)),
                      reads=[hTk[kc], rk], writes=[pFk[f]])
            fw.op('act', lambda a, m=m, f=f: a.activation(sgT[:, m, :], pFv[f], AF.Silu), reads=[pFk[f]], writes=['sgT%d' % m])
        lk = ['lvec%d' % i for i in range(8)]
        for nb in range(4):
            for ki in range(2):
                m = nb * 2 + ki
                xk_ = 'xbT%d' % m
                fw.op('pool', lambda g, m=m, ki=ki: g.tensor_scalar(xc2[:, ki, :], xbT[:, m, 0:256], lvec[:, 0, m:m + 1], lvec[:, 4, m:m + 1], ALU.mult, ALU.add),
                      reads=[xk_, 'xh%d' % m] + lk, writes=['xc2_%d' % ki])
                for jj in range(1, 4):
                    fw.op('dve', lambda g, m=m, ki=ki, jj=jj: g.scalar_tensor_tensor(xc2[:, ki, :], xbT[:, m, jj:jj + 256], lvec[:, jj, m:m + 1], xc2[:, ki, :], ALU.mult, ALU.add),
                          reads=[xk_, 'xh%d' % m, 'xc2_%d' % ki] + lk, writes=['xc2_%d' % ki])
                fw.op('act', lambda a, ki=ki: a.copy(xcb2[:, ki, :], xc2[:, ki, :]), reads=['xc2_%d' % ki], writes=['xcb2_%d' % ki])
                fw.op('pool', lambda g, m=m: g.tensor_copy(xbT[:, m, 0:3], xbT[:, m, 256:259]), reads=[xk_, 'xc2_%d' % ki], writes=['xh%d' % m])
            for mo in range(2):
                m = nb * 2 + mo
                gb = gbuf
                for g_ in range(2):
                    f = rr['f'] % 2
                    rr['f'] += 1
                    for ki in range(2):
                        fw.op('pe', lambda t_, g_=g_, ki=ki, mo=mo, f=f: t_.matmul(pFv[f], wab[:, g_, nb * 2 + ki, mo * 128:(mo + 1) * 128], xcb2[:, ki, :],
                                                                              start=(ki == 0), stop=(ki == 1)),
                              reads=['wab%d' % g_, 'xcb2_0', 'xcb2_1'], writes=[pFk[f]])
                    fw.op('act', lambda a, g_=g_, f=f, m=m: a.activation(gb[:, g_, :], pFv[f], AF.Sigmoid, bias=lvec[:, 5 + g_, m:m + 1], scale=1.0),
                          reads=[pFk[f]] + lk, writes=['gb%d' % g_])
                fw.op('act', lambda a, m=m: a.activation(gb[:, 2, :], gb[:, 0, :], AF.Exp, scale=lc8[:, 0, m:m + 1]), reads=['gb0', 'lc8'], writes=['gb2'])
                fw.op('act', lambda a, m=m: a.activation(gb[:, 3, :], gb[:, 0, :], AF.Exp, scale=lc8[:, 1, m:m + 1]), reads=['gb0', 'lc8b'], writes=['gb3'])
                fw.op('act', lambda a: a.activation(gb[:, 3, :], gb[:, 3, :], AF.Sqrt, bias=onec[:, 0:1], scale=-1.0), reads=['gb3', 'onec'], writes=['gb3'])
                fw.op('pool', lambda g, mo=mo: g.tensor_tensor(gb[:, 1, :], gb[:, 1, :], xc2[:, mo, :], ALU.mult), reads=['gb1', 'xc2_%d' % mo], writes=['gb1'])
                fw.op('pool', lambda g: g.tensor_tensor(gb[:, 1, :], gb[:, 1, :], gb[:, 3, :], ALU.mult), reads=['gb1', 'gb3'], writes=['gb1'])
                fw.op('dve', lambda v, m=m: v.tensor_tensor_scan(gb[:, 4, :], gb[:, 2, :], gb[:, 1, :], hstate[:, m:m + 1], ALU.mult, ALU.add),
                      reads=['gb2', 'gb1', 'hst%d' % m], writes=['gb4'])
                fw.op('dve', lambda v, m=m: v.tensor_copy(hstate[:, m:m + 1], gb[:, 4, 255:256]), reads=['gb4'], writes=['hst%d' % m])
                fw.op('pool', lambda g, m=m: g.tensor_tensor(ogT[:, m, :], gb[:, 4, :], sgT[:, m, :], ALU.mult), reads=['gb4', 'sgT%d' % m], writes=['ogL%d' % m, 'ogT0_%d' % (m // 4), 'ogT1_%d' % (m // 4)])
        slabs = [slab(s_lwout, i * 512, 512, k_lwout) for i in range(2)]
        for t in range(2):
            proj_tok(pA[t], pAk[t], ogT, ['ogL%d' % m for m in range(8)], t, slabs)
            residual(buf, t, 1, 'y', pa=t)
            row0 = s * SEQ + (tile0 + t) * 128
            fw.dma('pool', y_p[row0:row0 + 128, :], xt[buf][:, t, :], reads=['xt%d_%d' % (buf, t)], writes=['o_y'])

    def phase2(s):
        make_gate(0, s)
        make_gate(1, s)
        fw.op('pool', lambda g: g.memset(hstate[:], 0.0), reads=['hst%d' % m for m in range(8)], writes=['hst%d' % m for m in range(8)])
        for m in range(8):
            fw.op('pool', lambda g, m=m: g.memset(xbT[:, m, 0:3], 0.0), reads=['xbT%d' % m], writes=['xh%d' % m])
        for sbi in range(min(NT // 2, NSB_LIMIT)):
            buf = sbi % 2
            row0 = s * SEQ + sbi * 256
            for t in range(2):
                fw.dma('sp', xt[buf][:, t, :], xp[row0 + t * 128:row0 + (t + 1) * 128, :], writes=['xt%d_%d' % (buf, t)])
            for t in range(2):
                norm_tile(xt[buf][:, t, :], 'xt%d_%d' % (buf, t), 'xn')
                to_hT(0, s, t * 128)
            qs = [slab(s_win, OQ + i * 512, 512, k_win) for i in range(2)]
            qis = [slab(s_win, OQI + i * 512, 512, k_win) for i in range(2)]
            for t in range(2):
                proj_tok(pA[0], pAk[0], hT, hTk, t, qs)
                proj_tok(pA[1], pAk[1], hT, hTk, t, qis)
                q_post(sbi * 2 + t, t)
                qi_post(sbi * 2 + t, t)
            gs = [slab(s_win, OG + i * 512, 512, k_win) for i in range(2)]
            for m in range(8):
                f = rr['f'] % 2
                rr['f'] += 1
                rt, rk = gs[m // 4]
                for kc in range(8):
                    fw.op('pe', lambda t_, kc=kc, rt=rt, m=m, f=f: t_.matmul(pFv[f], rt[:, kc, (m % 4) * 128:(m % 4 + 1) * 128], hT[:, kc, :],
                                                                           start=(kc == 0), stop=(kc == 7)),
                          reads=[hTk[kc], rk], writes=[pFk[f]])
                fw.op('act', lambda a, m=m, f=f: a.activation(sgT[:, m, :], pFv[f], AF.Silu), reads=[pFk[f]], writes=['sgT%d' % m])
            for kc in range(8):
                fw.op('pe', lambda t_, kc=kc: t_.matmul(pN[0:16, 0:256], wwi[:, kc, :], hT[:, kc, :], start=(kc == 0), stop=(kc == 7)),
                      reads=[hTk[kc], 'wwi'], writes=['pN'])
            fw.op('act', lambda a: a.activation(wiT[:], pN[0:16, 0:256], AF.Copy, scale=1.0 / 32.0), reads=['pN'], writes=['wiT'])
            for t in range(2):
                indexer(s, sbi * 2 + t, t)
            for t in range(2):
                attention(s, sbi * 2 + t, t)
            wos = [slab(s_wout0, i * 512, 512, k_wout0) for i in range(2)]
            for t in range(2):
                proj_tok(pA[t], pAk[t], ogT, ['ogT%d_%d' % (t, kv) for kv in range(2) for _ in range(4)], t, wos)
                residual(buf, t, 0, 'x1', pa=t)
            if STAGE >= 3:
                layer1(s, sbi, buf)
            else:
                for t in range(2):
                    r0 = row0 + t * 128
                    fw.dma('pool', y_p[r0:r0 + 128, :], xt[buf][:, t, :], reads=['xt%d_%d' % (buf, t)], writes=['o_y'])
        if STAGE >= 3:
            for j in range(3):
                fw.dma('pool', conv_p[s * 3 + j, :].rearrange("(c p) -> p c", p=128), xbT[:, :, j],
                       reads=['xh%d' % m for m in range(8)], writes=['o_conv'], allow_slow_non_contiguous=True)
            fw.dma('pool', h_p[s, :].rearrange("(c p) -> p c", p=128), hstate[:], reads=['hst%d' % m for m in range(8)], writes=['o_h'],
                   allow_slow_non_contiguous=True)

    def sample_path():
        from contextlib import ExitStack
        st = ExitStack()

        def sbs(name, shape, dt=F32):
            return st.enter_context(nc.sbuf_tensor(name, shape, dt))
        NS = 16
        NITS = 20
        RC = 6
        NCAND = RC * 8
        scale = 1.0 / math.sqrt(128.0)
        xs_t = sbs("xs_t", [NS, D])
        AT = sbs("AT", [128, 2, 8, NS])
        BT = sbs("BT", [128, 2, 8, NS])
        gate_s = sbs("gate_s", [NS, 2, D])
        kfs = sbs("kfs", [NS, 576])
        sgs = sbs("sgs", [NS, D])
        qiTs = sbs("qiTs", [128, 4, 64], BF16)
        ki2T = sbs("ki2T", [128, NS], BF16)
        wiTs = sbs("wiTs", [16, NS], BF16)
        wsm = sbs("wsm", [64, NS], BF16)
        wbig = sbs("wbig", [64, 4, 8, 128], BF16)
        Gb = [sbs("Gb%d" % i, [128, 1024]) for i in range(2)]
        KIc = [sbs("KIc%d" % i, [128, 2048], BF16) for i in range(2)]
        Rs = [sbs("Rs%d" % i, [64, 512], BF16) for i in range(2)]
        SCs = sbs("SCs", [128, 2052])
        cs = {k: sbs("cs_" + k, shp, dt) for k, (shp, dt) in {
            "rope_qs": ([NS, 32], F32), "rope_is": ([NS, 16], F32), "asel_s": ([16, 64], F32), "mask_s": ([64, NS], F32),
            "g128": ([128, 128], F32), "rep": ([NS, 128], F32), "repT": ([128, NS], F32), "maskE": ([128, 4], F32),
            "cbase": ([128, 1], F32), "selj": ([NS, 4 * NS], F32), "iota": ([128, 128], F32), "pws": ([128, 2 * (NITS + 1)], F32)}.items()}
        asel_sb = sbs("asel_sb", [16, 64], BF16)
        ptT = sbs("ptT_sb", [128, 4], I32)
        pt128i = sbs("pt128i", [128, 128], I32)
        pt128f = sbs("pt128f", [128, 128])
        onesf = sbs("onesf", [2, 128])
        cnb = sbs("cnb", [128, 1])
        th = sbs("th", [128, 16])
        wks = sbs("wks", [128, 2, NITS + 1])
        cval = sbs("cval", [128, NCAND])
        cidx = sbs("cidx", [128, NCAND], mybir.dt.uint32)
        cf = sbs("cf", [128, 6, NCAND])
        physi = sbs("physi", [128, NCAND], I32)
        oh = stage[:].rearrange("p (s g) -> p s g", g=128)
        Kg = [sbs("Kg0", [128, 8, 256])]
        Vg = [sbs("Vg0", [128, 8, 256])]
        qrep = sbs("qrep", [128, D])
        acc = sbs("acc", [128, D])
        prod = sbs("prod", [128, 8, 128])
        sc8 = sbs("sc8", [128, 8, 8])
        pe8 = sbs("pe8", [128, 8, 8])
        dens = sbs("dens", [128, 2, 8])
        k2rep = Kg[0][0:NS, 0:4, :]
        v2rep = Vg[0][0:NS, 0:4, :]
        scn = sbs("scn", [NS, 3, 8, 4])
        accn = sbs("accn", [NS, D])
        ogb = sbs("ogb", [NS, D], BF16)
        xbs = sbs("xbs", [128, 8, 4, 7])
        xcs = sbs("xcs", [128, 2, 8, NS])
        xcbs = sbs("xcbs", [128, 8, NS], BF16)
        sg1s = sbs("sg1s", [128, 8, NS])
        rg = sbs("rg", [128, 6, 8, NS])
        hst = sbs("hst", [128, 2, 8, 4])
        hss = sbs("hss", [128, 8, 4, 4])
        cso = sbs("cso", [128, 8, 12])
        tok12 = accn

        for k in cs:
            fw.dma('sp', cs[k][:], cst[k], writes=['cs_' + k])
        fw.op('dve', lambda v: v.tensor_copy(asel_sb[:], cs["asel_s"][:]), reads=['cs_asel_s'], writes=['asel_sb'])
        fw.dma('sp', ptT[:], d_ptT, writes=['ptT'])
        fw.dma('sp', pt128i[:], d_pt128, writes=['pt128i'])
        fw.op('dve', lambda v: v.tensor_copy(pt128f[:], pt128i[:]), reads=['pt128i'], writes=['pt128f'])
        pt8f = sbs("pt8f", [128, 4, 8])
        pt8i = sbs("pt8i", [128, 4, 8], I32)
        fw.op('dve', lambda v: v.tensor_copy(pt8f[:, :, 0], ptT[:]), reads=['ptT'], writes=['pt8f'])
        fw.op('dve', lambda v: v.tensor_scalar(pt8f[:], pt8f[:, :, 0:1].to_broadcast([128, 4, 8]), 8.0, None, ALU.mult), reads=['pt8f'], writes=['pt8f'])
        fw.op('dve', lambda v: v.tensor_tensor(pt8f[:], pt8f[:], cs["iota"][:, 0:8].unsqueeze(1).to_broadcast([128, 4, 8]), ALU.add), reads=['pt8f', 'cs_iota'], writes=['pt8f'])
        fw.op('dve', lambda v: v.tensor_copy(pt8i[:], pt8f[:]), reads=['pt8f'], writes=['pt8i'])
        fw.op('pool', lambda g: g.memset(onesf[:], 1.0), writes=['onesf'])
        fw.op('pool', lambda g: g.memset(cnb[:], -1e30), writes=['cnb'])
        fw.op('pool', lambda g: g.memset(wbig[:], 0.0), writes=['wbig'])
        for l in range(2):
            for g_ in range(4):
                fw.op('dve', lambda v, l=l, g_=g_: v.tensor_copy(AT[:, l, :, 4 * g_:4 * g_ + 4], Sfm[l][:, :, 2 + g_].unsqueeze(2).to_broadcast([128, 8, 4])),
                      reads=modkeys[l], writes=['AT%d_%d' % (l, g_)])
                fw.op('dve', lambda v, l=l, g_=g_: v.tensor_copy(BT[:, l, :, 4 * g_:4 * g_ + 4], Bfm[l][:, :, 2 + g_].unsqueeze(2).to_broadcast([128, 8, 4])),
                      reads=modkeys[l], writes=['BT%d_%d' % (l, g_)])
            for h in range(2):
                fw.op('pe', lambda t, h=h, l=l: t.matmul(pA[0][0:NS, h * 512:(h + 1) * 512], self_[:, 2, 0:NS], mgate[l][:, h * 512:(h + 1) * 512], start=True, stop=True),
                      reads=['sel', 'mgate%d_%d' % (l, 4 + h)], writes=[pAk[0][h]])
                fw.op('act', lambda a, h=h, l=l: a.copy(gate_s[:, l, h * 512:(h + 1) * 512], pA[0][0:NS, h * 512:(h + 1) * 512]),
                      reads=[pAk[0][h]], writes=['gate_s%d_%d' % (l, h)])
        ATk = [['AT%d_%d' % (l, g_) for g_ in range(4)] + ['BT%d_%d' % (l, g_) for g_ in range(4)] for l in range(2)]

        def to_hT_s(l):
            for c in range(8):
                fw.op('pe', lambda t, c=c: t.transpose(pT[:, c, 0:NS], xn[0:NS, c * 128:(c + 1) * 128], identb[0:NS, 0:NS]),
                      reads=['xn', 'identb'], writes=['pT'])
            fw.op('dve', lambda v: v.tensor_tensor(qrep[:, 0:128].rearrange("p (c t) -> p c t", t=NS), pT[:, :, 0:NS], AT[:, l], ALU.mult),
                  reads=['pT'] + ATk[l], writes=['qrep'])
            fw.op('dve', lambda v: v.tensor_tensor(hT[:, :, 0:NS], qrep[:, 0:128].rearrange("p (c t) -> p c t", t=NS), BT[:, l], ALU.add),
                  reads=['qrep'] + ATk[l], writes=hTk)

        def proj_s(o, okeys, slabs, ncols=512):
            for h in range(len(slabs)):
                rt, rk = slabs[h]
                for kc in range(8):
                    fw.op('pe', lambda t_, kc=kc, rt=rt, h=h: t_.matmul(o[0:NS, h * 512:h * 512 + ncols], hT[:, kc, 0:NS], rt[:, kc, 0:ncols],
                                                                      start=(kc == 0), stop=(kc == 7)),
                          reads=[hTk[kc], rk], writes=[okeys[h]])

        def rope_s(e, x1, x2, cs_, sn_, H, half, xkey):
            shape = [NS, H, half]
            cb_ = cs_.unsqueeze(1).to_broadcast(shape)
            sb_ = sn_.unsqueeze(1).to_broadcast(shape)
            t = [rt_[0:NS, i, 0:H * half].rearrange("p (h d) -> p h d", d=half) for i in range(4)]
            fw.op(e, lambda g: g.tensor_tensor(t[0], x1, cb_, ALU.mult), reads=xkey, writes=['rt0'])
            fw.op(e, lambda g: g.tensor_tensor(t[1], x2, sb_, ALU.mult), reads=xkey, writes=['rt1'])
            fw.op(e, lambda g: g.tensor_tensor(t[2], x2, cb_, ALU.mult), reads=xkey, writes=['rt2'])
            fw.op(e, lambda g: g.tensor_tensor(t[3], x1, sb_, ALU.mult), reads=xkey, writes=['rt3'])
            return t
        rtk = ['rt0', 'rt1', 'rt2', 'rt3']

        fw.dma('sp', xs_t[:], d_xs, writes=['xs_t'])
        norm_tile(xs_t[:], 'xs_t', 'xn', n=NS)
        to_hT_s(0)
        proj_s(pA[0], pAk[0], [slab(s_win, OQ + i * 512, 512, k_win) for i in range(2)])
        o = pA[0]
        for h in range(8):
            fw.op('act', lambda a, h=h: a.activation(junk[0:NS, 0:128], o[0:NS, h * 128:(h + 1) * 128], AF.Square,
                                                     scale=1.0 / math.sqrt(128.0), accum_out=small[0:NS, 16 + h:17 + h]),
                  reads=[pAk[0][h // 4]], writes=['junk', 'sq%d' % h])
        sqk = ['sq%d' % h for h in range(8)]
        fw.op('act', lambda a: a.activation(small[0:NS, 24:32], small[0:NS, 16:24], AF.Sqrt, bias=epsc[0:NS, 0:1], scale=1.0),
              reads=sqk + ['epsc'], writes=['sq_s'])
        fw.op('dve', lambda v: v.reciprocal(small[0:NS, 32:40], small[0:NS, 24:32]), reads=['sq_s'], writes=['sq_r'])
        q3 = qf[0:NS, :].rearrange("p (h d) -> p h d", d=128)
        fw.op('dve', lambda v: v.tensor_tensor(q3, o[0:NS, :].rearrange("p (h d) -> p h d", d=128),
                                               small[0:NS, 32:40].unsqueeze(2).to_broadcast([NS, 8, 128]), ALU.mult),
              reads=['pA0a', 'pA0b', 'sq_r'], writes=['qf'])
        fw.op('dve', lambda v: v.tensor_tensor(q3, q3, qn_bc[0:NS, :].unsqueeze(1).to_broadcast([NS, 8, 128]), ALU.mult),
              reads=['qf', 'qn_bc'], writes=['qf'])
        tt = rope_s('dve', q3[:, :, 0:16], q3[:, :, 16:32], cs["rope_qs"][:, 0:16], cs["rope_qs"][:, 16:32], 8, 16, ['qf', 'cs_rope_qs'])
        fw.op('dve', lambda v: v.tensor_tensor(q3[:, :, 0:16], tt[0], tt[1], ALU.subtract), reads=rtk, writes=['qf'])
        fw.op('dve', lambda v: v.tensor_tensor(q3[:, :, 16:32], tt[2], tt[3], ALU.add), reads=rtk, writes=['qf'])
        for h in range(2):
            fw.op('pe', lambda t, h=h: t.matmul(pA[0][:, h * 512:(h + 1) * 512], cs["rep"][:], qf[0:NS, h * 512:(h + 1) * 512], start=True, stop=True),
                  reads=['cs_rep', 'qf'], writes=[pAk[0][h]])
            fw.op('act', lambda a, h=h: a.copy(qrep[:, h * 512:(h + 1) * 512], pA[0][:, h * 512:(h + 1) * 512]), reads=[pAk[0][h]], writes=['qrep'])
        proj_s(pA[1], pAk[1], [slab(s_win, OK_, 512, k_win)])
        for kc in range(8):
            fw.op('pe', lambda t, kc=kc: t.matmul(pA[1][0:NS, 512:576], hT[:, kc, 0:NS], wki[:, kc, :], start=(kc == 0), stop=(kc == 7)),
                  reads=[hTk[kc], 'wki'], writes=['pA1b'])
        o = pA[1]
        for h in range(2):
            fw.op('act', lambda a, h=h: a.activation(junk[0:NS, 0:128], o[0:NS, h * 128:(h + 1) * 128], AF.Square,
                                                     scale=1.0 / math.sqrt(128.0), accum_out=small[0:NS, 4 + h:5 + h]),
                  reads=['pA1a'], writes=['junk', 'sm4_%d' % h])
        fw.op('act', lambda a: a.activation(small[0:NS, 6:8], small[0:NS, 4:6], AF.Sqrt, bias=epsc[0:NS, 0:1], scale=1.0),
              reads=['sm4_0', 'sm4_1', 'epsc'], writes=['sm6'])
        fw.op('dve', lambda v: v.reciprocal(small[0:NS, 8:10], small[0:NS, 6:8]), reads=['sm6'], writes=['sm8'])
        kv3 = kfs[:, 0:256].rearrange("p (h d) -> p h d", d=128)
        fw.op('dve', lambda v: v.tensor_tensor(kv3, o[0:NS, 0:256].rearrange("p (h d) -> p h d", d=128),
                                               small[0:NS, 8:10].unsqueeze(2).to_broadcast([NS, 2, 128]), ALU.mult),
              reads=['pA1a', 'sm8'], writes=['kfs_k'])
        fw.op('dve', lambda v: v.tensor_tensor(kv3, kv3, kn_bc[0:NS, :].unsqueeze(1).to_broadcast([NS, 2, 128]), ALU.mult),
              reads=['kfs_k', 'kn_bc'], writes=['kfs_k'])
        tt = rope_s('dve', kv3[:, :, 0:16], kv3[:, :, 16:32], cs["rope_qs"][:, 0:16], cs["rope_qs"][:, 16:32], 2, 16, ['kfs_k', 'cs_rope_qs'])
        fw.op('dve', lambda v: v.tensor_tensor(kv3[:, :, 0:16], tt[0], tt[1], ALU.subtract), reads=rtk, writes=['kfs_k'])
        fw.op('dve', lambda v: v.tensor_tensor(kv3[:, :, 16:32], tt[2], tt[3], ALU.add), reads=rtk, writes=['kfs_k'])
        fw.op('act', lambda a: a.copy(kfs[:, 256:512], o[0:NS, 256:512]), reads=['pA1a'], writes=['kfs_v'])
        fw.op('act', lambda a: a.copy(kfs[:, 512:576], o[0:NS, 512:576]), reads=['pA1b'], writes=['kfs_i'])
        y1 = kfs[:, 512:520].unsqueeze(1)
        y2 = kfs[:, 520:528].unsqueeze(1)
        tt = rope_s('dve', y1, y2, cs["rope_is"][:, 0:8], cs["rope_is"][:, 8:16], 1, 8, ['kfs_i', 'cs_rope_is'])
        fw.op('dve', lambda v: v.tensor_tensor(y1, tt[0], tt[1], ALU.subtract), reads=rtk, writes=['kfs_i'])
        fw.op('dve', lambda v: v.tensor_tensor(y2, tt[2], tt[3], ALU.add), reads=rtk, writes=['kfs_i'])
        fw.dma('pool', k_s, kfs[:, 0:256], reads=['kfs_k'], writes=['o_ks'])
        fw.dma('pool', v_s, kfs[:, 256:512], reads=['kfs_v'], writes=['o_vs'])
        fw.dma('pool', ik_s, kfs[:, 512:576], reads=['kfs_i'], writes=['o_iks'])
        fw.op('pe', lambda t: t.transpose(pS[0:64, 0:NS], kfs[:, 512:576], identf[0:NS, 0:NS]), reads=['kfs_i', 'identf'], writes=['pS'])
        fw.op('dve', lambda v: v.tensor_copy(ki2T[0:64, :], pS[0:64, 0:NS]), reads=['pS'], writes=['ki2Ta'])
        fw.op('dve', lambda v: v.tensor_copy(ki2T[64:128, :], pS[0:64, 0:NS]), reads=['pS'], writes=['ki2Tb'])
        proj_s(pA[0], pAk[0], [slab(s_win, OQI + i * 512, 512, k_win) for i in range(2)])
        o = pA[0]
        fw.op('act', lambda a: a.copy(qb[0:NS, :], o[0:NS, :]), reads=['pA0a', 'pA0b'], writes=['qb'])
        o3 = o[0:NS, :].rearrange("p (h d) -> p h d", d=64)
        b3 = qb[0:NS, :].rearrange("p (h d) -> p h d", d=64)
        tt = rope_s('dve', o3[:, :, 0:8], o3[:, :, 8:16], cs["rope_is"][:, 0:8], cs["rope_is"][:, 8:16], 16, 8, ['pA0a', 'pA0b', 'cs_rope_is'])
        fw.op('dve', lambda v: v.tensor_tensor(b3[:, :, 0:8], tt[0], tt[1], ALU.subtract), reads=rtk, writes=['qb'])
        fw.op('dve', lambda v: v.tensor_tensor(b3[:, :, 8:16], tt[2], tt[3], ALU.add), reads=rtk, writes=['qb'])
        for c in range(8):
            fw.op('pe', lambda t_, c=c: t_.transpose(pT[:, c, 0:NS], qb[0:NS, c * 128:(c + 1) * 128], identb[0:NS, 0:NS]),
                  reads=['qb', 'identb'], writes=['pT'])
        fw.op('pool', lambda g: g.memset(qiTs[:], 0.0), writes=['qiTs'])
        for half in (0, 1):
            fw.op('dve', lambda v: v.tensor_copy(qiTs[half * 64:(half + 1) * 64, :, half * 32:(half + 1) * 32].rearrange("p b (c t) -> p b c t", t=4),
                                                 pT[half * 64:(half + 1) * 64, :, 0:NS].rearrange("p c (b t) -> p b c t", t=4)), reads=['pT', 'qiTs'], writes=['qiTs'])
        proj_s(pA[1], pAk[1], [slab(s_win, OG + i * 512, 512, k_win) for i in range(2)])
        fw.op('act', lambda a: a.activation(sgs[:], pA[1][0:NS, :], AF.Silu), reads=['pA1a', 'pA1b'], writes=['sgs'])
        for kc in range(8):
            fw.op('pe', lambda t_, kc=kc: t_.matmul(pN[0:16, 0:NS], wwi[:, kc, :], hT[:, kc, 0:NS], start=(kc == 0), stop=(kc == 7)),
                  reads=[hTk[kc], 'wwi'], writes=['pN'])
        fw.op('act', lambda a: a.activation(wiTs[:], pN[0:16, 0:NS], AF.Copy, scale=1.0 / 32.0), reads=['pN'], writes=['wiTs'])
        fw.op('pe', lambda t_: t_.matmul(pN[0:64, 0:NS], asel_sb[:], wiTs[:], start=True, stop=True), reads=['asel_sb', 'wiTs'], writes=['pN'])
        fw.op('dve', lambda v: v.tensor_tensor(wsm[:], pN[0:64, 0:NS], cs["mask_s"][:], ALU.mult), reads=['pN', 'cs_mask_s'], writes=['wsm'])
        for bl in range(4):
            for cb in range(8):
                fw.op('pool', lambda g, bl=bl, cb=cb: g.tensor_copy(wbig[:, bl, cb, cb * 16 + bl * 4:cb * 16 + bl * 4 + 4], wsm[:, bl * 4:bl * 4 + 4]),
                      reads=['wsm'], writes=['wbig'])

        accb = [pA[0][:, 0:512], pA[0][:, 512:1024], pA[1][:, 0:512], pA[1][:, 512:1024]]
        acck = ['pA0a', 'pA0b', 'pA1a', 'pA1b']
        pFd = pF[:].rearrange("p a b -> p (a b)")
        trb = [pS, pN]
        trk = ['pS', 'pN']
        n_it = 0
        for bl in range(4):
            for cb in range(8):
                gi = n_it % 2
                fw.dma('pool', Gb[gi][:], d_cik[:, :], reads=['pt8i'], writes=['Gb%d' % gi],
                       indirect=dict(out_offset=None, in_offset=bass.IndirectOffsetOnAxis(ap=pt8i[:, bl, cb:cb + 1], axis=0)))
                for q4 in range(4):
                    tb = (n_it * 4 + q4) % 2
                    for s4 in range(4):
                        sl = q4 * 4 + s4
                        fw.op('pe', lambda t, tb=tb, s4=s4, sl=sl: t.transpose(trb[tb][0:64, s4 * 128:(s4 + 1) * 128], Gb[gi][:, sl * 64:(sl + 1) * 64], identf[:]),
                              reads=['Gb%d' % gi, 'identf'], writes=[trk[tb]])
                    fw.op('act', lambda a, tb=tb, q4=q4: a.copy(KIc[gi][0:64, q4 * 512:(q4 + 1) * 512], trb[tb][0:64, :]), reads=[trk[tb]], writes=['KIc%d_%da' % (gi, q4)])
                    fw.op('dve', lambda v, tb=tb, q4=q4: v.tensor_copy(KIc[gi][64:128, q4 * 512:(q4 + 1) * 512], trb[tb][0:64, :]), reads=[trk[tb]], writes=['KIc%d_%db' % (gi, q4)])
                for q4 in range(4):
                    kk = ['KIc%d_%da' % (gi, q4), 'KIc%d_%db' % (gi, q4)]
                    fw.op('pe', lambda t, q4=q4: t.matmul(pFd[0:64, :], qiTs[:, bl, :], KIc[gi][:, q4 * 512:(q4 + 1) * 512], start=True, stop=True),
                          reads=['qiTs'] + kk, writes=['pF0'])
                    ri = (n_it * 4 + q4) % 2
                    if q4 % 2 == 0:
                        fw.op('act', lambda a, ri=ri: a.activation(Rs[ri][:], pFd[0:64, :], AF.Relu), reads=['pF0'], writes=['Rs%d' % ri])
                    else:
                        fw.op('dve', lambda v, ri=ri: v.tensor_scalar_max(Rs[ri][:], pFd[0:64, :], 0.0), reads=['pF0'], writes=['Rs%d' % ri])
                    first = (bl == 0 and cb == 0)
                    last = (bl == 3 and cb == 7)
                    fw.op('pe', lambda t, q4=q4, ri=ri: t.matmul(accb[q4], wbig[:, bl, cb, :], Rs[ri][:], start=first, stop=last),
                          reads=['wbig', 'Rs%d' % ri], writes=[acck[q4]])
                n_it += 1
        for q4 in range(4):
            fw.op('act' if q4 % 2 == 0 else 'dve', lambda e_, q4=q4: (e_.copy if q4 % 2 == 0 else e_.tensor_copy)(SCs[:, q4 * 512:(q4 + 1) * 512], accb[q4]),
                  reads=[acck[q4]], writes=['SCs%d' % q4])
        for bl in range(4):
            fw.op('pe', lambda t, bl=bl: t.matmul(pFd[0:64, bl * 4:bl * 4 + 4], qiTs[:, bl, :], ki2T[:, bl * 4:bl * 4 + 4], start=True, stop=True),
                  reads=['qiTs', 'ki2Ta', 'ki2Tb'], writes=['pF0'])
        fw.op('act', lambda a: a.activation(Rs[0][:, 0:NS], pFd[0:64, 0:NS], AF.Relu), reads=['pF0'], writes=['Rs0'])
        for bl in range(4):
            fw.op('pe', lambda t, bl=bl: t.matmul(pS[:, 0:4], wbig[:, bl, 0, :], Rs[0][:, bl * 4:bl * 4 + 4], start=(bl == 0), stop=(bl == 3)),
                  reads=['wbig', 'Rs0'], writes=['pS'])
        fw.op('dve', lambda v: v.tensor_tensor(SCs[:, 2048:2052], pS[:, 0:4], cs["maskE"][:], ALU.add), reads=['pS', 'cs_maskE'], writes=['SCs4'])
        SCk = ['SCs%d' % i for i in range(5)]
        fw.op('dve', lambda v: v.tensor_reduce(th[:, 0:1], SCs[:, 0:2048], AX.X, ALU.max), reads=SCk, writes=['th0'])
        fw.op('dve', lambda v: v.tensor_reduce(th[:, 1:2], SCs[:, 0:2048], AX.X, ALU.min), reads=SCk, writes=['th1'])
        fw.op('dve', lambda v: v.tensor_scalar_mul(th[:, 1:2], th[:, 1:2], -1.0), reads=['th1'], writes=['th1'])
        fw.op('pe', lambda t: t.transpose(pN[0:2, 0:128], th[:, 0:2], identf[:]), reads=['th0', 'th1', 'identf'], writes=['pN'])
        fw.op('dve', lambda v: v.tensor_reduce(th[0:2, 2:3], pN[0:2, 0:128], AX.X, ALU.max), reads=['pN'], writes=['th2'])
        fw.op('dve', lambda v: v.tensor_scalar(th[0:2, 4:6], identf[0:2, 0:2], th[0:2, 2:3], None, ALU.mult), reads=['identf', 'th2'], writes=['th4'])
        fw.op('pe', lambda t: t.matmul(pN[:, 0:2], onesf[:], th[0:2, 4:6], start=True, stop=True), reads=['onesf', 'th4'], writes=['pN'])
        fw.op('dve', lambda v: v.tensor_copy(th[:, 6:8], pN[:, 0:2]), reads=['pN'], writes=['th6'])
        fw.op('dve', lambda v: v.tensor_tensor(th[:, 8:9], th[:, 6:7], th[:, 7:8], ALU.add), reads=['th6'], writes=['th8'])
        fw.op('dve', lambda v: v.tensor_scalar(th[:, 9:10], th[:, 8:9], 1.02, 2e-3, ALU.mult, ALU.add), reads=['th8'], writes=['th9'])
        fw.op('dve', lambda v: v.tensor_scalar(th[:, 10:11], th[:, 8:9], -0.01, -1e-3, ALU.mult, ALU.add), reads=['th8'], writes=['th10'])
        fw.op('dve', lambda v: v.tensor_tensor(th[:, 10:11], th[:, 10:11], th[:, 7:8], ALU.subtract), reads=['th10', 'th6'], writes=['th10'])
        pws = cs["pws"][:].rearrange("p (a b) -> p a b", a=2)
        fw.op('dve', lambda v: v.tensor_scalar(wks[:, 0, :], pws[:, 0, :], th[:, 9:10], None, ALU.mult), reads=['cs_pws', 'th9'], writes=['wks0'])
        fw.op('dve', lambda v: v.tensor_scalar(wks[:, 1, :], pws[:, 1, :], th[:, 9:10], None, ALU.mult), reads=['cs_pws', 'th9'], writes=['wks1'])
        fw.op('dve', lambda v: v.tensor_tensor(th[:, 11:12], th[:, 10:11], wks[:, 1, 0:1], ALU.add), reads=['th10', 'wks1'], writes=['mids'])
        for it in range(NITS):
            fw.op('dve', lambda v: v.tensor_scalar(stage[:, 0:2048], SCs[:, 0:2048], th[:, 11:12], None, ALU.is_ge, ALU.add, accum_out=th[:, 12:13]),
                  reads=SCk + ['mids', 'stage'], writes=['stage', 'cnta'])
            fw.op('dve', lambda v: v.tensor_scalar(junk[:, 0:4], SCs[:, 2048:2052], th[:, 11:12], None, ALU.is_ge, ALU.add, accum_out=th[:, 13:14]),
                  reads=SCk + ['mids'], writes=['junk', 'cntb'])
            fw.op('dve', lambda v: v.tensor_tensor(th[:, 12:13], th[:, 12:13], th[:, 13:14], ALU.add), reads=['cnta', 'cntb'], writes=['cnta'])
            fw.op('pe', lambda t: t.matmul(pN[:, 0:1], cs["g128"][:], th[:, 12:13], start=True, stop=True), reads=['cs_g128', 'cnta'], writes=['pN'])
            fw.op('dve', lambda v, it=it: v.tensor_scalar(th[:, 14:15], pN[:, 0:1], c255[:, 0:1], wks[:, 1, it:it + 1], ALU.is_ge, ALU.mult),
                  reads=['pN', 'wks1', 'c255'], writes=['dlts'])
            fw.op('dve', lambda v, it=it: v.scalar_tensor_tensor(th[:, 11:12], th[:, 11:12], wks[:, 0, it:it + 1], th[:, 14:15], ALU.subtract, ALU.add),
                  reads=['mids', 'wks0', 'dlts'], writes=['mids'])
        fw.op('dve', lambda v: v.tensor_tensor(th[:, 11:12], th[:, 11:12], wks[:, 0, NITS - 1:NITS], ALU.subtract), reads=['mids', 'wks0'], writes=['mids'])
        fw.op('dve', lambda v: v.tensor_scalar(stage[:], SCs[:, 0:2048], th[:, 11:12], cnb[:, 0:1], ALU.is_lt, ALU.mult), reads=SCk + ['mids', 'cnb', 'stage'], writes=['stage'])
        fw.op('dve', lambda v: v.tensor_tensor(stage[:], stage[:], SCs[:, 0:2048], ALU.add), reads=['stage'] + SCk, writes=['stage'])
        fw.op('dve', lambda v: v.tensor_scalar(scn[:, 2, 0, :], SCs[0:NS, 2048:2052], th[0:NS, 11:12], cneg[0:NS, 0:1], ALU.is_lt, ALU.mult),
              reads=SCk + ['mids', 'cneg'], writes=['nbias'])
        for r in range(RC):
            cv = cval[:, r * 8:(r + 1) * 8]
            fw.op('dve', lambda v: v.max(out=cv, in_=stage[:]), reads=['stage'], writes=['cval%d' % r])
            fw.op('dve', lambda v: v.max_index(out=cidx[:, r * 8:(r + 1) * 8], in_max=cv, in_values=stage[:]), reads=['stage', 'cval%d' % r], writes=['cidx%d' % r])
            fw.op('dve', lambda v: v.match_replace(out=stage[:], in_to_replace=cv, in_values=stage[:], imm_value=-1e30),
                  reads=['stage', 'cval%d' % r, 'cidx%d' % r], writes=['stage'])
        cvk = ['cval%d' % r for r in range(RC)]
        cik_ = ['cidx%d' % r for r in range(RC)]
        fw.op('dve', lambda v: v.tensor_copy(cf[:, 0, :], cidx[:]), reads=cik_, writes=['cf0'])
        thr16 = th[:, 0:16]
        fw.op('dve', lambda v: v.tensor_scalar(thr16, cs["iota"][:, 0:16], 1.0, 128.0, ALU.add, ALU.mult), reads=['cs_iota', 'mids', 'th0', 'th1', 'th2', 'th4', 'th6', 'th8', 'th9', 'th10', 'cnta', 'cntb', 'dlts', 'nbias'], writes=['thr16'])
        cmp3 = stage[:, 0:NCAND * 16].rearrange("p (s k) -> p s k", k=16)
        fw.op('dve', lambda v: v.tensor_tensor(cmp3, cf[:, 0, :].unsqueeze(2).to_broadcast([128, NCAND, 16]), thr16.unsqueeze(1).to_broadcast([128, NCAND, 16]), ALU.is_ge),
              reads=['cf0', 'thr16', 'stage'], writes=['stage'])
        fw.op('dve', lambda v: v.tensor_reduce(cf[:, 2, :], cmp3, AX.X, ALU.add), reads=['stage'], writes=['cf2'])
        fw.op('dve', lambda v: v.scalar_tensor_tensor(cf[:, 1, :], cf[:, 2, :], -128.0, cf[:, 0, :], ALU.mult, ALU.add), reads=['cf2', 'cf0'], writes=['cf1'])
        fw.op('dve', lambda v: v.tensor_scalar(cf[:, 2, :], cf[:, 2, :], cs["cbase"][:, 0:1], None, ALU.add), reads=['cf2', 'cf1', 'cs_cbase'], writes=['cf2'])
        for hf in range(NCAND // 16):
            pgv = cf[:, 1, hf * 16:(hf + 1) * 16]
            fw.op('dve', lambda v: v.tensor_tensor(oh, cs["iota"][:].unsqueeze(1).to_broadcast([128, 16, 128]), pgv.unsqueeze(2).to_broadcast([128, 16, 128]), ALU.is_equal),
                  reads=['cs_iota', 'cf1', 'stage'], writes=['stage'])
            fw.op('dve', lambda v: v.tensor_tensor(oh, oh, pt128f[:].unsqueeze(1).to_broadcast([128, 16, 128]), ALU.mult), reads=['stage', 'pt128f'], writes=['stage'])
            fw.op('dve', lambda v, hf=hf: v.tensor_reduce(cf[:, 3, hf * 16:(hf + 1) * 16], oh, AX.X, ALU.add), reads=['stage'], writes=['cf3_%d' % hf])
        fw.op('dve', lambda v: v.scalar_tensor_tensor(cf[:, 4, :], cf[:, 3, :], 128.0, cf[:, 2, :], ALU.mult, ALU.add), reads=['cf3_%d' % hf for hf in range(NCAND // 16)] + ['cf2'], writes=['cf4'])
        fw.op('dve', lambda v: v.tensor_copy(physi[:], cf[:, 4, :]), reads=['cf4'], writes=['physi'])
        fw.op('dve', lambda v: v.tensor_scalar(cf[:, 5, :], cval[:], -1e29, NEG, ALU.is_lt, ALU.mult), reads=cvk, writes=['cf5'])
        fw.op('pool', lambda g: g.memset(acc[:], 0.0), writes=['acc'])
        fw.op('pool', lambda g: g.memset(dens[:, 0, :], 0.0), writes=['dens'])
        for gi_ in range(NCAND // 8):
            gb = 0
            if os.environ.get('K_MULTIGATHER', '0') == '1':
                fw.dma('pool', Kg[gb][:], d_ck[:, :], reads=['physi'], writes=['Kg%d_%d' % (gb, s_) for s_ in range(8)],
                       indirect=dict(out_offset=None, in_offset=bass.IndirectOffsetOnAxis(ap=physi[:, gi_ * 8:gi_ * 8 + 8], axis=0)))
                fw.dma('pool', Vg[gb][:], d_cv[:, :], reads=['physi'], writes=['Vg%d_%d' % (gb, s_) for s_ in range(8)],
                       indirect=dict(out_offset=None, in_offset=bass.IndirectOffsetOnAxis(ap=physi[:, gi_ * 8:gi_ * 8 + 8], axis=0)))
            else:
              for s_ in range(8):
                col = gi_ * 8 + s_
                fw.dma('pool', Kg[gb][:, s_, :], d_ck[:, :], reads=['physi'], writes=['Kg%d_%d' % (gb, s_)],
                       indirect=dict(out_offset=None, in_offset=bass.IndirectOffsetOnAxis(ap=physi[:, col:col + 1], axis=0)))
                fw.dma('pool', Vg[gb][:, s_, :], d_cv[:, :], reads=['physi'], writes=['Vg%d_%d' % (gb, s_)],
                       indirect=dict(out_offset=None, in_offset=bass.IndirectOffsetOnAxis(ap=physi[:, col:col + 1], axis=0)))
            Kk = ['Kg%d_%d' % (gb, s_) for s_ in range(8)]
            Vk = ['Vg%d_%d' % (gb, s_) for s_ in range(8)]
            for h in range(8):
                kvh = h // 4
                fw.op('pool', lambda g, h=h, kvh=kvh: g.tensor_tensor(prod[:], Kg[gb][:, :, kvh * 128:(kvh + 1) * 128],
                                                                      qrep[:, h * 128:(h + 1) * 128].unsqueeze(1).to_broadcast([128, 8, 128]), ALU.mult),
                      reads=Kk + ['qrep'], writes=['prod'])
                fw.op('dve', lambda v, h=h: v.tensor_reduce(sc8[:, h, :], prod[:], AX.X, ALU.add), reads=['prod'], writes=['sc8_%d' % h])
            sck8 = ['sc8_%d' % h for h in range(8)]
            fw.op('dve', lambda v: v.scalar_tensor_tensor(sc8[:], sc8[:], scale, cf[:, 5, gi_ * 8:(gi_ + 1) * 8].unsqueeze(1).to_broadcast([128, 8, 8]), ALU.mult, ALU.add),
                  reads=sck8 + ['cf5'], writes=['sc8'])
            fw.op('act', lambda a: a.activation(pe8[:], sc8[:], AF.Exp), reads=['sc8'], writes=['pe8'])
            fw.op('dve', lambda v: v.tensor_reduce(dens[:, 1, :], pe8[:], AX.X, ALU.add), reads=['pe8'], writes=['dens1'])
            fw.op('dve', lambda v: v.tensor_tensor(dens[:, 0, :], dens[:, 0, :], dens[:, 1, :], ALU.add), reads=['dens', 'dens1'], writes=['dens'])
            for s_ in range(8):
                vv = Vg[gb][:, s_, :].rearrange("p (k d) -> p k d", k=2).unsqueeze(2).to_broadcast([128, 2, 4, 128])
                pp = pe8[:, :, s_].rearrange("p (k g) -> p k g", k=2).unsqueeze(3).to_broadcast([128, 2, 4, 128])
                p4 = prod[:].rearrange("p (k g) d -> p k g d", k=2)
                fw.op('dve', lambda v: v.tensor_tensor(p4, vv, pp, ALU.mult), reads=Vk + ['pe8', 'prod'] + sck8, writes=['prod'])
                fw.op('pool', lambda g: g.tensor_tensor(acc[:], acc[:], prod[:].rearrange("p h d -> p (h d)"), ALU.add), reads=['acc', 'prod'], writes=['acc'])
        for h in range(2):
            fw.op('pe', lambda t, h=h: t.matmul(pA[0][0:NS, h * 512:(h + 1) * 512], cs["repT"][:], acc[:, h * 512:(h + 1) * 512], start=True, stop=True),
                  reads=['cs_repT', 'acc'], writes=[pAk[0][h]])
        fw.op('pe', lambda t: t.matmul(pN[0:NS, 0:8], cs["repT"][:], dens[:, 0, :], start=True, stop=True), reads=['cs_repT', 'dens'], writes=['pN'])
        selj = cs["selj"][:].rearrange("p (j q) -> p j q", j=4)
        for j in range(4):
            fw.op('pe', lambda t, j=j: t.matmul(pA[1][0:NS, j * 256:(j + 1) * 256], selj[:, j, :], kfs[:, 0:256], start=True, stop=True),
                  reads=['cs_selj', 'kfs_k'], writes=[pAk[1][j // 2]])
        fw.op('act', lambda a: a.copy(k2rep[:].rearrange("p j d -> p (j d)"), pA[1][0:NS, :]), reads=['pA1a', 'pA1b'], writes=['k2rep'] + ['Kg0_%d' % s_ for s_ in range(8)])
        for j in range(4):
            fw.op('pe', lambda t, j=j: t.matmul(pA[1][0:NS, j * 256:(j + 1) * 256], selj[:, j, :], kfs[:, 256:512], start=True, stop=True),
                  reads=['cs_selj', 'kfs_v'], writes=[pAk[1][j // 2]])
        fw.op('act', lambda a: a.copy(v2rep[:].rearrange("p j d -> p (j d)"), pA[1][0:NS, :]), reads=['pA1a', 'pA1b'], writes=['v2rep'] + ['Vg0_%d' % s_ for s_ in range(8)])
        for h in range(8):
            kvh = h // 4
            fw.op('pool', lambda g, h=h, kvh=kvh: g.tensor_tensor(prod[0:NS, 0:4, :], k2rep[:, :, kvh * 128:(kvh + 1) * 128],
                                                                  qf[0:NS, h * 128:(h + 1) * 128].unsqueeze(1).to_broadcast([NS, 4, 128]), ALU.mult),
                  reads=['k2rep', 'qf', 'prod'], writes=['prod'])
            fw.op('dve', lambda v, h=h: v.tensor_reduce(scn[:, 0, h, :], prod[0:NS, 0:4, :], AX.X, ALU.add), reads=['prod'], writes=['scn_%d' % h])
        scnk = ['scn_%d' % h for h in range(8)]
        fw.op('dve', lambda v: v.scalar_tensor_tensor(scn[:, 0], scn[:, 0], scale, scn[:, 2, 0, :].unsqueeze(1).to_broadcast([NS, 8, 4]), ALU.mult, ALU.add),
              reads=scnk + ['nbias'], writes=['scn'])
        fw.op('act', lambda a: a.activation(scn[:, 1], scn[:, 0], AF.Exp), reads=['scn'], writes=['pn'])
        fw.op('dve', lambda v: v.tensor_reduce(small[0:NS, 40:48], scn[:, 1], AX.X, ALU.add), reads=['pn'], writes=['dnew'])
        fw.op('dve', lambda v: v.tensor_tensor(small[0:NS, 40:48], small[0:NS, 40:48], pN[0:NS, 0:8], ALU.add), reads=['dnew', 'pN'], writes=['dnew'])
        fw.op('dve', lambda v: v.reciprocal(small[0:NS, 48:56], small[0:NS, 40:48]), reads=['dnew'], writes=['rdn'])
        fw.op('act', lambda a: a.copy(accn[:], pA[0][0:NS, :]), reads=['pA0a', 'pA0b'], writes=['accn'])
        for j in range(4):
            vv = v2rep[:, j, :].rearrange("p (k d) -> p k d", k=2).unsqueeze(2).to_broadcast([NS, 2, 4, 128])
            pp = scn[:, 1, :, j].rearrange("p (k g) -> p k g", k=2).unsqueeze(3).to_broadcast([NS, 2, 4, 128])
            p4 = prod[0:NS].rearrange("p (k g) d -> p k g d", k=2)
            fw.op('dve', lambda v: v.tensor_tensor(p4, vv, pp, ALU.mult), reads=['v2rep', 'pn', 'prod'], writes=['prod'])
            fw.op('dve', lambda v: v.tensor_tensor(accn[:], accn[:], prod[0:NS].rearrange("p h d -> p (h d)"), ALU.add), reads=['accn', 'prod'], writes=['accn'])
        a3 = accn[:].rearrange("p (h d) -> p h d", d=128)
        fw.op('dve', lambda v: v.tensor_tensor(a3, a3, small[0:NS, 48:56].unsqueeze(2).to_broadcast([NS, 8, 128]), ALU.mult), reads=['accn', 'rdn'], writes=['accn'])
        fw.op('dve', lambda v: v.tensor_tensor(ogb[:], accn[:], sgs[:], ALU.mult), reads=['accn', 'sgs'], writes=['ogb'])
        for c in range(8):
            fw.op('pe', lambda t_, c=c: t_.transpose(pT[:, c, 0:NS], ogb[:, c * 128:(c + 1) * 128], identb[0:NS, 0:NS]), reads=['ogb', 'identb'], writes=['pT'])
        fw.op('dve', lambda v: v.tensor_copy(hT[:, :, 0:NS], pT[:, :, 0:NS]), reads=['pT'], writes=hTk)
        proj_s(pA[0], pAk[0], [slab(s_wout0, i * 512, 512, k_wout0) for i in range(2)])
        fw.op('dve', lambda v: v.tensor_tensor(accn[:], pA[0][0:NS, :], gate_s[:, 0, :], ALU.mult), reads=['pA0a', 'pA0b', 'gate_s0_0', 'gate_s0_1'], writes=['accn'])
        fw.op('dve', lambda v: v.tensor_tensor(xs_t[:], xs_t[:], accn[:], ALU.add), reads=['accn', 'xs_t'], writes=['xs_t'])

        norm_tile(xs_t[:], 'xs_t', 'xn', n=NS)
        to_hT_s(1)
        fw.dma('sp', tok12[0:12, :], d_sconv, writes=['accn'])
        for c in range(8):
            fw.op('pe', lambda t_, c=c: t_.transpose(pS[:, c * 12:(c + 1) * 12], tok12[0:12, c * 128:(c + 1) * 128], identf[0:12, 0:12]), reads=['accn', 'identf'], writes=['pS'])
        fw.op('dve', lambda v: v.tensor_copy(xbs[:, :, :, 0:3], pS[:, 0:96].rearrange("p (c b j) -> p c b j", c=8, b=4)), reads=['pS'], writes=['xbs_h'])
        fw.dma('sp', tok12[0:4, :], d_sh, reads=['accn'], writes=['accn'])
        for c in range(8):
            fw.op('pe', lambda t_, c=c: t_.transpose(pS[:, c * 4:(c + 1) * 4], tok12[0:4, c * 128:(c + 1) * 128], identf[0:4, 0:4]), reads=['accn', 'identf'], writes=['pS'])
        fw.op('dve', lambda v: v.tensor_copy(hst[:, 0], pS[:, 0:32].rearrange("p (c b) -> p c b", c=8)), reads=['pS'], writes=['hst'])
        slabs = [slab(s_lwin, i * 512, 512, k_lwin) for i in range(2)]
        for m in range(8):
            rt, rk = slabs[m // 4]
            for kc in range(8):
                fw.op('pe', lambda t_, kc=kc, rt=rt, m=m: t_.matmul(pFd[:, m * 16:(m + 1) * 16], rt[:, kc, (m % 4) * 128:(m % 4 + 1) * 128], hT[:, kc, 0:NS],
                                                                  start=(kc == 0), stop=(kc == 7)),
                      reads=[hTk[kc], rk], writes=['pF0'])
        fw.op('dve', lambda v: v.tensor_copy(xbs[:, :, :, 3:7], pFd[:, 0:128].rearrange("p (c b t) -> p c b t", c=8, b=4)), reads=['pF0'], writes=['xbs_x'])
        slabs = [slab(s_lwin, 1024 + i * 512, 512, k_lwin) for i in range(2)]
        for m in range(8):
            rt, rk = slabs[m // 4]
            for kc in range(8):
                fw.op('pe', lambda t_, kc=kc, rt=rt, m=m: t_.matmul(pFd[:, m * 16:(m + 1) * 16], rt[:, kc, (m % 4) * 128:(m % 4 + 1) * 128], hT[:, kc, 0:NS],
                                                                  start=(kc == 0), stop=(kc == 7)),
                      reads=[hTk[kc], rk], writes=['pF0'])
        fw.op('act', lambda a: a.activation(sg1s[:].rearrange("p c t -> p (c t)"), pFd[:, 0:128], AF.Silu), reads=['pF0'], writes=['sg1s'])
        lk = ['lvec%d' % i for i in range(8)]
        x4 = xcs[:, 0].rearrange("p c (b t) -> p c b t", b=4)
        t4 = xcs[:, 1].rearrange("p c (b t) -> p c b t", b=4)

        def wb(j):
            return lvec[:, j, :].unsqueeze(2).unsqueeze(3).to_broadcast([128, 8, 4, 4])
        fw.op('dve', lambda v: v.tensor_tensor(x4, xbs[:, :, :, 0:4], wb(0), ALU.mult), reads=['xbs_h', 'xbs_x'] + lk, writes=['xcs'])
        for j in range(1, 4):
            fw.op('dve', lambda v, j=j: v.tensor_tensor(t4, xbs[:, :, :, j:j + 4], wb(j), ALU.mult), reads=['xbs_h', 'xbs_x'] + lk, writes=['xcs_t'])
            fw.op('dve', lambda v: v.tensor_tensor(x4, x4, t4, ALU.add), reads=['xcs', 'xcs_t'], writes=['xcs'])
        fw.op('dve', lambda v: v.tensor_tensor(x4, x4, wb(4), ALU.add), reads=['xcs'] + lk, writes=['xcs'])
        fw.op('act', lambda a: a.copy(xcbs[:], xcs[:, 0]), reads=['xcs'], writes=['xcbs'])
        fw.op('dve', lambda v: v.tensor_copy(cso[:].rearrange("p c (b j) -> p c b j", b=4), xbs[:, :, :, 4:7]), reads=['xbs_h', 'xbs_x'], writes=['cso'])
        for c in range(8):
            fw.op('pe', lambda t_, c=c: t_.transpose(pA[0][0:12, c * 128:(c + 1) * 128], cso[:, c, :], identf[:]), reads=['cso', 'identf'], writes=[pAk[0][c // 4]])
        fw.op('act', lambda a: a.copy(tok12[0:12, :], pA[0][0:12, :]), reads=['pA0a', 'pA0b', 'accn'], writes=['accn'])
        fw.dma('pool', conv_s, tok12[0:12, :], reads=['accn'], writes=['o_convs'])
        for g_ in range(2):
            for m in range(8):
                nb, mo = m // 2, m % 2
                for ki in range(2):
                    fw.op('pe', lambda t_, g_=g_, ki=ki, mo=mo, nb=nb, m=m: t_.matmul(pFd[:, m * 16:(m + 1) * 16], wab[:, g_, nb * 2 + ki, mo * 128:(mo + 1) * 128],
                                                                                   xcbs[:, nb * 2 + ki, :], start=(ki == 0), stop=(ki == 1)),
                          reads=['wab%d' % g_, 'xcbs'], writes=['pF0'])
            fw.op('dve', lambda v, g_=g_: v.tensor_tensor(rg[:, g_], pFd[:, 0:128].rearrange("p (c t) -> p c t", c=8),
                                                        lvec[:, 5 + g_, :].unsqueeze(2).to_broadcast([128, 8, NS]), ALU.add), reads=['pF0'] + lk, writes=['rg%d' % g_])
            fw.op('act', lambda a, g_=g_: a.activation(rg[:, g_], rg[:, g_], AF.Sigmoid), reads=['rg%d' % g_], writes=['rg%d' % g_])
        fw.op('dve', lambda v: v.tensor_tensor(rg[:, 2], rg[:, 0], lc8[:, 0, :].unsqueeze(2).to_broadcast([128, 8, NS]), ALU.mult), reads=['rg0', 'lc8'], writes=['rg2'])
        fw.op('act', lambda a: a.activation(rg[:, 3], rg[:, 2], AF.Exp), reads=['rg2'], writes=['rg3'])
        fw.op('act', lambda a: a.activation(rg[:, 4], rg[:, 2], AF.Exp, scale=2.0), reads=['rg2'], writes=['rg4'])
        fw.op('act', lambda a: a.activation(rg[:, 4], rg[:, 4], AF.Sqrt, bias=onec[:, 0:1], scale=-1.0), reads=['rg4', 'onec'], writes=['rg4'])
        fw.op('dve', lambda v: v.tensor_tensor(rg[:, 5], rg[:, 1], xcs[:, 0], ALU.mult), reads=['rg1', 'xcs'], writes=['rg5'])
        fw.op('dve', lambda v: v.tensor_tensor(rg[:, 5], rg[:, 5], rg[:, 4], ALU.mult), reads=['rg5', 'rg4'], writes=['rg5'])
        a4 = rg[:, 3].rearrange("p c (b t) -> p c b t", b=4)
        b4 = rg[:, 5].rearrange("p c (b t) -> p c b t", b=4)
        for t in range(4):
            hprev = hst[:, 0] if t == 0 else hss[:, :, :, t - 1]
            fw.op('dve', lambda v, t=t, hprev=hprev: v.tensor_tensor(hst[:, 1], a4[:, :, :, t], hprev, ALU.mult), reads=['rg3', 'hst', 'hss'], writes=['hst1'])
            fw.op('dve', lambda v, t=t: v.tensor_tensor(hss[:, :, :, t], hst[:, 1], b4[:, :, :, t], ALU.add), reads=['hst1', 'rg5'], writes=['hss'])
        fw.op('dve', lambda v: v.tensor_copy(hst[:, 0], hss[:, :, :, 3]), reads=['hss', 'hst1'], writes=['hst'])
        for c in range(8):
            fw.op('pe', lambda t_, c=c: t_.transpose(pA[1][0:4, c * 128:(c + 1) * 128], hst[:, 0, c, :], identf[:]), reads=['hst', 'identf'], writes=[pAk[1][c // 4]])
        fw.op('act', lambda a: a.copy(tok12[0:4, :], pA[1][0:4, :]), reads=['pA1a', 'pA1b', 'accn'], writes=['accn'])
        fw.dma('pool', h_s, tok12[0:4, :], reads=['accn'], writes=['o_hs'])
        fw.op('dve', lambda v: v.tensor_tensor(hT[:, :, 0:NS], hss[:].rearrange("p c b t -> p c (b t)"), sg1s[:], ALU.mult), reads=['hss', 'sg1s'], writes=hTk)
        proj_s(pA[0], pAk[0], [slab(s_lwout, i * 512, 512, k_lwout) for i in range(2)])
        fw.op('dve', lambda v: v.tensor_tensor(accn[:], pA[0][0:NS, :], gate_s[:, 1, :], ALU.mult), reads=['pA0a', 'pA0b', 'gate_s1_0', 'gate_s1_1'], writes=['accn'])
        fw.op('dve', lambda v: v.tensor_tensor(xs_t[:], xs_t[:], accn[:], ALU.add), reads=['accn', 'xs_t'], writes=['xs_t'])
        fw.dma('pool', y_s, xs_t[:], reads=['xs_t'], writes=['o_ys'])
        fw.barrier()
        st.close()

    setup_l1()
    if STAGE >= 4:
        sample_path()

    maskw = sb("maskw", [128, 16, 128], BF16)
    fw.dma('sp', stage[:], cst["maskw"], writes=['stage'])
    fw.op('dve', lambda v: v.tensor_copy(maskw[:].rearrange("p r q -> p (r q)"), stage[:]), reads=['stage'], writes=['maskw'])
    gate_bc = [sb("gate_bc0", [128, D]), sb("gate_bc1", [128, D])]
    KT = sb("KT", [128, 2, SEQ], BF16)
    Vr = sb("Vr", [128, NT, 256], BF16)
    KIT2 = sb("KIT2", [128, SEQ], BF16)
    xt = [sb("xt0", [128, 2, D]), sb("xt1", [128, 2, D])]
    qT = sb("qT", [128, 2, 2, 512], BF16)
    qiT = sb("qiT", [128, 2, 16, 128], BF16)
    fw.op('pool', lambda g: g.memset(qiT[:], 0.0), writes=['qiT0', 'qiTb0', 'qiT1', 'qiTb1'])
    sgT = sb("sgT", [128, 8, 256], BF16)
    ogT = sb("ogT", [128, 8, 256], BF16)
    wiT = sb("wiT", [16, 256], BF16)
    wall = sb("wall", [128, 16, 128], BF16)
    s1sb = sb("s1sb", [128, 128], BF16)
    biasb = [sb("biasm0", [128, SEQ], BF16), sb("biasm1", [128, SEQ], BF16)]
    scb = [stage, sb("scores1", [128, SEQ])]
    Rb = [sb("Rb%d" % i, [128, 512], BF16) for i in range(2)]
    Pb = [sb("Pb%d" % i, [128, 512], BF16) for i in range(2)]
    rden = sb("rden", [128, 512])
    otmp = sb("otmp", [128, 512], BF16)
    xbT = sb("xbT", [128, 8, 259])
    hstate = sb("hstate", [128, 8])
    xc2 = sb("xc2", [128, 2, 256])
    xcb2 = sb("xcb2", [128, 2, 256], BF16)
    gbuf = sb("gbuf", [128, 6, 256])
    for s in range(NSEQ):
        if STAGE >= 1:
            phase1(s)
        if STAGE >= 2:
            phase2(s)
    fw.finish('sp')
    print("ops", fw.nops, "waits", fw.nwaits, "sbuf_left", nc.sbuf_bytes_remaining)
    return nc


_CACHE = {}


def kernel(**inputs):
    x_prompt = np.asarray(inputs["x_prompt"], np.float32)
    B = x_prompt.shape[0]
    ncores = NCORES
    consts = host_consts()
    if "nc" not in _CACHE:
        _CACHE["nc"] = build_program()
    nc = _CACHE["nc"]
    g = lambda k: np.ascontiguousarray(np.asarray(inputs[k]))
    shared = {
        "norm_g": g("norm_g"), "ada_w": g("ada_w"), "ada_b": g("ada_b"),
        "attn_w_in": g("attn_w_in")[0], "q_norm": g("attn_q_norm"), "k_norm": g("attn_k_norm"),
        "attn_w_out": g("attn_w_out")[0], "lru_w_in": g("lru_w_in")[0], "conv_w": g("lru_conv_w")[0],
        "conv_b": g("lru_conv_b"), "w_a": g("lru_w_a")[0].reshape(1024, 256), "b_a": g("lru_b_a"),
        "w_x": g("lru_w_x")[0].reshape(1024, 256), "b_x": g("lru_b_x"), "lam": g("lru_lam"),
        "lru_w_out": g("lru_w_out")[0],
    }
    for k, v in consts.items():
        shared["c_" + k] = v
    c_prompt = g("c_prompt")
    c_sample = g("c_sample")
    x_sample = g("x_sample")
    cik = g("cache_idx_k")[0].reshape(5120 * 8, 1024)
    ck = g("cache_k")[0].reshape(5120 * 128, 256)
    cv = g("cache_v")[0].reshape(5120 * 128, 256)
    page_table = np.asarray(inputs["page_table"]).astype(np.int32)
    state_conv = g("state_conv")[0]
    state_h = g("state_h")[0]
    core_ids = list(range(ncores))
    if os.environ.get("K_ONECORE"):
        core_ids = [0]
    in_maps = []
    for c in core_ids:
        m = dict(shared)
        m["xp"] = np.ascontiguousarray(x_prompt[2 * c:2 * c + 2].reshape(NSEQ * SEQ, D))
        m["c6"] = np.ascontiguousarray(np.concatenate([c_prompt[2 * c:2 * c + 2], c_sample[4 * c:4 * c + 4]], axis=0))
        if STAGE < 4:
            in_maps.append(m)
            continue
        m["xs"] = np.ascontiguousarray(x_sample[4 * c:4 * c + 4].reshape(16, D))
        m["cik"] = cik
        m["ck"] = ck
        m["cv"] = cv
        ptc = page_table[4 * c:4 * c + 4]
        m["ptT"] = np.ascontiguousarray(ptc.T)
        m["pt128"] = np.ascontiguousarray(ptc[(np.arange(128) % 16) // 4])
        m["sconv"] = np.ascontiguousarray(state_conv[4 * c:4 * c + 4].reshape(12, D))
        m["sh"] = np.ascontiguousarray(state_h[4 * c:4 * c + 4])
        in_maps.append(m)
    res = run_bass_kernel_spmd(nc, in_maps, core_ids=core_ids, trace=bool(os.environ.get('K_TRACE')))
    if os.environ.get('K_TRACE'):
        print('EXEC_TIME_NS', res.exec_time_ns)
    R = res.results
    nco = len(core_ids)
    y_prompt = np.concatenate([r["y_p"].reshape(NSEQ, SEQ, D) for r in R], axis=0)
    k_prompt = np.concatenate([r["k_p"].reshape(NSEQ, SEQ, 2, 128) for r in R], axis=0)[None]
    v_prompt = np.concatenate([r["v_p"].reshape(NSEQ, SEQ, 2, 128) for r in R], axis=0)[None]
    ik_prompt = np.concatenate([r["ik_p"].reshape(NSEQ, SEQ, 64) for r in R], axis=0)[None]
    conv_prompt = np.concatenate([r["conv_p"].reshape(NSEQ, 3, D) for r in R], axis=0)[None]
    h_prompt = np.concatenate([r["h_p"].reshape(NSEQ, D) for r in R], axis=0)[None]
    if STAGE < 4:
        Bd = 4 * nco
        z = lambda *s_: np.zeros(s_, np.float32)
        return (y_prompt, z(Bd, 4, D), k_prompt, v_prompt, ik_prompt, z(1, Bd, 4, 2, 128), z(1, Bd, 4, 2, 128), z(1, Bd, 4, 64),
                conv_prompt, h_prompt, z(1, Bd, 3, D), z(1, Bd, D))
    y_sample = np.concatenate([r["y_s"].reshape(4, 4, D) for r in R], axis=0)
    k_sample = np.concatenate([r["k_s"].reshape(4, 4, 2, 128) for r in R], axis=0)[None]
    v_sample = np.concatenate([r["v_s"].reshape(4, 4, 2, 128) for r in R], axis=0)[None]
    ik_sample = np.concatenate([r["ik_s"].reshape(4, 4, 64) for r in R], axis=0)[None]
    conv_sample = np.concatenate([r["conv_s"].reshape(4, 3, D) for r in R], axis=0)[None]
    h_sample = np.concatenate([r["h_s"].reshape(4, D) for r in R], axis=0)[None]
    return (y_prompt, y_sample, k_prompt, v_prompt, ik_prompt, k_sample, v_sample, ik_sample,
            conv_prompt, h_prompt, conv_sample, h_sample)
```

```python
import math
import os
import numpy as np
import concourse.bass as bass
import concourse.mybir as mybir
from concourse.bass_utils import run_bass_kernel_spmd

F32 = mybir.dt.float32
BF16 = mybir.dt.bfloat16
I32 = mybir.dt.int32
ALU = mybir.AluOpType
AF = mybir.ActivationFunctionType
AX = mybir.AxisListType

NCORES = 8
D = 1024
SEQ = 2048
NSEQ = 2
NT = SEQ // 128
INA = 3664
OQ, OK_, OV, OQI, OKI, OWI, OG = 0, 1024, 1280, 1536, 2560, 2624, 2640
EPS = 1e-6
NIT = 10
TOPK = 256
NEG = -30000.0


class FW:
    def __init__(self, nc):
        self.nc = nc
        self.eng = {'pe': nc.tensor, 'act': nc.scalar, 'dve': nc.vector, 'pool': nc.gpsimd, 'sp': nc.sync}
        self.csem = {e: nc.alloc_semaphore("cs_" + e) for e in ('pe', 'act', 'dve', 'pool')}
        self.ccnt = {e: 0 for e in self.csem}
        self.known = {e: {} for e in self.eng}
        self.lastw = {}
        self.readers = {}
        self.dpool = {e: [[nc.alloc_semaphore("ds_%s%d" % (e, i)), 0] for i in range(n)] for e, n in (('sp', 14), ('pool', 14), ('act', 4))}
        self.dnext = {'sp': 0, 'pool': 0, 'act': 0}
        self.nwaits = 0
        self.nops = {e: 0 for e in self.eng}
        self.pend = {e: [] for e in self.csem}
        self.bank_of = {}
        self.bank_last = {}
        self.last_ins = {}
        self.last_sig = {e: True for e in self.csem}

    def _flush(self, e):
        if self.pend[e]:
            if not self.last_sig[e]:
                self.ccnt[e] += 1
                self.last_ins[e].then_inc(self.csem[e], 1)
                self.last_sig[e] = True
            for t in self.pend[e]:
                t[1] = self.ccnt[e]
            self.pend[e] = []

    def _need(self, e, deps):
        best = {}
        for t in deps:
            if t[1] is None:
                self._flush(t[2])
            s, v = t[0], t[1]
            k = s.name
            if k not in best or best[k][1] < v:
                best[k] = (s, v)
        for k, (s, v) in best.items():
            if self.known[e].get(k, 0) >= v:
                continue
            self.eng[e].wait_ge(s, v)
            self.nwaits += 1
            self.known[e][k] = v

    def _deps(self, reads, writes, e):
        deps = []
        for k in reads:
            if k in self.lastw:
                deps.append(self.lastw[k])
        for k in writes:
            if k in self.lastw:
                deps.append(self.lastw[k])
            deps.extend(self.readers.get(k, []))
        for k in set(reads) | set(writes):
            b = self.bank_of.get(k)
            if b is not None:
                for e2, tok in self.bank_last.get(b, {}).items():
                    if e2 != e:
                        deps.append(tok)
        if e == 'pe':
            deps = [d for d in deps if d[2] != 'pe']
        return deps

    def _commit(self, reads, writes, tok):
        for k in set(reads) | set(writes):
            b = self.bank_of.get(k)
            if b is not None:
                self.bank_last.setdefault(b, {})[tok[2]] = tok
        for k in reads:
            self.readers.setdefault(k, []).append(tok)
        for k in writes:
            self.lastw[k] = tok
            self.readers[k] = []

    def op(self, e, fn, reads=(), writes=(), sig=None):
        if e == 'pool' and os.environ.get('K_NOPOOL'):
            e = 'dve'
        if sig is None:
            sig = (e != 'pe')
        self._need(e, self._deps(reads, writes, e))
        ins = fn(self.eng[e])
        self.nops[e] += 1
        self.last_ins[e] = ins
        if sig:
            self.ccnt[e] += 1
            ins.then_inc(self.csem[e], 1)
            self.last_sig[e] = True
            for t in self.pend[e]:
                t[1] = self.ccnt[e]
            self.pend[e] = []
            tok = [self.csem[e], self.ccnt[e], e]
        else:
            self.last_sig[e] = False
            tok = [self.csem[e], None, e]
            self.pend[e].append(tok)
        self._commit(reads, writes, tok)
        return ins

    def dma(self, e, out, in_, reads=(), writes=(), indirect=None, **kw):
        ent = self.dpool[e][self.dnext[e]]
        self.dnext[e] = (self.dnext[e] + 1) % len(self.dpool[e])
        deps = self._deps(reads, writes, e)
        if ent[1] > 0:
            deps.append([ent[0], ent[1], 'dma'])
        self._need(e, deps)
        if indirect is None:
            ins = self.eng[e].dma_start(out=out, in_=in_, **kw)
        else:
            ins = self.eng[e].indirect_dma_start(out=out, in_=in_, **indirect)
        self.nops[e] += 1
        ent[1] += 16
        ins.then_inc(ent[0], 16)
        self._commit(reads, writes, [ent[0], ent[1], 'dma'])
        return ins

    def barrier(self):
        for ee in self.csem:
            self._flush(ee)
        deps = list(self.lastw.values())
        for rl in self.readers.values():
            deps.extend(rl)
        for b in self.bank_last.values():
            deps.extend(b.values())
        for e in self.eng:
            self._need(e, deps)
        self.lastw = {}
        self.readers = {}
        self.bank_last = {}

    def finish(self, e='sp'):
        for ee in self.csem:
            self._flush(ee)
        deps = list(self.lastw.values())
        for rl in self.readers.values():
            deps.extend(rl)
        self._need(e, deps)


def host_consts():
    c = {}
    theta = 500000.0
    inv = np.exp(-math.log(theta) * np.arange(16, dtype=np.float32) * 2.0 / 32).astype(np.float32)
    pos = np.arange(SEQ, dtype=np.float32)
    ang = (pos[:, None] * inv[None, :]).astype(np.float32)
    cq = np.cos(ang).astype(np.float32).reshape(NT, 128, 16).transpose(1, 0, 2)
    sq = np.sin(ang).astype(np.float32).reshape(NT, 128, 16).transpose(1, 0, 2)
    inv8 = np.exp(-math.log(theta) * np.arange(8, dtype=np.float32) * 2.0 / 16).astype(np.float32)
    ang8 = (pos[:, None] * inv8[None, :]).astype(np.float32)
    ci = np.cos(ang8).astype(np.float32).reshape(NT, 128, 8).transpose(1, 0, 2)
    si = np.sin(ang8).astype(np.float32).reshape(NT, 128, 8).transpose(1, 0, 2)
    c["rope_q"] = np.ascontiguousarray(np.concatenate([cq, sq], axis=2)).reshape(128, NT * 32)
    c["rope_i"] = np.ascontiguousarray(np.concatenate([ci, si], axis=2)).reshape(128, NT * 16)
    ident = np.eye(128, dtype=np.float32)
    c["ident"] = ident
    p = np.arange(128)
    q8 = p % 8
    chunk = (p % 64) // 8
    par = p // 64
    mw = np.zeros((128, 16, 128), np.float32)
    for r in range(16):
        mw[p, r, 8 * r + q8] = 1.0
    c["maskw"] = mw.reshape(128, 2048)
    asel = np.zeros((16, 128), np.float32)
    asel[2 * chunk + par, p] = 1.0
    c["asel"] = asel
    cb = np.where(np.arange(128)[None, :] > np.arange(128)[:, None], -1e30, 0.0).astype(np.float32)
    c["cb"] = cb
    pw = np.zeros((128, 2, NIT + 1), np.float32)
    pw[:, 0, :] = (0.5 ** (np.arange(NIT + 1) + 2))[None, :]
    pw[:, 1, :] = (0.5 ** (np.arange(NIT + 1) + 1))[None, :]
    c["pw"] = pw.reshape(128, 2 * (NIT + 1))
    sel = np.zeros((6, 3, 128), np.float32)
    sel[0, 0, :] = 1.0
    sel[1, 1, :] = 1.0
    for t in range(16):
        sel[2 + t // 4, 2, t] = 1.0
    c["sel"] = sel.reshape(6, 384)
    NITS = 20
    ps_ = (16384 + (np.arange(16) % 4)).astype(np.float32)
    a16 = (ps_[:, None] * inv[None, :]).astype(np.float32)
    c["rope_qs"] = np.concatenate([np.cos(a16), np.sin(a16)], axis=1).astype(np.float32)
    a8 = (ps_[:, None] * inv8[None, :]).astype(np.float32)
    c["rope_is"] = np.concatenate([np.cos(a8), np.sin(a8)], axis=1).astype(np.float32)
    p64 = np.arange(64)
    par_s, c_s, t_s = p64 // 32, (p64 % 32) // 4, p64 % 4
    asel_s = np.zeros((16, 64), np.float32)
    asel_s[2 * c_s + par_s, p64] = 1.0
    c["asel_s"] = asel_s
    mask_s = np.zeros((64, 16), np.float32)
    for bl in range(4):
        mask_s[p64, bl * 4 + t_s] = 1.0
    c["mask_s"] = mask_s
    p128 = np.arange(128)
    c["g128"] = (p128[:, None] % 16 == p128[None, :] % 16).astype(np.float32)
    rep = (np.arange(16)[:, None] == p128[None, :] % 16).astype(np.float32)
    c["rep"] = rep
    c["repT"] = np.ascontiguousarray(rep.T)
    mE = np.full((128, 4), -1e30, np.float32)
    for p_ in range(16):
        for j in range(4):
            if j <= p_ % 4:
                mE[p_, j] = 0.0
    c["maskE"] = mE
    c["cbase"] = ((p128 // 16) * 16).astype(np.float32).reshape(128, 1)
    selj = np.zeros((16, 4, 16), np.float32)
    for bl in range(4):
        for j in range(4):
            for t in range(4):
                selj[bl * 4 + j, j, bl * 4 + t] = 1.0
    c["selj"] = selj.reshape(16, 64)
    c["iota"] = np.tile(np.arange(128, dtype=np.float32)[None, :], (128, 1))
    pws = np.zeros((128, 2, NITS + 1), np.float32)
    pws[:, 0, :] = (0.5 ** (np.arange(NITS + 1) + 2))[None, :]
    pws[:, 1, :] = (0.5 ** (np.arange(NITS + 1) + 1))[None, :]
    c["pws"] = pws.reshape(128, 2 * (NITS + 1))
    return c


CONST_SHAPES = {"rope_q": [128, NT * 32], "rope_i": [128, NT * 16], "ident": [128, 128], "maskw": [128, 2048],
                "asel": [16, 128], "rope_qs": [16, 32], "rope_is": [16, 16], "asel_s": [16, 64], "mask_s": [64, 16], "g128": [128, 128], "rep": [16, 128], "repT": [128, 16], "maskE": [128, 4], "cbase": [128, 1], "selj": [16, 64], "iota": [128, 128], "pws": [128, 42], "cb": [128, 128], "pw": [128, 2 * (NIT + 1)], "sel": [6, 384]}

STAGE = int(os.environ.get("K_STAGE", "9"))
NSB_LIMIT = int(os.environ.get("K_NSB", "8"))
NT1 = int(os.environ.get("K_NT1", str(NT)))
CUT = int(os.environ.get("K_CUT", "99"))


def build_program():
    nc = bass.Bass("TRN2", target_bir_lowering=False)
    fw = FW(nc)

    def din(name, shape, dt=F32):
        return nc.dram_tensor(name, shape, dt, kind="ExternalInput").ap()

    def dout(name, shape, dt=F32):
        return nc.dram_tensor(name, shape, dt, kind="ExternalOutput").ap()

    xp = din("xp", [NSEQ * SEQ, D])
    c6 = din("c6", [6, D])
    norm_g = din("norm_g", [2, D])
    ada_w = din("ada_w", [2, D, 3 * D])
    ada_b = din("ada_b", [2, 3 * D])
    attn_w_in = din("attn_w_in", [D, INA])
    q_norm = din("q_norm", [1, 128])
    k_norm = din("k_norm", [1, 128])
    attn_w_out = din("attn_w_out", [D, D])
    lru_w_in = din("lru_w_in", [D, 2 * D])
    conv_w = din("conv_w", [4, D])
    conv_b = din("conv_b", [1, D])
    w_a = din("w_a", [D, 256])
    b_a = din("b_a", [1, D])
    w_x = din("w_x", [D, 256])
    b_x = din("b_x", [1, D])
    lam = din("lam", [1, D])
    lru_w_out = din("lru_w_out", [D, D])
    cst = {k: din("c_" + k, v) for k, v in CONST_SHAPES.items()}
    sdin = din if STAGE >= 4 else (lambda *a, **k: None)
    sdout = dout if STAGE >= 4 else (lambda *a, **k: None)
    d_xs = sdin("xs", [16, D])
    d_cik = sdin("cik", [5120 * 8, 1024])
    d_ck = sdin("ck", [5120 * 128, 256])
    d_cv = sdin("cv", [5120 * 128, 256])
    d_ptT = sdin("ptT", [128, 4], I32)
    d_pt128 = sdin("pt128", [128, 128], I32)
    d_sconv = sdin("sconv", [12, D])
    d_sh = sdin("sh", [4, D])
    y_s = sdout("y_s", [16, D])
    k_s = sdout("k_s", [16, 256])
    v_s = sdout("v_s", [16, 256])
    ik_s = sdout("ik_s", [16, 64])
    conv_s = sdout("conv_s", [12, D])
    h_s = sdout("h_s", [4, D])

    y_p = dout("y_p", [NSEQ * SEQ, D])
    k_p = dout("k_p", [NSEQ * SEQ, 256])
    v_p = dout("v_p", [NSEQ * SEQ, 256])
    ik_p = dout("ik_p", [NSEQ * SEQ, 64])
    conv_p = dout("conv_p", [NSEQ * 3, D])
    h_p = dout("h_p", [NSEQ, D])

    s_win = nc.dram_tensor("s_win", [D, INA], BF16, kind="Internal").ap()
    s_wout0 = nc.dram_tensor("s_wout0", [D, D], BF16, kind="Internal").ap()
    s_lwin = nc.dram_tensor("s_lwin", [D, 2 * D], BF16, kind="Internal").ap()
    s_lwout = nc.dram_tensor("s_lwout", [D, D], BF16, kind="Internal").ap()

    def sb(name, shape, dt=F32):
        return nc.alloc_sbuf_tensor(name, shape, dt)

    def ps(name, shape, dt=F32):
        return nc.alloc_psum_tensor(name, shape, dt)

    identf = sb("identf", [128, 128])
    identb = sb("identb", [128, 128], BF16)
    ident4 = sb("ident4", [128, 4, 128], BF16)
    onesb = sb("onesb", [128, 128], BF16)
    aselb = sb("aselb", [16, 128], BF16)
    cbias = sb("cbias", [128, 128])
    pw = sb("pw", [128, 2, NIT + 1])
    rope_q = sb("rope_q", [128, NT, 32])
    rope_i = sb("rope_i", [128, NT, 16])
    self_ = sb("sel", [6, 3, 128])
    qn_bc = sb("qn_bc", [128, 128])
    kn_bc = sb("kn_bc", [128, 128])
    epsc = sb("epsc", [128, 1])
    onec = sb("onec", [128, 1])
    stage = sb("stage", [128, 2048])
    scores = stage

    fw.op('pool', lambda g: g.memset(epsc[:], EPS), writes=['epsc'])
    fw.op('pool', lambda g: g.memset(onec[:], 1.0), writes=['onec'])
    c255 = sb("c255", [128, 1])
    cneg = sb("cneg", [128, 1])
    fw.op('pool', lambda g: g.memset(c255[:], TOPK - 0.5), writes=['c255'])
    fw.op('pool', lambda g: g.memset(cneg[:], NEG), writes=['cneg'])
    fw.op('pool', lambda g: g.memset(onesb[:], 1.0), writes=['onesb'])
    fw.dma('sp', identf[:], cst["ident"], writes=['identf'])
    fw.op('dve', lambda v: v.tensor_copy(identb[:], identf[:]), reads=['identf'], writes=['identb'])
    for i in range(4):
        fw.op('dve', lambda v, i=i: v.tensor_copy(ident4[:, i, :], identf[:]), reads=['identf'], writes=['ident4_%d' % i])
    fw.dma('sp', stage[0:16, 0:128], cst["asel"], writes=['stage'])
    fw.op('dve', lambda v: v.tensor_copy(aselb[:], stage[0:16, 0:128]), reads=['stage'], writes=['aselb'])
    fw.dma('sp', cbias[:], cst["cb"], writes=['cbias'])
    fw.dma('sp', pw[:].rearrange("p a b -> p (a b)"), cst["pw"], writes=['pw'])
    fw.dma('sp', rope_q[:].rearrange("p t c -> p (t c)"), cst["rope_q"], writes=['rope_q'])
    fw.dma('sp', rope_i[:].rearrange("p t c -> p (t c)"), cst["rope_i"], writes=['rope_i'])
    fw.dma('sp', self_[:].rearrange("p a b -> p (a b)"), cst["sel"], writes=['sel'])
    fw.dma('sp', qn_bc[:], q_norm[0, :].partition_broadcast(128), writes=['qn_bc'])
    fw.dma('sp', kn_bc[:], k_norm[0, :].partition_broadcast(128), writes=['kn_bc'])

    def cast_w(dst, src, ncols, key):
        step = 512
        for r0 in range(0, D, 256):
            fw.dma('pool', dst[r0:r0 + 256, :], src[r0:r0 + 256, :], writes=[key + "_%d" % r0])
        return [key + "_%d" % r0 for r0 in range(0, D, 256)]

    k_win = cast_w(s_win, attn_w_in, INA, 'swin')
    k_wout0 = cast_w(s_wout0, attn_w_out, D, 'swout0')
    k_lwin = cast_w(s_lwin, lru_w_in, 2 * D, 'slwin')
    k_lwout = cast_w(s_lwout, lru_w_out, D, 'slwout')

    pA = [ps("pA0", [128, 1024]), ps("pA1", [128, 1024])]
    pT = ps("pT", [128, 8, 128], BF16)
    pF = ps("pF", [128, 2, 256])
    pS = ps("pS", [128, 512])
    pN = ps("pN", [128, 512])
    fw.bank_of.update({'pA0a': 0, 'pA0b': 1, 'pA1a': 2, 'pA1b': 3, 'pT': 4, 'pF0': 5, 'pF1': 5, 'pS': 6, 'pN': 7})
    pD = [pA[1][:, 0:512], pA[1][:, 512:1024]]
    pFv = [pA[1][:, 0:256], pA[1][:, 512:768]]
    pFk = ['pA1a', 'pA1b']
    pDk = ['pA1a', 'pA1b']
    pAk = [['pA0a', 'pA0b'], ['pA1a', 'pA1b']]

    NRING = 4
    ring = [sb("ring%d" % i, [128, 8, 512], BF16) for i in range(NRING)]
    rstate = {'n': 0}

    def slab(src2d, col0, ncols, srckeys, eng='sp', cast=False):
        i = rstate['n'] % NRING
        rstate['n'] += 1
        key = 'ring%d' % i
        fw.dma(eng, ring[i][:, :, 0:ncols], src2d[:, col0:col0 + ncols].rearrange("(kc p) n -> p kc n", p=128),
               reads=srckeys, writes=[key])
        return ring[i], key

    c6t = sb("c6t", [6, D])
    sc6 = sb("sc6", [6, D], BF16)
    scT = sb("scT", [128, 8, 6], BF16)
    adab = sb("adab", [6, 512])
    mst = sb("mst", [6, 512])
    mgate = [sb("mgate0", [6, D]), sb("mgate1", [6, D])]
    Sfm = [sb("Sfm0", [128, 8, 6]), sb("Sfm1", [128, 8, 6])]
    Bfm = [sb("Bfm0", [128, 8, 6]), sb("Bfm1", [128, 8, 6])]
    gfm = sb("gfm", [128, 2, 8])
    pTf = pN

    fw.dma('sp', c6t[:], c6, writes=['c6t'])
    fw.op('act', lambda a: a.activation(sc6[:], c6t[:], AF.Silu), reads=['c6t'], writes=['sc6'])
    for c in range(8):
        fw.op('pe', lambda t, c=c: t.transpose(pT[0:128, c, 0:6], sc6[0:6, c * 128:(c + 1) * 128], identb[0:6, 0:6]),
              reads=['sc6', 'identb'], writes=['pT'])
    fw.op('dve', lambda v: v.tensor_copy(scT[:], pT[:, :, 0:6]), reads=['pT'], writes=['scT'])
    for l in range(2):
        fw.dma('sp', gfm[:, l, :], norm_g[l, :].rearrange("(c p) -> p c", p=128), writes=['gfm%d' % l],
               allow_slow_non_contiguous=True)
    for l in range(2):
        for j in range(6):
            fw.dma('sp', adab[:], ada_b[l, j * 512:(j + 1) * 512].partition_broadcast(6), reads=[], writes=['adab'])
            rt, rk = slab(ada_w[l], j * 512, 512, [], eng='pool')
            for kc in range(8):
                fw.op('pe', lambda t, kc=kc, rt=rt: t.matmul(pN[0:6, :], scT[:, kc, :], rt[:, kc, :], start=(kc == 0), stop=(kc == 7)),
                      reads=['scT', rk], writes=['pN'])
            if j < 4:
                fw.op('dve', lambda v, j=j: v.tensor_tensor(mst[:], pN[0:6, :], adab[:], ALU.add),
                      reads=['pN', 'adab'], writes=['mst'])
                dst = Bfm[l] if j < 2 else Sfm[l]
                for cc in range(4):
                    c = (j % 2) * 4 + cc
                    fw.op('pe', lambda t, cc=cc: t.transpose(pS[:, cc * 8:cc * 8 + 6], mst[0:6, cc * 128:(cc + 1) * 128], identf[0:6, 0:6]),
                          reads=['mst', 'identf'], writes=['pS'])
                fw.op('dve', lambda v, j=j, dst=dst: v.tensor_copy(dst[:, (j % 2) * 4:(j % 2) * 4 + 4, :],
                                                                 pS[:, 0:32].rearrange("p (c r) -> p c r", r=8)[:, :, 0:6]),
                      reads=['pS'], writes=['mod%d_%d' % (l, j)])
            else:
                fw.op('dve', lambda v, j=j, l=l: v.tensor_tensor(mgate[l][:, (j - 4) * 512:(j - 3) * 512], pN[0:6, :],
                                                                adab[:], ALU.add),
                      reads=['pN', 'adab'], writes=['mgate%d_%d' % (l, j)])
        fw.op('dve', lambda v, l=l: v.tensor_scalar_add(Sfm[l][:], Sfm[l][:], 1.0),
              reads=['mod%d_2' % l, 'mod%d_3' % l], writes=['Afm%d' % l])
        fw.op('dve', lambda v, l=l: v.tensor_tensor(Sfm[l][:], Sfm[l][:], gfm[:, l, :].unsqueeze(2).to_broadcast([128, 8, 6]), ALU.mult),
              reads=['Afm%d' % l, 'gfm%d' % l], writes=['Afm%d' % l])
    modkeys = [['Afm%d' % l, 'mod%d_0' % l, 'mod%d_1' % l] for l in range(2)]


    def make_gate(l, row):
        for h in range(2):
            fw.op('pe', lambda t, h=h: t.matmul(pA[0][:, h * 512:(h + 1) * 512], self_[:, row, :], mgate[l][:, h * 512:(h + 1) * 512],
                                               start=True, stop=True),
                  reads=['sel', 'mgate%d_%d' % (l, 4 + h)], writes=[pAk[0][h]])
            fw.op('act', lambda a, h=h: a.copy(gate_bc[l][:, h * 512:(h + 1) * 512], pA[0][:, h * 512:(h + 1) * 512]),
                  reads=[pAk[0][h]], writes=['gate_bc%d_%d' % (l, h)])

    wki = sb("wki", [128, 8, 64], BF16)
    wwi = sb("wwi", [128, 8, 16], BF16)
    fw.dma('sp', wki[:], s_win[:, OKI:OKI + 64].rearrange("(kc p) n -> p kc n", p=128), reads=k_win, writes=['wki'])
    fw.dma('sp', wwi[:], s_win[:, OWI:OWI + 16].rearrange("(kc p) n -> p kc n", p=128), reads=k_win, writes=['wwi'])

    junk = sb("junk", [128, 1024], BF16)
    xn = sb("xn", [128, D], BF16)
    hT = sb("hT", [128, 8, 256], BF16)
    small = sb("small", [128, 64])
    smk = {'n': 0}
    qf = sb("qf", [128, D])
    qb = sb("qb", [128, D], BF16)
    rt_ = sb("rt_", [128, 4, 128])
    kb = sb("kb", [128, 320], BF16)

    def norm_tile(xap, xkey, okey, n=128):
        fw.op('act', lambda a: a.activation(junk[0:n, 0:D], xap, AF.Square, scale=1.0 / 32.0, accum_out=small[0:n, 0:1]),
              reads=[xkey], writes=['junk', 'sm0'])
        fw.op('act', lambda a: a.activation(small[0:n, 1:2], small[0:n, 0:1], AF.Sqrt, bias=epsc[0:n, 0:1], scale=1.0),
              reads=['sm0', 'epsc'], writes=['sm1'])
        fw.op('dve', lambda v: v.reciprocal(small[0:n, 2:3], small[0:n, 1:2]), reads=['sm1'], writes=['sm2'])
        fw.op('dve', lambda v: v.tensor_scalar(xn[0:n, :], xap, small[0:n, 2:3], None, ALU.mult), reads=[xkey, 'sm2'], writes=[okey])

    def to_hT(l, row, col0, ncols=128, src=None, srckey='xn'):
        src = xn if src is None else src
        for c in range(8):
            fw.op('pe', lambda t, c=c: t.transpose(pT[:, c, 0:ncols], src[0:ncols, c * 128:(c + 1) * 128], identb[0:ncols, 0:ncols]),
                  reads=[srckey, 'identb'], writes=['pT'])
        for c in range(8):
            e = 'act' if c % 2 == 0 else 'dve'
            e = os.environ.get('K_EV', e)
            if e == 'A':
                e = 'act' if c < 4 else 'dve'
            if e == 'B':
                e = 'dve' if c < 4 else 'act'
            if e == 'act':
                fw.op('act', lambda a, c=c: a.activation(hT[:, c, col0:col0 + ncols], pT[:, c, 0:ncols], AF.Identity,
                                                         bias=Bfm[l][:, c, row:row + 1], scale=Sfm[l][:, c, row:row + 1]),
                      reads=['pT'] + modkeys[l], writes=['hT%d' % c] + (['ser'] if os.environ.get('K_SER') else []))
            else:
                fw.op('dve', lambda v, c=c: v.tensor_scalar(hT[:, c, col0:col0 + ncols], pT[:, c, 0:ncols],
                                                          Sfm[l][:, c, row:row + 1], Bfm[l][:, c, row:row + 1], ALU.mult, ALU.add),
                      reads=['pT'] + modkeys[l], writes=['hT%d' % c] + (['ser'] if os.environ.get('K_SER') else []))
    hTk = ['hT%d' % c for c in range(8)]

    def rope_inplace(e, x1, x2, cs, sn, tmp, tkey, xkey, shape):
        cb_ = cs.unsqueeze(1).to_broadcast(shape)
        sb_ = sn.unsqueeze(1).to_broadcast(shape)
        t = [tmp[:, i, 0:shape[1] * shape[2]].rearrange("p (h d) -> p h d", d=shape[2]) for i in range(4)]
        fw.op(e, lambda g: g.tensor_tensor(t[0], x1, cb_, ALU.mult), reads=xkey, writes=[tkey + '0'])
        fw.op(e, lambda g: g.tensor_tensor(t[1], x2, sb_, ALU.mult), reads=xkey, writes=[tkey + '1'])
        fw.op(e, lambda g: g.tensor_tensor(t[2], x2, cb_, ALU.mult), reads=xkey, writes=[tkey + '2'])
        fw.op(e, lambda g: g.tensor_tensor(t[3], x1, sb_, ALU.mult), reads=xkey, writes=[tkey + '3'])
        return t

    def phase1(s):
        rt, rk = slab(s_win, OK_, 512, k_win)
        for i in range(NT1):
            buf = i % 2
            row0 = s * SEQ + i * 128
            xkey = 'xt%d_0' % buf
            fw.dma('sp', xt[buf][:, 0, :], xp[row0:row0 + 128, :], writes=[xkey])
            norm_tile(xt[buf][:, 0, :], xkey, 'xn')
            if CUT <= 1:
                continue
            to_hT(0, s, 0)
            if CUT <= 2:
                continue
            o = pA[0]
            for kc in range(8):
                fw.op('pe', lambda t, kc=kc: t.matmul(o[:, 0:512], hT[:, kc, 0:128], rt[:, kc, :], start=(kc == 0), stop=(kc == 7)),
                      reads=[hTk[kc], rk], writes=['pA0a'])
            for kc in range(8):
                fw.op('pe', lambda t, kc=kc: t.matmul(o[:, 512:576], hT[:, kc, 0:128], wki[:, kc, :], start=(kc == 0), stop=(kc == 7)),
                      reads=[hTk[kc], 'wki'], writes=['pA0b'])
            if CUT <= 3:
                continue
            kfi = qf[:, 0:576]
            kfk = 'qf'
            for h in range(2):
                fw.op('act', lambda a, h=h: a.activation(junk[:, 0:128], o[:, h * 128:(h + 1) * 128], AF.Square,
                                                         scale=1.0 / math.sqrt(128.0), accum_out=small[:, 4 + h:5 + h]),
                      reads=['pA0a'], writes=['junk', 'sm4_%d' % h])
            fw.op('act', lambda a: a.activation(small[:, 6:8], small[:, 4:6], AF.Sqrt, bias=epsc[:, 0:1], scale=1.0),
                  reads=['sm4_0', 'sm4_1', 'epsc'], writes=['sm6'])
            fw.op('dve', lambda v: v.reciprocal(small[:, 8:10], small[:, 6:8]), reads=['sm6'], writes=['sm8'])
            kv3 = kfi[:, 0:256].rearrange("p (h d) -> p h d", d=128)
            fw.op('dve', lambda v: v.tensor_tensor(kv3, o[:, 0:256].rearrange("p (h d) -> p h d", d=128),
                                                   small[:, 8:10].unsqueeze(2).to_broadcast([128, 2, 128]), ALU.mult),
                  reads=['pA0a', 'sm8'], writes=[kfk])
            fw.op('pool', lambda g: g.tensor_tensor(kv3, kv3, kn_bc[:].unsqueeze(1).to_broadcast([128, 2, 128]), ALU.mult),
                  reads=[kfk, 'kn_bc'], writes=[kfk])
            if CUT <= 4:
                continue
            x1 = kv3[:, :, 0:16]
            x2 = kv3[:, :, 16:32]
            t = rope_inplace('pool', x1, x2, rope_q[:, i, 0:16], rope_q[:, i, 16:32], rt_, 'rt', [kfk, 'rope_q'], [128, 2, 16])
            fw.op('pool', lambda g: g.tensor_tensor(x1, t[0], t[1], ALU.subtract), reads=['rt0', 'rt1', 'rt2', 'rt3'], writes=[kfk])
            fw.op('pool', lambda g: g.tensor_tensor(x2, t[2], t[3], ALU.add), reads=['rt2', 'rt3'], writes=[kfk])
            if CUT <= 5:
                continue
            fw.op('act', lambda a: a.copy(kfi[:, 256:512], o[:, 256:512]), reads=['pA0a'], writes=[kfk])
            fw.op('act', lambda a: a.copy(kfi[:, 512:576], o[:, 512:576]), reads=['pA0b'], writes=[kfk])
            y1 = kfi[:, 512:520].unsqueeze(1)
            y2 = kfi[:, 520:528].unsqueeze(1)
            t = rope_inplace('pool', y1, y2, rope_i[:, i, 0:8], rope_i[:, i, 8:16], rt_, 'rt', [kfk, 'rope_i'], [128, 1, 8])
            fw.op('pool', lambda g: g.tensor_tensor(y1, t[0], t[1], ALU.subtract), reads=['rt0', 'rt1', 'rt2', 'rt3'], writes=[kfk])
            fw.op('pool', lambda g: g.tensor_tensor(y2, t[2], t[3], ALU.add), reads=['rt2', 'rt3'], writes=[kfk])
            if CUT <= 6:
                continue
            fw.dma('pool', k_p[row0:row0 + 128, :], kfi[:, 0:256], reads=[kfk], writes=['o_k'])
            fw.dma('pool', v_p[row0:row0 + 128, :], kfi[:, 256:512], reads=[kfk], writes=['o_v'])
            fw.dma('pool', ik_p[row0:row0 + 128, :], kfi[:, 512:576], reads=[kfk], writes=['o_ik'])
            if CUT <= 7:
                continue
            fw.op('act', lambda a: a.copy(Vr[:, i, :], kfi[:, 256:512]), reads=[kfk], writes=['Vr%d' % i])
            fw.op('dve', lambda v: v.tensor_copy(kb[:, 0:256], kfi[:, 0:256]), reads=[kfk], writes=['kb'])
            fw.op('dve', lambda v: v.tensor_copy(kb[:, 256:320], kfi[:, 512:576]), reads=[kfk], writes=['kbi'])
            for h in range(2):
                fw.op('pe', lambda t, h=h: t.transpose(pT[:, h, :], kb[:, h * 128:(h + 1) * 128], identb[:]),
                      reads=['kb', 'identb'], writes=['pT'])
            fw.op('pe', lambda t: t.transpose(pT[0:64, 2, :], kb[:, 256:320], identb[:]), reads=['kbi', 'identb'], writes=['pT'])
            fw.op('act', lambda a: a.copy(KT[:, :, i * 128:(i + 1) * 128], pT[:, 0:2, :]), reads=['pT'], writes=['KT%d' % i])
            fw.op('dve', lambda v: v.tensor_copy(KIT2[0:64, i * 128:(i + 1) * 128], pT[0:64, 2, :]), reads=['pT'], writes=['KIa%d' % i])
            fw.op('dve', lambda v: v.tensor_copy(KIT2[64:128, i * 128:(i + 1) * 128], pT[0:64, 2, :]), reads=['pT'], writes=['KIb%d' % i])

    thr = sb("thr", [128, 8])
    wk = sb("wk", [128, 2, NIT + 1])
    rr = {'r': 0, 'p': 0, 'd': 0, 'f': 0}

    def q_post(tile, t):
        o = pA[0]
        for h in range(8):
            fw.op('act', lambda a, h=h: a.activation(junk[:, 0:128], o[:, h * 128:(h + 1) * 128], AF.Square,
                                                     scale=1.0 / math.sqrt(128.0), accum_out=small[:, 16 + h:17 + h]),
                  reads=[pAk[0][h // 4]], writes=['junk', 'sq%d' % h])
        sqk = ['sq%d' % h for h in range(8)]
        fw.op('act', lambda a: a.activation(small[:, 24:32], small[:, 16:24], AF.Sqrt, bias=epsc[:, 0:1], scale=1.0),
              reads=sqk + ['epsc'], writes=['sq_s'])
        fw.op('dve', lambda v: v.reciprocal(small[:, 32:40], small[:, 24:32]), reads=['sq_s'], writes=['sq_r'])
        q3 = qf[:].rearrange("p (h d) -> p h d", d=128)
        fw.op('dve', lambda v: v.tensor_tensor(q3, o[:].rearrange("p (h d) -> p h d", d=128),
                                               small[:, 32:40].unsqueeze(2).to_broadcast([128, 8, 128]), ALU.mult),
              reads=['pA0a', 'pA0b', 'sq_r'], writes=['qf'])
        fw.op('pool', lambda g: g.tensor_tensor(q3, q3, qn_bc[:].unsqueeze(1).to_broadcast([128, 8, 128]), ALU.mult),
              reads=['qf', 'qn_bc'], writes=['qf'])
        x1 = q3[:, :, 0:16]
        x2 = q3[:, :, 16:32]
        tt = rope_inplace('pool', x1, x2, rope_q[:, tile, 0:16], rope_q[:, tile, 16:32], rt_, 'rt', ['qf', 'rope_q'], [128, 8, 16])
        fw.op('pool', lambda g: g.tensor_tensor(x1, tt[0], tt[1], ALU.subtract), reads=['rt0', 'rt1', 'rt2', 'rt3'], writes=['qf'])
        fw.op('pool', lambda g: g.tensor_tensor(x2, tt[2], tt[3], ALU.add), reads=['rt2', 'rt3'], writes=['qf'])
        fw.op('act', lambda a: a.copy(qb[:], qf[:]), reads=['qf'], writes=['qb'])
        for h in range(8):
            fw.op('pe', lambda t_, h=h: t_.transpose(pT[:, h, :], qb[:, h * 128:(h + 1) * 128], identb[:]),
                  reads=['qb', 'identb'], writes=['pT'])
        fw.op('dve', lambda v: v.tensor_copy(qT[:, t, 0, :], pT[:, 0:4, :].rearrange("p h q -> p (h q)")), reads=['pT'], writes=['qT%d' % t])
        fw.op('dve', lambda v: v.tensor_copy(qT[:, t, 1, :], pT[:, 4:8, :].rearrange("p h q -> p (h q)")), reads=['pT'], writes=['qTb%d' % t])

    def qi_post(tile, t):
        o = pA[1]
        fw.op('act', lambda a: a.copy(qb[:], o[:]), reads=['pA1a', 'pA1b'], writes=['qb'])
        o3 = o[:].rearrange("p (h d) -> p h d", d=64)
        b3 = qb[:].rearrange("p (h d) -> p h d", d=64)
        tt = rope_inplace('dve', o3[:, :, 0:8], o3[:, :, 8:16], rope_i[:, tile, 0:8], rope_i[:, tile, 8:16], rt_, 'rt',
                          ['pA1a', 'pA1b', 'rope_i'], [128, 16, 8])
        fw.op('pool', lambda g: g.tensor_tensor(b3[:, :, 0:8], tt[0], tt[1], ALU.subtract), reads=['rt0', 'rt1', 'rt2', 'rt3'], writes=['qb'])
        fw.op('pool', lambda g: g.tensor_tensor(b3[:, :, 8:16], tt[2], tt[3], ALU.add), reads=['rt2', 'rt3'], writes=['qb'])
        for c in range(8):
            fw.op('pe', lambda t_, c=c: t_.transpose(pT[:, c, :], qb[:, c * 128:(c + 1) * 128], identb[:]),
                  reads=['qb', 'identb'], writes=['pT'])
        for half in (0, 1):
            qv = qiT[half * 64:(half + 1) * 64, t, :, half * 64:(half + 1) * 64].rearrange("p r (c q) -> p r c q", q=8)
            for c0 in (0, 4):
                src = pT[half * 64:(half + 1) * 64, c0:c0 + 4, :].rearrange("p c (r q) -> p r c q", q=8)
                if half == 0:
                    fw.op('dve', lambda v: v.tensor_copy(qv[:, :, c0:c0 + 4, :], src), reads=['pT'], writes=['qiT%d' % t])
                else:
                    fw.op('act', lambda a: a.copy(qv[:, :, c0:c0 + 4, :], src), reads=['pT'], writes=['qiTb%d' % t])

    def proj_tok(o, okeys, srcT, srckeys, t, slabs):
        for h in range(2):
            rt, rk = slabs[h]
            for kc in range(8):
                fw.op('pe', lambda t_, kc=kc, rt=rt, h=h: t_.matmul(o[:, h * 512:(h + 1) * 512], srcT[:, kc, t * 128:(t + 1) * 128], rt[:, kc, :],
                                                                  start=(kc == 0), stop=(kc == 7)),
                      reads=[srckeys[kc], rk], writes=[okeys[h]])

    def indexer(s, j, t):
        scores = scb[t]
        biasm = biasb[t]
        nk = (j + 1) * 128
        nch = (nk + 511) // 512
        qsl = slice(t * 128, (t + 1) * 128)
        fw.op('pe', lambda t_: t_.matmul(pN[:, 0:128], aselb[:], wiT[:, qsl], start=True, stop=True),
              reads=['aselb', 'wiT'], writes=['pN'])
        fw.op('act', lambda a: a.copy(s1sb[:], pN[:, 0:128]), reads=['pN'], writes=['s1sb'])
        fw.op('pool', lambda g: g.tensor_tensor(wall[:], s1sb[:].unsqueeze(1).to_broadcast([128, 16, 128]), maskw[:], ALU.mult),
              reads=['s1sb', 'maskw'], writes=['wall'])
        for ch in range(nch):
            k0 = ch * 512
            w = min(512, nk - k0)
            kik = ['KIa%d' % i for i in range(k0 // 128, (k0 + w) // 128)] + ['KIb%d' % i for i in range(k0 // 128, (k0 + w) // 128)]
            rinfo = {}
            for r in range(17):
                if r < 16:
                    d = rr['d'] % 2
                    rr['d'] += 1
                    fw.op('pe', lambda t_, d=d, r=r: t_.matmul(pD[d][:, 0:w], qiT[:, t, r, :], KIT2[:, k0:k0 + w], start=True, stop=True),
                          reads=['qiT%d' % t, 'qiTb%d' % t] + kik, writes=[pDk[d]])
                    ri = rr['r'] % 2
                    rr['r'] += 1
                    fw.op('act', lambda a, d=d, ri=ri: a.activation(Rb[ri][:, 0:w], pD[d][:, 0:w], AF.Relu), reads=[pDk[d]], writes=['Rb%d' % ri])
                    rinfo[r] = ri
                if r >= 1:
                    r1 = r - 1
                    ri1 = rinfo[r1]
                    fw.op('pe', lambda t_, r1=r1, ri1=ri1: t_.matmul(pS[:, 0:w], wall[:, r1, :], Rb[ri1][:, 0:w], start=(r1 == 0), stop=(r1 == 15)),
                          reads=['wall', 'Rb%d' % ri1], writes=['pS'])
            fw.op('act', lambda a: a.copy(scores[:, k0:k0 + w], pS[:, 0:w]), reads=['pS', 'stage'], writes=['sc%d_%d' % (t, ch)])
        sck = ['sc%d_%d' % (t, ch) for ch in range(nch)]
        fw.op('dve', lambda v: v.tensor_reduce(thr[:, 0:1], scores[:, 0:nk], AX.X, ALU.max), reads=sck, writes=['thr0'])
        fw.op('dve', lambda v: v.tensor_reduce(thr[:, 1:2], scores[:, 0:nk], AX.X, ALU.min), reads=sck, writes=['thr1'])
        fw.op('pool', lambda g: g.tensor_tensor(scores[:, j * 128:nk], scores[:, j * 128:nk], cbias[:], ALU.add),
              reads=sck + ['cbias', 'thr0', 'thr1'], writes=['scd%d' % t])
        fw.op('dve', lambda v: v.tensor_tensor(thr[:, 2:3], thr[:, 0:1], thr[:, 1:2], ALU.subtract), reads=['thr0', 'thr1'], writes=['thr2'])
        fw.op('dve', lambda v: v.tensor_scalar(thr[:, 3:4], thr[:, 2:3], 1.02, 2e-3, ALU.mult, ALU.add), reads=['thr2'], writes=['thr3'])
        fw.op('dve', lambda v: v.tensor_scalar(thr[:, 4:5], thr[:, 2:3], -0.01, -1e-3, ALU.mult, ALU.add), reads=['thr2'], writes=['thr4'])
        fw.op('dve', lambda v: v.tensor_tensor(thr[:, 4:5], thr[:, 4:5], thr[:, 1:2], ALU.add), reads=['thr4', 'thr1'], writes=['thr4'])
        fw.op('dve', lambda v: v.tensor_scalar(wk[:, 0, :], pw[:, 0, :], thr[:, 3:4], None, ALU.mult), reads=['pw', 'thr3'], writes=['wk0'])
        fw.op('dve', lambda v: v.tensor_scalar(wk[:, 1, :], pw[:, 1, :], thr[:, 3:4], None, ALU.mult), reads=['pw', 'thr3'], writes=['wk1'])
        fw.op('dve', lambda v: v.tensor_tensor(thr[:, 5:6], thr[:, 4:5], wk[:, 1, 0:1], ALU.add), reads=['thr4', 'wk1'], writes=['mid'])
        for it in range(NIT):
            fw.op('dve', lambda v: v.tensor_scalar(biasm[:, 0:nk], scores[:, 0:nk], thr[:, 5:6], None, ALU.is_ge, ALU.add, accum_out=thr[:, 6:7]),
                  reads=sck + ['scd%d' % t, 'mid'], writes=['biasm%d' % t, 'cnt'])
            fw.op('dve', lambda v, it=it: v.tensor_scalar(thr[:, 7:8], thr[:, 6:7], c255[:, 0:1], wk[:, 1, it:it + 1], ALU.is_ge, ALU.mult),
                  reads=['cnt', 'wk1', 'c255'], writes=['dlt'])
            fw.op('dve', lambda v, it=it: v.scalar_tensor_tensor(thr[:, 5:6], thr[:, 5:6], wk[:, 0, it:it + 1], thr[:, 7:8], ALU.subtract, ALU.add),
                  reads=['mid', 'wk0', 'dlt'], writes=['mid'])
        fw.op('dve', lambda v: v.tensor_tensor(thr[:, 5:6], thr[:, 5:6], wk[:, 0, NIT - 1:NIT], ALU.subtract), reads=['mid', 'wk0'], writes=['mid'])
        fw.op('dve', lambda v: v.tensor_scalar(biasm[:, 0:nk], scores[:, 0:nk], thr[:, 5:6], cneg[:, 0:1], ALU.is_lt, ALU.mult),
              reads=sck + ['scd%d' % t, 'mid', 'cneg'], writes=['biasm%d' % t])

    def attention(s, j, t):
        biasm = biasb[t]
        qsl = slice(t * 128, (t + 1) * 128)
        nc_ = j + 1
        aOs = [pS[:], pA[0][:, 0:512]]
        aNs = [pN[:], pA[0][:, 512:1024]]
        kOs = ['pS', 'pA0a']
        kNs = ['pN', 'pA0b']
        items = [(kvh, c) for kvh in range(2) for c in range(nc_)]
        info = {}

        def stage1(i):
            kvh, c = items[i]
            d = rr['d'] % 2
            rr['d'] += 1
            fw.op('pe', lambda t_: t_.matmul(pD[d], KT[:, kvh, c * 128:(c + 1) * 128], qT[:, t, kvh, :], start=True, stop=False),
                  reads=['KT%d' % c, 'qT%d' % t, 'qTb%d' % t], writes=[pDk[d]])
            fw.op('pe', lambda t_: t_.matmul(pD[d], biasm[:, c * 128:(c + 1) * 128], ident4[:].rearrange("p a b -> p (a b)"), start=False, stop=True),
                  reads=['biasm%d' % t, 'ident4_0', 'ident4_1', 'ident4_2', 'ident4_3'], writes=[pDk[d]])
            pi = rr['p'] % 2
            rr['p'] += 1
            fw.op('act', lambda a: a.activation(Pb[pi][:], pD[d], AF.Exp, scale=1.0 / math.sqrt(128.0)), reads=[pDk[d]], writes=['Pb%d' % pi])
            info[i] = pi

        def stage2(i):
            kvh, c = items[i]
            pi = info[i]
            aO, aN, kO, kN = aOs[kvh], aNs[kvh], kOs[kvh], kNs[kvh]
            fw.op('pe', lambda t_: t_.matmul(aO, Vr[:, c, kvh * 128:(kvh + 1) * 128], Pb[pi][:], start=(c == 0), stop=(c == nc_ - 1)),
                  reads=['Vr%d' % c, 'Pb%d' % pi], writes=[kO])
            fw.op('pe', lambda t_: t_.matmul(aN, onesb[:], Pb[pi][:], start=(c == 0), stop=(c == nc_ - 1)),
                  reads=['onesb', 'Pb%d' % pi], writes=[kN])
            if c == nc_ - 1:
                fw.op('dve', lambda v: v.reciprocal(rden[:], aN), reads=[kN], writes=['rden'])
                fw.op('dve', lambda v: v.tensor_tensor(otmp[:], aO, rden[:], ALU.mult), reads=[kO, 'rden'], writes=['otmp'])
                fw.op('pool', lambda g: g.tensor_tensor(ogT[:, 4 * kvh:4 * kvh + 4, qsl], otmp[:].rearrange("p (h q) -> p h q", q=128),
                                                        sgT[:, 4 * kvh:4 * kvh + 4, qsl], ALU.mult),
                      reads=['otmp'] + ['sgT%d' % m for m in range(4 * kvh, 4 * kvh + 4)],
                      writes=['ogT%d_%d' % (t, kvh)] + ['ogL%d' % m for m in range(4 * kvh, 4 * kvh + 4)])
        for i in range(len(items) + 1):
            if i < len(items):
                stage1(i)
            if i >= 1:
                stage2(i - 1)

    def residual(buf, t, l, okey_prefix, pa=0):
        xk = 'xt%d_%d' % (buf, t)
        for h in range(2):
            fw.op('dve', lambda v, h=h: v.tensor_tensor(qf[:, h * 512:(h + 1) * 512], pA[pa][:, h * 512:(h + 1) * 512],
                                                       gate_bc[l][:, h * 512:(h + 1) * 512], ALU.mult),
                  reads=[pAk[pa][h], 'gate_bc%d_%d' % (l, h)], writes=['qf'])
        fw.op('pool', lambda g: g.tensor_tensor(xt[buf][:, t, :], xt[buf][:, t, :], qf[:], ALU.add), reads=['qf', xk], writes=[xk])

    lvec = sb("lvec", [128, 8, 8])
    lc8 = sb("lc8", [128, 2, 8])
    wab = sb("wab", [128, 2, 8, 256], BF16)

    def setup_l1():
        for j in range(4):
            fw.dma('sp', lvec[:, j, :], conv_w[j, :].rearrange("(c p) -> p c", p=128), writes=['lvec%d' % j], allow_slow_non_contiguous=True)
        for j, src in enumerate([conv_b, b_a, b_x, lam]):
            fw.dma('sp', lvec[:, 4 + j, :], src[0, :].rearrange("(c p) -> p c", p=128), writes=['lvec%d' % (4 + j)], allow_slow_non_contiguous=True)
        fw.op('act', lambda a: a.activation(lc8[:, 0, :], lvec[:, 7, :], AF.Exp, scale=-1.0), reads=['lvec7'], writes=['lc8'])
        fw.op('act', lambda a: a.activation(lc8[:, 0, :], lc8[:, 0, :], AF.Ln, bias=onec[:, 0:1], scale=1.0), reads=['lc8', 'onec'], writes=['lc8'])
        fw.op('dve', lambda v: v.tensor_scalar_mul(lc8[:, 1, :], lc8[:, 0, :], -16.0), reads=['lc8'], writes=['lc8b'])
        fw.op('dve', lambda v: v.tensor_scalar_mul(lc8[:, 0, :], lc8[:, 0, :], -8.0), reads=['lc8', 'lc8b'], writes=['lc8'])
        for g_, src in enumerate([w_a, w_x]):
            fw.dma('pool', wab[:, g_, :, :], src.rearrange("(n p) c -> p n c", p=128), writes=['wab%d' % g_])

    def layer1(s, sbi, buf):
        tile0 = sbi * 2
        for t in range(2):
            xk = 'xt%d_%d' % (buf, t)
            norm_tile(xt[buf][:, t, :], xk, 'xn')
            to_hT(1, s, t * 128)
        slabs = [slab(s_lwin, i * 512, 512, k_lwin) for i in range(2)]
        for m in range(8):
            f = rr['f'] % 2
            rr['f'] += 1
            rt, rk = slabs[m // 4]
            for kc in range(8):
                fw.op('pe', lambda t_, kc=kc, rt=rt, m=m, f=f: t_.matmul(pFv[f], rt[:, kc, (m % 4) * 128:(m % 4 + 1) * 128], hT[:, kc, :],
                                                                       start=(kc == 0), stop=(kc == 7)),
                      reads=[hTk[kc], rk], writes=[pFk[f]])
            if m % 2 == 0:
                fw.op('act', lambda a, m=m, f=f: a.copy(xbT[:, m, 3:259], pFv[f]), reads=[pFk[f]], writes=['xbT%d' % m])
            else:
                fw.op('dve', lambda v, m=m, f=f: v.tensor_copy(xbT[:, m, 3:259], pFv[f]), reads=[pFk[f]], writes=['xbT%d' % m])
        slabs = [slab(s_lwin, 1024 + i * 512, 512, k_lwin) for i in range(2)]
        for m in range(8):
            f = rr['f'] % 2
            rr['f'] += 1
            rt, rk = slabs[m // 4]
            for kc in range(8):
                fw.op('pe', lambda t_, kc=kc, rt=rt, m=m, f=f: t_.matmul(pFv[f], rt[:, kc, (m % 4) * 128:(m % 4 + 1) * 128], hT[:, kc, :],
                                                                       start=(kc == 0), stop=(kc == 7)),
                      reads=[hTk[kc], rk], writes=[pFk[f]])
            fw.op('act', lambda a, m=m, f=f: a.activation(sgT[:, m, :], pFv[f], AF.Silu), reads=[pFk[f]], writes=['sgT%d' % m])
        lk = ['lvec%d' % i for i in range(8)]
        for nb in range(4):
            for ki in range(2):
                m = nb * 2 + ki
                xk_ = 'xbT%d' % m
                fw.op('pool', lambda g, m=m, ki=ki: g.tensor_scalar(xc2[:, ki, :], xbT[:, m, 0:256], lvec[:, 0, m:m + 1], lvec[:, 4, m:m + 1], ALU.mult, ALU.add),
                      reads=[xk_, 'xh%d' % m] + lk, writes=['xc2_%d' % ki])
                for jj in range(1, 4):
                    fw.op('dve', lambda g, m=m, ki=ki, jj=jj: g.scalar_tensor_tensor(xc2[:, ki, :], xbT[:, m, jj:jj + 256], lvec[:, jj, m:m + 1], xc2[:, ki, :], ALU.mult, ALU.add),
                          reads=[xk_, 'xh%d' % m, 'xc2_%d' % ki] + lk, writes=['xc2_%d' % ki])
                fw.op('act', lambda a, ki=ki: a.copy(xcb2[:, ki, :], xc2[:, ki, :]), reads=['xc2_%d' % ki], writes=['xcb2_%d' % ki])
                fw.op('pool', lambda g, m=m: g.tensor_copy(xbT[:, m, 0:3], xbT[:, m, 256:259]), reads=[xk_, 'xc2_%d' % ki], writes=['xh%d' % m])
            for mo in range(2):
                m = nb * 2 + mo
                gb = gbuf
                for g_ in range(2):
                    f = rr['f'] % 2
                    rr['f'] += 1
                    for ki in range(2):
                        fw.op('pe', lambda t_, g_=g_, ki=ki, mo=mo, f=f: t_.matmul(pFv[f], wab[:, g_, nb * 2 + ki, mo * 128:(mo + 1) * 128], xcb2[:, ki, :],
                                                                              start=(ki == 0), stop=(ki == 1)),
                              reads=['wab%d' % g_, 'xcb2_0', 'xcb2_1'], writes=[pFk[f]])
                    fw.op('act', lambda a, g_=g_, f=f, m=m: a.activation(gb[:, g_, :], pFv[f], AF.Sigmoid, bias=lvec[:, 5 + g_, m:m + 1], scale=1.0),
                          reads=[pFk[f]] + lk, writes=['gb%d' % g_])
                fw.op('act', lambda a, m=m: a.activation(gb[:, 2, :], gb[:, 0, :], AF.Exp, scale=lc8[:, 0, m:m + 1]), reads=['gb0', 'lc8'], writes=['gb2'])
                fw.op('act', lambda a, m=m: a.activation(gb[:, 3, :], gb[:, 0, :], AF.Exp, scale=lc8[:, 1, m:m + 1]), reads=['gb0', 'lc8b'], writes=['gb3'])
                fw.op('act', lambda a: a.activation(gb[:, 3, :], gb[:, 3, :], AF.Sqrt, bias=onec[:, 0:1], scale=-1.0), reads=['gb3', 'onec'], writes=['gb3'])
                fw.op('pool', lambda g, mo=mo: g.tensor_tensor(gb[:, 1, :], gb[:, 1, :], xc2[:, mo, :], ALU.mult), reads=['gb1', 'xc2_%d' % mo], writes=['gb1'])
                fw.op('pool', lambda g: g.tensor_tensor(gb[:, 1, :], gb[:, 1, :], gb[:, 3, :], ALU.mult), reads=['gb1', 'gb3'], writes=['gb1'])
                fw.op('dve', lambda v, m=m: v.tensor_tensor_scan(gb[:, 4, :], gb[:, 2, :], gb[:, 1, :], hstate[:, m:m + 1], ALU.mult, ALU.add),
                      reads=['gb2', 'gb1', 'hst%d' % m], writes=['gb4'])
                fw.op('dve', lambda v, m=m: v.tensor_copy(hstate[:, m:m + 1], gb[:, 4, 255:256]), reads=['gb4'], writes=['hst%d' % m])
                fw.op('pool', lambda g, m=m: g.tensor_tensor(ogT[:, m, :], gb[:, 4, :], sgT[:, m, :], ALU.mult), reads=['gb4', 'sgT%d' % m], writes=['ogL%d' % m, 'ogT0_%d' % (m // 4), 'ogT1_%d' % (m // 4)])
        slabs = [slab(s_lwout, i * 512, 512, k_lwout) for i in range(2)]
        for t in range(2):
            proj_tok(pA[t], pAk[t], ogT, ['ogL%d' % m for m in range(8)], t, slabs)
            residual(buf, t, 1, 'y', pa=t)
            row0 = s * SEQ + (tile0 + t) * 128
            fw.dma('pool', y_p[row0:row0 + 128, :], xt[buf][:, t, :], reads=['xt%d_%d' % (buf, t)], writes=['o_y'])

    def phase2(s):
        make_gate(0, s)
        make_gate(1, s)
        fw.op('pool', lambda g: g.memset(hstate[:], 0.0), reads=['hst%d' % m for m in range(8)], writes=['hst%d' % m for m in range(8)])
        for m in range(8):
            fw.op('pool', lambda g, m=m: g.memset(xbT[:, m, 0:3], 0.0), reads=['xbT%d' % m], writes=['xh%d' % m])
        for sbi in range(min(NT // 2, NSB_LIMIT)):
            buf = sbi % 2
            row0 = s * SEQ + sbi * 256
            for t in range(2):
                fw.dma('sp', xt[buf][:, t, :], xp[row0 + t * 128:row0 + (t + 1) * 128, :], writes=['xt%d_%d' % (buf, t)])
            for t in range(2):
                norm_tile(xt[buf][:, t, :], 'xt%d_%d' % (buf, t), 'xn')
                to_hT(0, s, t * 128)
            qs = [slab(s_win, OQ + i * 512, 512, k_win) for i in range(2)]
            qis = [slab(s_win, OQI + i * 512, 512, k_win) for i in range(2)]
            for t in range(2):
                proj_tok(pA[0], pAk[0], hT, hTk, t, qs)
                proj_tok(pA[1], pAk[1], hT, hTk, t, qis)
                q_post(sbi * 2 + t, t)
                qi_post(sbi * 2 + t, t)
            gs = [slab(s_win, OG + i * 512, 512, k_win) for i in range(2)]
            for m in range(8):
                f = rr['f'] % 2
                rr['f'] += 1
                rt, rk = gs[m // 4]
                for kc in range(8):
                    fw.op('pe', lambda t_, kc=kc, rt=rt, m=m, f=f: t_.matmul(pFv[f], rt[:, kc, (m % 4) * 128:(m % 4 + 1) * 128], hT[:, kc, :],
                                                                           start=(kc == 0), stop=(kc == 7)),
                          reads=[hTk[kc], rk], writes=[pFk[f]])
                fw.op('act', lambda a, m=m, f=f: a.activation(sgT[:, m, :], pFv[f], AF.Silu), reads=[pFk[f]], writes=['sgT%d' % m])
            for kc in range(8):
                fw.op('pe', lambda t_, kc=kc: t_.matmul(pN[0:16, 0:256], wwi[:, kc, :], hT[:, kc, :], start=(kc == 0), stop=(kc == 7)),
                      reads=[hTk[kc], 'wwi'], writes=['pN'])
            fw.op('act', lambda a: a.activation(wiT[:], pN[0:16, 0:256], AF.Copy, scale=1.0 / 32.0), reads=['pN'], writes=['wiT'])
            for t in range(2):
                indexer(s, sbi * 2 + t, t)
            for t in range(2):
                attention(s, sbi * 2 + t, t)
            wos = [slab(s_wout0, i * 512, 512, k_wout0) for i in range(2)]
            for t in range(2):
                proj_tok(pA[t], pAk[t], ogT, ['ogT%d_%d' % (t, kv) for kv in range(2) for _ in range(4)], t, wos)
                residual(buf, t, 0, 'x1', pa=t)
            if STAGE >= 3:
                layer1(s, sbi, buf)
            else:
                for t in range(2):
                    r0 = row0 + t * 128
                    fw.dma('pool', y_p[r0:r0 + 128, :], xt[buf][:, t, :], reads=['xt%d_%d' % (buf, t)], writes=['o_y'])
        if STAGE >= 3:
            for j in range(3):
                fw.dma('pool', conv_p[s * 3 + j, :].rearrange("(c p) -> p c", p=128), xbT[:, :, j],
                       reads=['xh%d' % m for m in range(8)], writes=['o_conv'], allow_slow_non_contiguous=True)
            fw.dma('pool', h_p[s, :].rearrange("(c p) -> p c", p=128), hstate[:], reads=['hst%d' % m for m in range(8)], writes=['o_h'],
                   allow_slow_non_contiguous=True)

    def sample_path():
        from contextlib import ExitStack
        st = ExitStack()

        def sbs(name, shape, dt=F32):
            return st.enter_context(nc.sbuf_tensor(name, shape, dt))
        NS = 16
        NITS = 20
        RC = 6
        NCAND = RC * 8
        scale = 1.0 / math.sqrt(128.0)
        xs_t = sbs("xs_t", [NS, D])
        AT = sbs("AT", [128, 2, 8, NS])
        BT = sbs("BT", [128, 2, 8, NS])
        gate_s = sbs("gate_s", [NS, 2, D])
        kfs = sbs("kfs", [NS, 576])
        sgs = sbs("sgs", [NS, D])
        qiTs = sbs("qiTs", [128, 4, 64], BF16)
        ki2T = sbs("ki2T", [128, NS], BF16)
        wiTs = sbs("wiTs", [16, NS], BF16)
        wsm = sbs("wsm", [64, NS], BF16)
        wbig = sbs("wbig", [64, 4, 8, 128], BF16)
        Gb = [sbs("Gb%d" % i, [128, 1024]) for i in range(2)]
        KIc = [sbs("KIc%d" % i, [128, 2048], BF16) for i in range(2)]
        Rs = [sbs("Rs%d" % i, [64, 512], BF16) for i in range(2)]
        SCs = sbs("SCs", [128, 2052])
        cs = {k: sbs("cs_" + k, shp, dt) for k, (shp, dt) in {
            "rope_qs": ([NS, 32], F32), "rope_is": ([NS, 16], F32), "asel_s": ([16, 64], F32), "mask_s": ([64, NS], F32),
            "g128": ([128, 128], F32), "rep": ([NS, 128], F32), "repT": ([128, NS], F32), "maskE": ([128, 4], F32),
            "cbase": ([128, 1], F32), "selj": ([NS, 4 * NS], F32), "iota": ([128, 128], F32), "pws": ([128, 2 * (NITS + 1)], F32)}.items()}
        asel_sb = sbs("asel_sb", [16, 64], BF16)
        ptT = sbs("ptT_sb", [128, 4], I32)
        pt128i = sbs("pt128i", [128, 128], I32)
        pt128f = sbs("pt128f", [128, 128])
        onesf = sbs("onesf", [2, 128])
        cnb = sbs("cnb", [128, 1])
        th = sbs("th", [128, 16])
        wks = sbs("wks", [128, 2, NITS + 1])
        cval = sbs("cval", [128, NCAND])
        cidx = sbs("cidx", [128, NCAND], mybir.dt.uint32)
        cf = sbs("cf", [128, 6, NCAND])
        physi = sbs("physi", [128, NCAND], I32)
        oh = stage[:].rearrange("p (s g) -> p s g", g=128)
        Kg = [sbs("Kg0", [128, 8, 256])]
        Vg = [sbs("Vg0", [128, 8, 256])]
        qrep = sbs("qrep", [128, D])
        acc = sbs("acc", [128, D])
        prod = sbs("prod", [128, 8, 128])
        sc8 = sbs("sc8", [128, 8, 8])
        pe8 = sbs("pe8", [128, 8, 8])
        dens = sbs("dens", [128, 2, 8])
        k2rep = Kg[0][0:NS, 0:4, :]
        v2rep = Vg[0][0:NS, 0:4, :]
        scn = sbs("scn", [NS, 3, 8, 4])
        accn = sbs("accn", [NS, D])
        ogb = sbs("ogb", [NS, D], BF16)
        xbs = sbs("xbs", [128, 8, 4, 7])
        xcs = sbs("xcs", [128, 2, 8, NS])
        xcbs = sbs("xcbs", [128, 8, NS], BF16)
        sg1s = sbs("sg1s", [128, 8, NS])
        rg = sbs("rg", [128, 6, 8, NS])
        hst = sbs("hst", [128, 2, 8, 4])
        hss = sbs("hss", [128, 8, 4, 4])
        cso = sbs("cso", [128, 8, 12])
        tok12 = accn

        for k in cs:
            fw.dma('sp', cs[k][:], cst[k], writes=['cs_' + k])
        fw.op('dve', lambda v: v.tensor_copy(asel_sb[:], cs["asel_s"][:]), reads=['cs_asel_s'], writes=['asel_sb'])
        fw.dma('sp', ptT[:], d_ptT, writes=['ptT'])
        fw.dma('sp', pt128i[:], d_pt128, writes=['pt128i'])
        fw.op('dve', lambda v: v.tensor_copy(pt128f[:], pt128i[:]), reads=['pt128i'], writes=['pt128f'])
        pt8f = sbs("pt8f", [128, 4, 8])
        pt8i = sbs("pt8i", [128, 4, 8], I32)
        fw.op('dve', lambda v: v.tensor_copy(pt8f[:, :, 0], ptT[:]), reads=['ptT'], writes=['pt8f'])
        fw.op('dve', lambda v: v.tensor_scalar(pt8f[:], pt8f[:, :, 0:1].to_broadcast([128, 4, 8]), 8.0, None, ALU.mult), reads=['pt8f'], writes=['pt8f'])
        fw.op('dve', lambda v: v.tensor_tensor(pt8f[:], pt8f[:], cs["iota"][:, 0:8].unsqueeze(1).to_broadcast([128, 4, 8]), ALU.add), reads=['pt8f', 'cs_iota'], writes=['pt8f'])
        fw.op('dve', lambda v: v.tensor_copy(pt8i[:], pt8f[:]), reads=['pt8f'], writes=['pt8i'])
        fw.op('pool', lambda g: g.memset(onesf[:], 1.0), writes=['onesf'])
        fw.op('pool', lambda g: g.memset(cnb[:], -1e30), writes=['cnb'])
        fw.op('pool', lambda g: g.memset(wbig[:], 0.0), writes=['wbig'])
        for l in range(2):
            for g_ in range(4):
                fw.op('dve', lambda v, l=l, g_=g_: v.tensor_copy(AT[:, l, :, 4 * g_:4 * g_ + 4], Sfm[l][:, :, 2 + g_].unsqueeze(2).to_broadcast([128, 8, 4])),
                      reads=modkeys[l], writes=['AT%d_%d' % (l, g_)])
                fw.op('dve', lambda v, l=l, g_=g_: v.tensor_copy(BT[:, l, :, 4 * g_:4 * g_ + 4], Bfm[l][:, :, 2 + g_].unsqueeze(2).to_broadcast([128, 8, 4])),
                      reads=modkeys[l], writes=['BT%d_%d' % (l, g_)])
            for h in range(2):
                fw.op('pe', lambda t, h=h, l=l: t.matmul(pA[0][0:NS, h * 512:(h + 1) * 512], self_[:, 2, 0:NS], mgate[l][:, h * 512:(h + 1) * 512], start=True, stop=True),
                      reads=['sel', 'mgate%d_%d' % (l, 4 + h)], writes=[pAk[0][h]])
                fw.op('act', lambda a, h=h, l=l: a.copy(gate_s[:, l, h * 512:(h + 1) * 512], pA[0][0:NS, h * 512:(h + 1) * 512]),
                      reads=[pAk[0][h]], writes=['gate_s%d_%d' % (l, h)])
        ATk = [['AT%d_%d' % (l, g_) for g_ in range(4)] + ['BT%d_%d' % (l, g_) for g_ in range(4)] for l in range(2)]

        def to_hT_s(l):
            for c in range(8):
                fw.op('pe', lambda t, c=c: t.transpose(pT[:, c, 0:NS], xn[0:NS, c * 128:(c + 1) * 128], identb[0:NS, 0:NS]),
                      reads=['xn', 'identb'], writes=['pT'])
            fw.op('dve', lambda v: v.tensor_tensor(qrep[:, 0:128].rearrange("p (c t) -> p c t", t=NS), pT[:, :, 0:NS], AT[:, l], ALU.mult),
                  reads=['pT'] + ATk[l], writes=['qrep'])
            fw.op('dve', lambda v: v.tensor_tensor(hT[:, :, 0:NS], qrep[:, 0:128].rearrange("p (c t) -> p c t", t=NS), BT[:, l], ALU.add),
                  reads=['qrep'] + ATk[l], writes=hTk)

        def proj_s(o, okeys, slabs, ncols=512):
            for h in range(len(slabs)):
                rt, rk = slabs[h]
                for kc in range(8):
                    fw.op('pe', lambda t_, kc=kc, rt=rt, h=h: t_.matmul(o[0:NS, h * 512:h * 512 + ncols], hT[:, kc, 0:NS], rt[:, kc, 0:ncols],
                                                                      start=(kc == 0), stop=(kc == 7)),
                          reads=[hTk[kc], rk], writes=[okeys[h]])

        def rope_s(e, x1, x2, cs_, sn_, H, half, xkey):
            shape = [NS, H, half]
            cb_ = cs_.unsqueeze(1).to_broadcast(shape)
            sb_ = sn_.unsqueeze(1).to_broadcast(shape)
            t = [rt_[0:NS, i, 0:H * half].rearrange("p (h d) -> p h d", d=half) for i in range(4)]
            fw.op(e, lambda g: g.tensor_tensor(t[0], x1, cb_, ALU.mult), reads=xkey, writes=['rt0'])
            fw.op(e, lambda g: g.tensor_tensor(t[1], x2, sb_, ALU.mult), reads=xkey, writes=['rt1'])
            fw.op(e, lambda g: g.tensor_tensor(t[2], x2, cb_, ALU.mult), reads=xkey, writes=['rt2'])
            fw.op(e, lambda g: g.tensor_tensor(t[3], x1, sb_, ALU.mult), reads=xkey, writes=['rt3'])
            return t
        rtk = ['rt0', 'rt1', 'rt2', 'rt3']

        fw.dma('sp', xs_t[:], d_xs, writes=['xs_t'])
        norm_tile(xs_t[:], 'xs_t', 'xn', n=NS)
        to_hT_s(0)
        proj_s(pA[0], pAk[0], [slab(s_win, OQ + i * 512, 512, k_win) for i in range(2)])
        o = pA[0]
        for h in range(8):
            fw.op('act', lambda a, h=h: a.activation(junk[0:NS, 0:128], o[0:NS, h * 128:(h + 1) * 128], AF.Square,
                                                     scale=1.0 / math.sqrt(128.0), accum_out=small[0:NS, 16 + h:17 + h]),
                  reads=[pAk[0][h // 4]], writes=['junk', 'sq%d' % h])
        sqk = ['sq%d' % h for h in range(8)]
        fw.op('act', lambda a: a.activation(small[0:NS, 24:32], small[0:NS, 16:24], AF.Sqrt, bias=epsc[0:NS, 0:1], scale=1.0),
              reads=sqk + ['epsc'], writes=['sq_s'])
        fw.op('dve', lambda v: v.reciprocal(small[0:NS, 32:40], small[0:NS, 24:32]), reads=['sq_s'], writes=['sq_r'])
        q3 = qf[0:NS, :].rearrange("p (h d) -> p h d", d=128)
        fw.op('dve', lambda v: v.tensor_tensor(q3, o[0:NS, :].rearrange("p (h d) -> p h d", d=128),
                                               small[0:NS, 32:40].unsqueeze(2).to_broadcast([NS, 8, 128]), ALU.mult),
              reads=['pA0a', 'pA0b', 'sq_r'], writes=['qf'])
        fw.op('dve', lambda v: v.tensor_tensor(q3, q3, qn_bc[0:NS, :].unsqueeze(1).to_broadcast([NS, 8, 128]), ALU.mult),
              reads=['qf', 'qn_bc'], writes=['qf'])
        tt = rope_s('dve', q3[:, :, 0:16], q3[:, :, 16:32], cs["rope_qs"][:, 0:16], cs["rope_qs"][:, 16:32], 8, 16, ['qf', 'cs_rope_qs'])
        fw.op('dve', lambda v: v.tensor_tensor(q3[:, :, 0:16], tt[0], tt[1], ALU.subtract), reads=rtk, writes=['qf'])
        fw.op('dve', lambda v: v.tensor_tensor(q3[:, :, 16:32], tt[2], tt[3], ALU.add), reads=rtk, writes=['qf'])
        for h in range(2):
            fw.op('pe', lambda t, h=h: t.matmul(pA[0][:, h * 512:(h + 1) * 512], cs["rep"][:], qf[0:NS, h * 512:(h + 1) * 512], start=True, stop=True),
                  reads=['cs_rep', 'qf'], writes=[pAk[0][h]])
            fw.op('act', lambda a, h=h: a.copy(qrep[:, h * 512:(h + 1) * 512], pA[0][:, h * 512:(h + 1) * 512]), reads=[pAk[0][h]], writes=['qrep'])
        proj_s(pA[1], pAk[1], [slab(s_win, OK_, 512, k_win)])
        for kc in range(8):
            fw.op('pe', lambda t, kc=kc: t.matmul(pA[1][0:NS, 512:576], hT[:, kc, 0:NS], wki[:, kc, :], start=(kc == 0), stop=(kc == 7)),
                  reads=[hTk[kc], 'wki'], writes=['pA1b'])
        o = pA[1]
        for h in range(2):
            fw.op('act', lambda a, h=h: a.activation(junk[0:NS, 0:128], o[0:NS, h * 128:(h + 1) * 128], AF.Square,
                                                     scale=1.0 / math.sqrt(128.0), accum_out=small[0:NS, 4 + h:5 + h]),
                  reads=['pA1a'], writes=['junk', 'sm4_%d' % h])
        fw.op('act', lambda a: a.activation(small[0:NS, 6:8], small[0:NS, 4:6], AF.Sqrt, bias=epsc[0:NS, 0:1], scale=1.0),
              reads=['sm4_0', 'sm4_1', 'epsc'], writes=['sm6'])
        fw.op('dve', lambda v: v.reciprocal(small[0:NS, 8:10], small[0:NS, 6:8]), reads=['sm6'], writes=['sm8'])
        kv3 = kfs[:, 0:256].rearrange("p (h d) -> p h d", d=128)
        fw.op('dve', lambda v: v.tensor_tensor(kv3, o[0:NS, 0:256].rearrange("p (h d) -> p h d", d=128),
                                               small[0:NS, 8:10].unsqueeze(2).to_broadcast([NS, 2, 128]), ALU.mult),
              reads=['pA1a', 'sm8'], writes=['kfs_k'])
        fw.op('dve', lambda v: v.tensor_tensor(kv3, kv3, kn_bc[0:NS, :].unsqueeze(1).to_broadcast([NS, 2, 128]), ALU.mult),
              reads=['kfs_k', 'kn_bc'], writes=['kfs_k'])
        tt = rope_s('dve', kv3[:, :, 0:16], kv3[:, :, 16:32], cs["rope_qs"][:, 0:16], cs["rope_qs"][:, 16:32], 2, 16, ['kfs_k', 'cs_rope_qs'])
        fw.op('dve', lambda v: v.tensor_tensor(kv3[:, :, 0:16], tt[0], tt[1], ALU.subtract), reads=rtk, writes=['kfs_k'])
        fw.op('dve', lambda v: v.tensor_tensor(kv3[:, :, 16:32], tt[2], tt[3], ALU.add), reads=rtk, writes=['kfs_k'])
        fw.op('act', lambda a: a.copy(kfs[:, 256:512], o[0:NS, 256:512]), reads=['pA1a'], writes=['kfs_v'])
        fw.op('act', lambda a: a.copy(kfs[:, 512:576], o[0:NS, 512:576]), reads=['pA1b'], writes=['kfs_i'])
        y1 = kfs[:, 512:520].unsqueeze(1)
        y2 = kfs[:, 520:528].unsqueeze(1)
        tt = rope_s('dve', y1, y2, cs["rope_is"][:, 0:8], cs["rope_is"][:, 8:16], 1, 8, ['kfs_i', 'cs_rope_is'])
        fw.op('dve', lambda v: v.tensor_tensor(y1, tt[0], tt[1], ALU.subtract), reads=rtk, writes=['kfs_i'])
        fw.op('dve', lambda v: v.tensor_tensor(y2, tt[2], tt[3], ALU.add), reads=rtk, writes=['kfs_i'])
        fw.dma('pool', k_s, kfs[:, 0:256], reads=['kfs_k'], writes=['o_ks'])
        fw.dma('pool', v_s, kfs[:, 256:512], reads=['kfs_v'], writes=['o_vs'])
        fw.dma('pool', ik_s, kfs[:, 512:576], reads=['kfs_i'], writes=['o_iks'])
        fw.op('pe', lambda t: t.transpose(pS[0:64, 0:NS], kfs[:, 512:576], identf[0:NS, 0:NS]), reads=['kfs_i', 'identf'], writes=['pS'])
        fw.op('dve', lambda v: v.tensor_copy(ki2T[0:64, :], pS[0:64, 0:NS]), reads=['pS'], writes=['ki2Ta'])
        fw.op('dve', lambda v: v.tensor_copy(ki2T[64:128, :], pS[0:64, 0:NS]), reads=['pS'], writes=['ki2Tb'])
        proj_s(pA[0], pAk[0], [slab(s_win, OQI + i * 512, 512, k_win) for i in range(2)])
        o = pA[0]
        fw.op('act', lambda a: a.copy(qb[0:NS, :], o[0:NS, :]), reads=['pA0a', 'pA0b'], writes=['qb'])
        o3 = o[0:NS, :].rearrange("p (h d) -> p h d", d=64)
        b3 = qb[0:NS, :].rearrange("p (h d) -> p h d", d=64)
        tt = rope_s('dve', o3[:, :, 0:8], o3[:, :, 8:16], cs["rope_is"][:, 0:8], cs["rope_is"][:, 8:16], 16, 8, ['pA0a', 'pA0b', 'cs_rope_is'])
        fw.op('dve', lambda v: v.tensor_tensor(b3[:, :, 0:8], tt[0], tt[1], ALU.subtract), reads=rtk, writes=['qb'])
        fw.op('dve', lambda v: v.tensor_tensor(b3[:, :, 8:16], tt[2], tt[3], ALU.add), reads=rtk, writes=['qb'])
        for c in range(8):
            fw.op('pe', lambda t_, c=c: t_.transpose(pT[:, c, 0:NS], qb[0:NS, c * 128:(c + 1) * 128], identb[0:NS, 0:NS]),
                  reads=['qb', 'identb'], writes=['pT'])
        fw.op('pool', lambda g: g.memset(qiTs[:], 0.0), writes=['qiTs'])
        for half in (0, 1):
            fw.op('dve', lambda v: v.tensor_copy(qiTs[half * 64:(half + 1) * 64, :, half * 32:(half + 1) * 32].rearrange("p b (c t) -> p b c t", t=4),
                                                 pT[half * 64:(half + 1) * 64, :, 0:NS].rearrange("p c (b t) -> p b c t", t=4)), reads=['pT', 'qiTs'], writes=['qiTs'])
        proj_s(pA[1], pAk[1], [slab(s_win, OG + i * 512, 512, k_win) for i in range(2)])
        fw.op('act', lambda a: a.activation(sgs[:], pA[1][0:NS, :], AF.Silu), reads=['pA1a', 'pA1b'], writes=['sgs'])
        for kc in range(8):
            fw.op('pe', lambda t_, kc=kc: t_.matmul(pN[0:16, 0:NS], wwi[:, kc, :], hT[:, kc, 0:NS], start=(kc == 0), stop=(kc == 7)),
                  reads=[hTk[kc], 'wwi'], writes=['pN'])
        fw.op('act', lambda a: a.activation(wiTs[:], pN[0:16, 0:NS], AF.Copy, scale=1.0 / 32.0), reads=['pN'], writes=['wiTs'])
        fw.op('pe', lambda t_: t_.matmul(pN[0:64, 0:NS], asel_sb[:], wiTs[:], start=True, stop=True), reads=['asel_sb', 'wiTs'], writes=['pN'])
        fw.op('dve', lambda v: v.tensor_tensor(wsm[:], pN[0:64, 0:NS], cs["mask_s"][:], ALU.mult), reads=['pN', 'cs_mask_s'], writes=['wsm'])
        for bl in range(4):
            for cb in range(8):
                fw.op('pool', lambda g, bl=bl, cb=cb: g.tensor_copy(wbig[:, bl, cb, cb * 16 + bl * 4:cb * 16 + bl * 4 + 4], wsm[:, bl * 4:bl * 4 + 4]),
                      reads=['wsm'], writes=['wbig'])

        accb = [pA[0][:, 0:512], pA[0][:, 512:1024], pA[1][:, 0:512], pA[1][:, 512:1024]]
        acck = ['pA0a', 'pA0b', 'pA1a', 'pA1b']
        pFd = pF[:].rearrange("p a b -> p (a b)")
        trb = [pS, pN]
        trk = ['pS', 'pN']
        n_it = 0
        for bl in range(4):
            for cb in range(8):
                gi = n_it % 2
                fw.dma('pool', Gb[gi][:], d_cik[:, :], reads=['pt8i'], writes=['Gb%d' % gi],
                       indirect=dict(out_offset=None, in_offset=bass.IndirectOffsetOnAxis(ap=pt8i[:, bl, cb:cb + 1], axis=0)))
                for q4 in range(4):
                    tb = (n_it * 4 + q4) % 2
                    for s4 in range(4):
                        sl = q4 * 4 + s4
                        fw.op('pe', lambda t, tb=tb, s4=s4, sl=sl: t.transpose(trb[tb][0:64, s4 * 128:(s4 + 1) * 128], Gb[gi][:, sl * 64:(sl + 1) * 64], identf[:]),
                              reads=['Gb%d' % gi, 'identf'], writes=[trk[tb]])
                    fw.op('act', lambda a, tb=tb, q4=q4: a.copy(KIc[gi][0:64, q4 * 512:(q4 + 1) * 512], trb[tb][0:64, :]), reads=[trk[tb]], writes=['KIc%d_%da' % (gi, q4)])
                    fw.op('dve', lambda v, tb=tb, q4=q4: v.tensor_copy(KIc[gi][64:128, q4 * 512:(q4 + 1) * 512], trb[tb][0:64, :]), reads=[trk[tb]], writes=['KIc%d_%db' % (gi, q4)])
                for q4 in range(4):
                    kk = ['KIc%d_%da' % (gi, q4), 'KIc%d_%db' % (gi, q4)]
                    fw.op('pe', lambda t, q4=q4: t.matmul(pFd[0:64, :], qiTs[:, bl, :], KIc[gi][:, q4 * 512:(q4 + 1) * 512], start=True, stop=True),
                          reads=['qiTs'] + kk, writes=['pF0'])
                    ri = (n_it * 4 + q4) % 2
                    if q4 % 2 == 0:
                        fw.op('act', lambda a, ri=ri: a.activation(Rs[ri][:], pFd[0:64, :], AF.Relu), reads=['pF0'], writes=['Rs%d' % ri])
                    else:
                        fw.op('dve', lambda v, ri=ri: v.tensor_scalar_max(Rs[ri][:], pFd[0:64, :], 0.0), reads=['pF0'], writes=['Rs%d' % ri])
                    first = (bl == 0 and cb == 0)
                    last = (bl == 3 and cb == 7)
                    fw.op('pe', lambda t, q4=q4, ri=ri: t.matmul(accb[q4], wbig[:, bl, cb, :], Rs[ri][:], start=first, stop=last),
                          reads=['wbig', 'Rs%d' % ri], writes=[acck[q4]])
                n_it += 1
        for q4 in range(4):
            fw.op('act' if q4 % 2 == 0 else 'dve', lambda e_, q4=q4: (e_.copy if q4 % 2 == 0 else e_.tensor_copy)(SCs[:, q4 * 512:(q4 + 1) * 512], accb[q4]),
                  reads=[acck[q4]], writes=['SCs%d' % q4])
        for bl in range(4):
            fw.op('pe', lambda t, bl=bl: t.matmul(pFd[0:64, bl * 4:bl * 4 + 4], qiTs[:, bl, :], ki2T[:, bl * 4:bl * 4 + 4], start=True, stop=True),
                  reads=['qiTs', 'ki2Ta', 'ki2Tb'], writes=['pF0'])
        fw.op('act', lambda a: a.activation(Rs[0][:, 0:NS], pFd[0:64, 0:NS], AF.Relu), reads=['pF0'], writes=['Rs0'])
        for bl in range(4):
            fw.op('pe', lambda t, bl=bl: t.matmul(pS[:, 0:4], wbig[:, bl, 0, :], Rs[0][:, bl * 4:bl * 4 + 4], start=(bl == 0), stop=(bl == 3)),
                  reads=['wbig', 'Rs0'], writes=['pS'])
        fw.op('dve', lambda v: v.tensor_tensor(SCs[:, 2048:2052], pS[:, 0:4], cs["maskE"][:], ALU.add), reads=['pS', 'cs_maskE'], writes=['SCs4'])
        SCk = ['SCs%d' % i for i in range(5)]
        fw.op('dve', lambda v: v.tensor_reduce(th[:, 0:1], SCs[:, 0:2048], AX.X, ALU.max), reads=SCk, writes=['th0'])
        fw.op('dve', lambda v: v.tensor_reduce(th[:, 1:2], SCs[:, 0:2048], AX.X, ALU.min), reads=SCk, writes=['th1'])
        fw.op('dve', lambda v: v.tensor_scalar_mul(th[:, 1:2], th[:, 1:2], -1.0), reads=['th1'], writes=['th1'])
        fw.op('pe', lambda t: t.transpose(pN[0:2, 0:128], th[:, 0:2], identf[:]), reads=['th0', 'th1', 'identf'], writes=['pN'])
        fw.op('dve', lambda v: v.tensor_reduce(th[0:2, 2:3], pN[0:2, 0:128], AX.X, ALU.max), reads=['pN'], writes=['th2'])
        fw.op('dve', lambda v: v.tensor_scalar(th[0:2, 4:6], identf[0:2, 0:2], th[0:2, 2:3], None, ALU.mult), reads=['identf', 'th2'], writes=['th4'])
        fw.op('pe', lambda t: t.matmul(pN[:, 0:2], onesf[:], th[0:2, 4:6], start=True, stop=True), reads=['onesf', 'th4'], writes=['pN'])
        fw.op('dve', lambda v: v.tensor_copy(th[:, 6:8], pN[:, 0:2]), reads=['pN'], writes=['th6'])
        fw.op('dve', lambda v: v.tensor_tensor(th[:, 8:9], th[:, 6:7], th[:, 7:8], ALU.add), reads=['th6'], writes=['th8'])
        fw.op('dve', lambda v: v.tensor_scalar(th[:, 9:10], th[:, 8:9], 1.02, 2e-3, ALU.mult, ALU.add), reads=['th8'], writes=['th9'])
        fw.op('dve', lambda v: v.tensor_scalar(th[:, 10:11], th[:, 8:9], -0.01, -1e-3, ALU.mult, ALU.add), reads=['th8'], writes=['th10'])
        fw.op('dve', lambda v: v.tensor_tensor(th[:, 10:11], th[:, 10:11], th[:, 7:8], ALU.subtract), reads=['th10', 'th6'], writes=['th10'])
        pws = cs["pws"][:].rearrange("p (a b) -> p a b", a=2)
        fw.op('dve', lambda v: v.tensor_scalar(wks[:, 0, :], pws[:, 0, :], th[:, 9:10], None, ALU.mult), reads=['cs_pws', 'th9'], writes=['wks0'])
        fw.op('dve', lambda v: v.tensor_scalar(wks[:, 1, :], pws[:, 1, :], th[:, 9:10], None, ALU.mult), reads=['cs_pws', 'th9'], writes=['wks1'])
        fw.op('dve', lambda v: v.tensor_tensor(th[:, 11:12], th[:, 10:11], wks[:, 1, 0:1], ALU.add), reads=['th10', 'wks1'], writes=['mids'])
        for it in range(NITS):
            fw.op('dve', lambda v: v.tensor_scalar(stage[:, 0:2048], SCs[:, 0:2048], th[:, 11:12], None, ALU.is_ge, ALU.add, accum_out=th[:, 12:13]),
                  reads=SCk + ['mids', 'stage'], writes=['stage', 'cnta'])
            fw.op('dve', lambda v: v.tensor_scalar(junk[:, 0:4], SCs[:, 2048:2052], th[:, 11:12], None, ALU.is_ge, ALU.add, accum_out=th[:, 13:14]),
                  reads=SCk + ['mids'], writes=['junk', 'cntb'])
            fw.op('dve', lambda v: v.tensor_tensor(th[:, 12:13], th[:, 12:13], th[:, 13:14], ALU.add), reads=['cnta', 'cntb'], writes=['cnta'])
            fw.op('pe', lambda t: t.matmul(pN[:, 0:1], cs["g128"][:], th[:, 12:13], start=True, stop=True), reads=['cs_g128', 'cnta'], writes=['pN'])
            fw.op('dve', lambda v, it=it: v.tensor_scalar(th[:, 14:15], pN[:, 0:1], c255[:, 0:1], wks[:, 1, it:it + 1], ALU.is_ge, ALU.mult),
                  reads=['pN', 'wks1', 'c255'], writes=['dlts'])
            fw.op('dve', lambda v, it=it: v.scalar_tensor_tensor(th[:, 11:12], th[:, 11:12], wks[:, 0, it:it + 1], th[:, 14:15], ALU.subtract, ALU.add),
                  reads=['mids', 'wks0', 'dlts'], writes=['mids'])
        fw.op('dve', lambda v: v.tensor_tensor(th[:, 11:12], th[:, 11:12], wks[:, 0, NITS - 1:NITS], ALU.subtract), reads=['mids', 'wks0'], writes=['mids'])
        fw.op('dve', lambda v: v.tensor_scalar(stage[:], SCs[:, 0:2048], th[:, 11:12], cnb[:, 0:1], ALU.is_lt, ALU.mult), reads=SCk + ['mids', 'cnb', 'stage'], writes=['stage'])
        fw.op('dve', lambda v: v.tensor_tensor(stage[:], stage[:], SCs[:, 0:2048], ALU.add), reads=['stage'] + SCk, writes=['stage'])
        fw.op('dve', lambda v: v.tensor_scalar(scn[:, 2, 0, :], SCs[0:NS, 2048:2052], th[0:NS, 11:12], cneg[0:NS, 0:1], ALU.is_lt, ALU.mult),
              reads=SCk + ['mids', 'cneg'], writes=['nbias'])
        for r in range(RC):
            cv = cval[:, r * 8:(r + 1) * 8]
            fw.op('dve', lambda v: v.max(out=cv, in_=stage[:]), reads=['stage'], writes=['cval%d' % r])
            fw.op('dve', lambda v: v.max_index(out=cidx[:, r * 8:(r + 1) * 8], in_max=cv, in_values=stage[:]), reads=['stage', 'cval%d' % r], writes=['cidx%d' % r])
            fw.op('dve', lambda v: v.match_replace(out=stage[:], in_to_replace=cv, in_values=stage[:], imm_value=-1e30),
                  reads=['stage', 'cval%d' % r, 'cidx%d' % r], writes=['stage'])
        cvk = ['cval%d' % r for r in range(RC)]
        cik_ = ['cidx%d' % r for r in range(RC)]
        fw.op('dve', lambda v: v.tensor_copy(cf[:, 0, :], cidx[:]), reads=cik_, writes=['cf0'])
        thr16 = th[:, 0:16]
        fw.op('dve', lambda v: v.tensor_scalar(thr16, cs["iota"][:, 0:16], 1.0, 128.0, ALU.add, ALU.mult), reads=['cs_iota', 'mids', 'th0', 'th1', 'th2', 'th4', 'th6', 'th8', 'th9', 'th10', 'cnta', 'cntb', 'dlts', 'nbias'], writes=['thr16'])
        cmp3 = stage[:, 0:NCAND * 16].rearrange("p (s k) -> p s k", k=16)
        fw.op('dve', lambda v: v.tensor_tensor(cmp3, cf[:, 0, :].unsqueeze(2).to_broadcast([128, NCAND, 16]), thr16.unsqueeze(1).to_broadcast([128, NCAND, 16]), ALU.is_ge),
              reads=['cf0', 'thr16', 'stage'], writes=['stage'])
        fw.op('dve', lambda v: v.tensor_reduce(cf[:, 2, :], cmp3, AX.X, ALU.add), reads=['stage'], writes=['cf2'])
        fw.op('dve', lambda v: v.scalar_tensor_tensor(cf[:, 1, :], cf[:, 2, :], -128.0, cf[:, 0, :], ALU.mult, ALU.add), reads=['cf2', 'cf0'], writes=['cf1'])
        fw.op('dve', lambda v: v.tensor_scalar(cf[:, 2, :], cf[:, 2, :], cs["cbase"][:, 0:1], None, ALU.add), reads=['cf2', 'cf1', 'cs_cbase'], writes=['cf2'])
        for hf in range(NCAND // 16):
            pgv = cf[:, 1, hf * 16:(hf + 1) * 16]
            fw.op('dve', lambda v: v.tensor_tensor(oh, cs["iota"][:].unsqueeze(1).to_broadcast([128, 16, 128]), pgv.unsqueeze(2).to_broadcast([128, 16, 128]), ALU.is_equal),
                  reads=['cs_iota', 'cf1', 'stage'], writes=['stage'])
            fw.op('dve', lambda v: v.tensor_tensor(oh, oh, pt128f[:].unsqueeze(1).to_broadcast([128, 16, 128]), ALU.mult), reads=['stage', 'pt128f'], writes=['stage'])
            fw.op('dve', lambda v, hf=hf: v.tensor_reduce(cf[:, 3, hf * 16:(hf + 1) * 16], oh, AX.X, ALU.add), reads=['stage'], writes=['cf3_%d' % hf])
        fw.op('dve', lambda v: v.scalar_tensor_tensor(cf[:, 4, :], cf[:, 3, :], 128.0, cf[:, 2, :], ALU.mult, ALU.add), reads=['cf3_%d' % hf for hf in range(NCAND // 16)] + ['cf2'], writes=['cf4'])
        fw.op('dve', lambda v: v.tensor_copy(physi[:], cf[:, 4, :]), reads=['cf4'], writes=['physi'])
        fw.op('dve', lambda v: v.tensor_scalar(cf[:, 5, :], cval[:], -1e29, NEG, ALU.is_lt, ALU.mult), reads=cvk, writes=['cf5'])
        fw.op('pool', lambda g: g.memset(acc[:], 0.0), writes=['acc'])
        fw.op('pool', lambda g: g.memset(dens[:, 0, :], 0.0), writes=['dens'])
        for gi_ in range(NCAND // 8):
            gb = 0
            if os.environ.get('K_MULTIGATHER', '0') == '1':
                fw.dma('pool', Kg[gb][:], d_ck[:, :], reads=['physi'], writes=['Kg%d_%d' % (gb, s_) for s_ in range(8)],
                       indirect=dict(out_offset=None, in_offset=bass.IndirectOffsetOnAxis(ap=physi[:, gi_ * 8:gi_ * 8 + 8], axis=0)))
                fw.dma('pool', Vg[gb][:], d_cv[:, :], reads=['physi'], writes=['Vg%d_%d' % (gb, s_) for s_ in range(8)],
                       indirect=dict(out_offset=None, in_offset=bass.IndirectOffsetOnAxis(ap=physi[:, gi_ * 8:gi_ * 8 + 8], axis=0)))
            else:
              for s_ in range(8):
                col = gi_ * 8 + s_
                fw.dma('pool', Kg[gb][:, s_, :], d_ck[:, :], reads=['physi'], writes=['Kg%d_%d' % (gb, s_)],
                       indirect=dict(out_offset=None, in_offset=bass.IndirectOffsetOnAxis(ap=physi[:, col:col + 1], axis=0)))
                fw.dma('pool', Vg[gb][:, s_, :], d_cv[:, :], reads=['physi'], writes=['Vg%d_%d' % (gb, s_)],
                       indirect=dict(out_offset=None, in_offset=bass.IndirectOffsetOnAxis(ap=physi[:, col:col + 1], axis=0)))
            Kk = ['Kg%d_%d' % (gb, s_) for s_ in range(8)]
            Vk = ['Vg%d_%d' % (gb, s_) for s_ in range(8)]
            for h in range(8):
                kvh = h // 4
                fw.op('pool', lambda g, h=h, kvh=kvh: g.tensor_tensor(prod[:], Kg[gb][:, :, kvh * 128:(kvh + 1) * 128],
                                                                      qrep[:, h * 128:(h + 1) * 128].unsqueeze(1).to_broadcast([128, 8, 128]), ALU.mult),
                      reads=Kk + ['qrep'], writes=['prod'])
                fw.op('dve', lambda v, h=h: v.tensor_reduce(sc8[:, h, :], prod[:], AX.X, ALU.add), reads=['prod'], writes=['sc8_%d' % h])
            sck8 = ['sc8_%d' % h for h in range(8)]
            fw.op('dve', lambda v: v.scalar_tensor_tensor(sc8[:], sc8[:], scale, cf[:, 5, gi_ * 8:(gi_ + 1) * 8].unsqueeze(1).to_broadcast([128, 8, 8]), ALU.mult, ALU.add),
                  reads=sck8 + ['cf5'], writes=['sc8'])
            fw.op('act', lambda a: a.activation(pe8[:], sc8[:], AF.Exp), reads=['sc8'], writes=['pe8'])
            fw.op('dve', lambda v: v.tensor_reduce(dens[:, 1, :], pe8[:], AX.X, ALU.add), reads=['pe8'], writes=['dens1'])
            fw.op('dve', lambda v: v.tensor_tensor(dens[:, 0, :], dens[:, 0, :], dens[:, 1, :], ALU.add), reads=['dens', 'dens1'], writes=['dens'])
            for s_ in range(8):
                vv = Vg[gb][:, s_, :].rearrange("p (k d) -> p k d", k=2).unsqueeze(2).to_broadcast([128, 2, 4, 128])
                pp = pe8[:, :, s_].rearrange("p (k g) -> p k g", k=2).unsqueeze(3).to_broadcast([128, 2, 4, 128])
                p4 = prod[:].rearrange("p (k g) d -> p k g d", k=2)
                fw.op('dve', lambda v: v.tensor_tensor(p4, vv, pp, ALU.mult), reads=Vk + ['pe8', 'prod'] + sck8, writes=['prod'])
                fw.op('pool', lambda g: g.tensor_tensor(acc[:], acc[:], prod[:].rearrange("p h d -> p (h d)"), ALU.add), reads=['acc', 'prod'], writes=['acc'])
        for h in range(2):
            fw.op('pe', lambda t, h=h: t.matmul(pA[0][0:NS, h * 512:(h + 1) * 512], cs["repT"][:], acc[:, h * 512:(h + 1) * 512], start=True, stop=True),
                  reads=['cs_repT', 'acc'], writes=[pAk[0][h]])
        fw.op('pe', lambda t: t.matmul(pN[0:NS, 0:8], cs["repT"][:], dens[:, 0, :], start=True, stop=True), reads=['cs_repT', 'dens'], writes=['pN'])
        selj = cs["selj"][:].rearrange("p (j q) -> p j q", j=4)
        for j in range(4):
            fw.op('pe', lambda t, j=j: t.matmul(pA[1][0:NS, j * 256:(j + 1) * 256], selj[:, j, :], kfs[:, 0:256], start=True, stop=True),
                  reads=['cs_selj', 'kfs_k'], writes=[pAk[1][j // 2]])
        fw.op('act', lambda a: a.copy(k2rep[:].rearrange("p j d -> p (j d)"), pA[1][0:NS, :]), reads=['pA1a', 'pA1b'], writes=['k2rep'] + ['Kg0_%d' % s_ for s_ in range(8)])
        for j in range(4):
            fw.op('pe', lambda t, j=j: t.matmul(pA[1][0:NS, j * 256:(j + 1) * 256], selj[:, j, :], kfs[:, 256:512], start=True, stop=True),
                  reads=['cs_selj', 'kfs_v'], writes=[pAk[1][j // 2]])
        fw.op('act', lambda a: a.copy(v2rep[:].rearrange("p j d -> p (j d)"), pA[1][0:NS, :]), reads=['pA1a', 'pA1b'], writes=['v2rep'] + ['Vg0_%d' % s_ for s_ in range(8)])
        for h in range(8):
            kvh = h // 4
            fw.op('pool', lambda g, h=h, kvh=kvh: g.tensor_tensor(prod[0:NS, 0:4, :], k2rep[:, :, kvh * 128:(kvh + 1) * 128],
                                                                  qf[0:NS, h * 128:(h + 1) * 128].unsqueeze(1).to_broadcast([NS, 4, 128]), ALU.mult),
                  reads=['k2rep', 'qf', 'prod'], writes=['prod'])
            fw.op('dve', lambda v, h=h: v.tensor_reduce(scn[:, 0, h, :], prod[0:NS, 0:4, :], AX.X, ALU.add), reads=['prod'], writes=['scn_%d' % h])
        scnk = ['scn_%d' % h for h in range(8)]
        fw.op('dve', lambda v: v.scalar_tensor_tensor(scn[:, 0], scn[:, 0], scale, scn[:, 2, 0, :].unsqueeze(1).to_broadcast([NS, 8, 4]), ALU.mult, ALU.add),
              reads=scnk + ['nbias'], writes=['scn'])
        fw.op('act', lambda a: a.activation(scn[:, 1], scn[:, 0], AF.Exp), reads=['scn'], writes=['pn'])
        fw.op('dve', lambda v: v.tensor_reduce(small[0:NS, 40:48], scn[:, 1], AX.X, ALU.add), reads=['pn'], writes=['dnew'])
        fw.op('dve', lambda v: v.tensor_tensor(small[0:NS, 40:48], small[0:NS, 40:48], pN[0:NS, 0:8], ALU.add), reads=['dnew', 'pN'], writes=['dnew'])
        fw.op('dve', lambda v: v.reciprocal(small[0:NS, 48:56], small[0:NS, 40:48]), reads=['dnew'], writes=['rdn'])
        fw.op('act', lambda a: a.copy(accn[:], pA[0][0:NS, :]), reads=['pA0a', 'pA0b'], writes=['accn'])
        for j in range(4):
            vv = v2rep[:, j, :].rearrange("p (k d) -> p k d", k=2).unsqueeze(2).to_broadcast([NS, 2, 4, 128])
            pp = scn[:, 1, :, j].rearrange("p (k g) -> p k g", k=2).unsqueeze(3).to_broadcast([NS, 2, 4, 128])
            p4 = prod[0:NS].rearrange("p (k g) d -> p k g d", k=2)
            fw.op('dve', lambda v: v.tensor_tensor(p4, vv, pp, ALU.mult), reads=['v2rep', 'pn', 'prod'], writes=['prod'])
            fw.op('dve', lambda v: v.tensor_tensor(accn[:], accn[:], prod[0:NS].rearrange("p h d -> p (h d)"), ALU.add), reads=['accn', 'prod'], writes=['accn'])
        a3 = accn[:].rearrange("p (h d) -> p h d", d=128)
        fw.op('dve', lambda v: v.tensor_tensor(a3, a3, small[0:NS, 48:56].unsqueeze(2).to_broadcast([NS, 8, 128]), ALU.mult), reads=['accn', 'rdn'], writes=['accn'])
        fw.op('dve', lambda v: v.tensor_tensor(ogb[:], accn[:], sgs[:], ALU.mult), reads=['accn', 'sgs'], writes=['ogb'])
        for c in range(8):
            fw.op('pe', lambda t_, c=c: t_.transpose(pT[:, c, 0:NS], ogb[:, c * 128:(c + 1) * 128], identb[0:NS, 0:NS]), reads=['ogb', 'identb'], writes=['pT'])
        fw.op('dve', lambda v: v.tensor_copy(hT[:, :, 0:NS], pT[:, :, 0:NS]), reads=['pT'], writes=hTk)
        proj_s(pA[0], pAk[0], [slab(s_wout0, i * 512, 512, k_wout0) for i in range(2)])
        fw.op('dve', lambda v: v.tensor_tensor(accn[:], pA[0][0:NS, :], gate_s[:, 0, :], ALU.mult), reads=['pA0a', 'pA0b', 'gate_s0_0', 'gate_s0_1'], writes=['accn'])
        fw.op('dve', lambda v: v.tensor_tensor(xs_t[:], xs_t[:], accn[:], ALU.add), reads=['accn', 'xs_t'], writes=['xs_t'])

        norm_tile(xs_t[:], 'xs_t', 'xn', n=NS)
        to_hT_s(1)
        fw.dma('sp', tok12[0:12, :], d_sconv, writes=['accn'])
        for c in range(8):
            fw.op('pe', lambda t_, c=c: t_.transpose(pS[:, c * 12:(c + 1) * 12], tok12[0:12, c * 128:(c + 1) * 128], identf[0:12, 0:12]), reads=['accn', 'identf'], writes=['pS'])
        fw.op('dve', lambda v: v.tensor_copy(xbs[:, :, :, 0:3], pS[:, 0:96].rearrange("p (c b j) -> p c b j", c=8, b=4)), reads=['pS'], writes=['xbs_h'])
        fw.dma('sp', tok12[0:4, :], d_sh, reads=['accn'], writes=['accn'])
        for c in range(8):
            fw.op('pe', lambda t_, c=c: t_.transpose(pS[:, c * 4:(c + 1) * 4], tok12[0:4, c * 128:(c + 1) * 128], identf[0:4, 0:4]), reads=['accn', 'identf'], writes=['pS'])
        fw.op('dve', lambda v: v.tensor_copy(hst[:, 0], pS[:, 0:32].rearrange("p (c b) -> p c b", c=8)), reads=['pS'], writes=['hst'])
        slabs = [slab(s_lwin, i * 512, 512, k_lwin) for i in range(2)]
        for m in range(8):
            rt, rk = slabs[m // 4]
            for kc in range(8):
                fw.op('pe', lambda t_, kc=kc, rt=rt, m=m: t_.matmul(pFd[:, m * 16:(m + 1) * 16], rt[:, kc, (m % 4) * 128:(m % 4 + 1) * 128], hT[:, kc, 0:NS],
                                                                  start=(kc == 0), stop=(kc == 7)),
                      reads=[hTk[kc], rk], writes=['pF0'])
        fw.op('dve', lambda v: v.tensor_copy(xbs[:, :, :, 3:7], pFd[:, 0:128].rearrange("p (c b t) -> p c b t", c=8, b=4)), reads=['pF0'], writes=['xbs_x'])
        slabs = [slab(s_lwin, 1024 + i * 512, 512, k_lwin) for i in range(2)]
        for m in range(8):
            rt, rk = slabs[m // 4]
            for kc in range(8):
                fw.op('pe', lambda t_, kc=kc, rt=rt, m=m: t_.matmul(pFd[:, m * 16:(m + 1) * 16], rt[:, kc, (m % 4) * 128:(m % 4 + 1) * 128], hT[:, kc, 0:NS],
                                                                  start=(kc == 0), stop=(kc == 7)),
                      reads=[hTk[kc], rk], writes=['pF0'])
        fw.op('act', lambda a: a.activation(sg1s[:].rearrange("p c t -> p (c t)"), pFd[:, 0:128], AF.Silu), reads=['pF0'], writes=['sg1s'])
        lk = ['lvec%d' % i for i in range(8)]
        x4 = xcs[:, 0].rearrange("p c (b t) -> p c b t", b=4)
        t4 = xcs[:, 1].rearrange("p c (b t) -> p c b t", b=4)

        def wb(j):
            return lvec[:, j, :].unsqueeze(2).unsqueeze(3).to_broadcast([128, 8, 4, 4])
        fw.op('dve', lambda v: v.tensor_tensor(x4, xbs[:, :, :, 0:4], wb(0), ALU.mult), reads=['xbs_h', 'xbs_x'] + lk, writes=['xcs'])
        for j in range(1, 4):
            fw.op('dve', lambda v, j=j: v.tensor_tensor(t4, xbs[:, :, :, j:j + 4], wb(j), ALU.mult), reads=['xbs_h', 'xbs_x'] + lk, writes=['xcs_t'])
            fw.op('dve', lambda v: v.tensor_tensor(x4, x4, t4, ALU.add), reads=['xcs', 'xcs_t'], writes=['xcs'])
        fw.op('dve', lambda v: v.tensor_tensor(x4, x4, wb(4), ALU.add), reads=['xcs'] + lk, writes=['xcs'])
        fw.op('act', lambda a: a.copy(xcbs[:], xcs[:, 0]), reads=['xcs'], writes=['xcbs'])
        fw.op('dve', lambda v: v.tensor_copy(cso[:].rearrange("p c (b j) -> p c b j", b=4), xbs[:, :, :, 4:7]), reads=['xbs_h', 'xbs_x'], writes=['cso'])
        for c in range(8):
            fw.op('pe', lambda t_, c=c: t_.transpose(pA[0][0:12, c * 128:(c + 1) * 128], cso[:, c, :], identf[:]), reads=['cso', 'identf'], writes=[pAk[0][c // 4]])
        fw.op('act', lambda a: a.copy(tok12[0:12, :], pA[0][0:12, :]), reads=['pA0a', 'pA0b', 'accn'], writes=['accn'])
        fw.dma('pool', conv_s, tok12[0:12, :], reads=['accn'], writes=['o_convs'])
        for g_ in range(2):
            for m in range(8):
                nb, mo = m // 2, m % 2
                for ki in range(2):
                    fw.op('pe', lambda t_, g_=g_, ki=ki, mo=mo, nb=nb, m=m: t_.matmul(pFd[:, m * 16:(m + 1) * 16], wab[:, g_, nb * 2 + ki, mo * 128:(mo + 1) * 128],
                                                                                   xcbs[:, nb * 2 + ki, :], start=(ki == 0), stop=(ki == 1)),
                          reads=['wab%d' % g_, 'xcbs'], writes=['pF0'])
            fw.op('dve', lambda v, g_=g_: v.tensor_tensor(rg[:, g_], pFd[:, 0:128].rearrange("p (c t) -> p c t", c=8),
                                                        lvec[:, 5 + g_, :].unsqueeze(2).to_broadcast([128, 8, NS]), ALU.add), reads=['pF0'] + lk, writes=['rg%d' % g_])
            fw.op('act', lambda a, g_=g_: a.activation(rg[:, g_], rg[:, g_], AF.Sigmoid), reads=['rg%d' % g_], writes=['rg%d' % g_])
        fw.op('dve', lambda v: v.tensor_tensor(rg[:, 2], rg[:, 0], lc8[:, 0, :].unsqueeze(2).to_broadcast([128, 8, NS]), ALU.mult), reads=['rg0', 'lc8'], writes=['rg2'])
        fw.op('act', lambda a: a.activation(rg[:, 3], rg[:, 2], AF.Exp), reads=['rg2'], writes=['rg3'])
        fw.op('act', lambda a: a.activation(rg[:, 4], rg[:, 2], AF.Exp, scale=2.0), reads=['rg2'], writes=['rg4'])
        fw.op('act', lambda a: a.activation(rg[:, 4], rg[:, 4], AF.Sqrt, bias=onec[:, 0:1], scale=-1.0), reads=['rg4', 'onec'], writes=['rg4'])
        fw.op('dve', lambda v: v.tensor_tensor(rg[:, 5], rg[:, 1], xcs[:, 0], ALU.mult), reads=['rg1', 'xcs'], writes=['rg5'])
        fw.op('dve', lambda v: v.tensor_tensor(rg[:, 5], rg[:, 5], rg[:, 4], ALU.mult), reads=['rg5', 'rg4'], writes=['rg5'])
        a4 = rg[:, 3].rearrange("p c (b t) -> p c b t", b=4)
        b4 = rg[:, 5].rearrange("p c (b t) -> p c b t", b=4)
        for t in range(4):
            hprev = hst[:, 0] if t == 0 else hss[:, :, :, t - 1]
            fw.op('dve', lambda v, t=t, hprev=hprev: v.tensor_tensor(hst[:, 1], a4[:, :, :, t], hprev, ALU.mult), reads=['rg3', 'hst', 'hss'], writes=['hst1'])
            fw.op('dve', lambda v, t=t: v.tensor_tensor(hss[:, :, :, t], hst[:, 1], b4[:, :, :, t], ALU.add), reads=['hst1', 'rg5'], writes=['hss'])
        fw.op('dve', lambda v: v.tensor_copy(hst[:, 0], hss[:, :, :, 3]), reads=['hss', 'hst1'], writes=['hst'])
        for c in range(8):
            fw.op('pe', lambda t_, c=c: t_.transpose(pA[1][0:4, c * 128:(c + 1) * 128], hst[:, 0, c, :], identf[:]), reads=['hst', 'identf'], writes=[pAk[1][c // 4]])
        fw.op('act', lambda a: a.copy(tok12[0:4, :], pA[1][0:4, :]), reads=['pA1a', 'pA1b', 'accn'], writes=['accn'])
        fw.dma('pool', h_s, tok12[0:4, :], reads=['accn'], writes=['o_hs'])
        fw.op('dve', lambda v: v.tensor_tensor(hT[:, :, 0:NS], hss[:].rearrange("p c b t -> p c (b t)"), sg1s[:], ALU.mult), reads=['hss', 'sg1s'], writes=hTk)
        proj_s(pA[0], pAk[0], [slab(s_lwout, i * 512, 512, k_lwout) for i in range(2)])
        fw.op('dve', lambda v: v.tensor_tensor(accn[:], pA[0][0:NS, :], gate_s[:, 1, :], ALU.mult), reads=['pA0a', 'pA0b', 'gate_s1_0', 'gate_s1_1'], writes=['accn'])
        fw.op('dve', lambda v: v.tensor_tensor(xs_t[:], xs_t[:], accn[:], ALU.add), reads=['accn', 'xs_t'], writes=['xs_t'])
        fw.dma('pool', y_s, xs_t[:], reads=['xs_t'], writes=['o_ys'])
        fw.barrier()
        st.close()

    setup_l1()
    if STAGE >= 4:
        sample_path()

    maskw = sb("maskw", [128, 16, 128], BF16)
    fw.dma('sp', stage[:], cst["maskw"], writes=['stage'])
    fw.op('dve', lambda v: v.tensor_copy(maskw[:].rearrange("p r q -> p (r q)"), stage[:]), reads=['stage'], writes=['maskw'])
    gate_bc = [sb("gate_bc0", [128, D]), sb("gate_bc1", [128, D])]
    KT = sb("KT", [128, 2, SEQ], BF16)
    Vr = sb("Vr", [128, NT, 256], BF16)
    KIT2 = sb("KIT2", [128, SEQ], BF16)
    xt = [sb("xt0", [128, 2, D]), sb("xt1", [128, 2, D])]
    qT = sb("qT", [128, 2, 2, 512], BF16)
    qiT = sb("qiT", [128, 2, 16, 128], BF16)
    fw.op('pool', lambda g: g.memset(qiT[:], 0.0), writes=['qiT0', 'qiTb0', 'qiT1', 'qiTb1'])
    sgT = sb("sgT", [128, 8, 256], BF16)
    ogT = sb("ogT", [128, 8, 256], BF16)
    wiT = sb("wiT", [16, 256], BF16)
    wall = sb("wall", [128, 16, 128], BF16)
    s1sb = sb("s1sb", [128, 128], BF16)
    biasb = [sb("biasm0", [128, SEQ], BF16), sb("biasm1", [128, SEQ], BF16)]
    scb = [stage, sb("scores1", [128, SEQ])]
    Rb = [sb("Rb%d" % i, [128, 512], BF16) for i in range(2)]
    Pb = [sb("Pb%d" % i, [128, 512], BF16) for i in range(2)]
    rden = sb("rden", [128, 512])
    otmp = sb("otmp", [128, 512], BF16)
    xbT = sb("xbT", [128, 8, 259])
    hstate = sb("hstate", [128, 8])
    xc2 = sb("xc2", [128, 2, 256])
    xcb2 = sb("xcb2", [128, 2, 256], BF16)
    gbuf = sb("gbuf", [128, 6, 256])
    for s in range(NSEQ):
        if STAGE >= 1:
            phase1(s)
        if STAGE >= 2:
            phase2(s)
    fw.finish('sp')
    print("ops", fw.nops, "waits", fw.nwaits, "sbuf_left", nc.sbuf_bytes_remaining)
    return nc


_CACHE = {}


def kernel(**inputs):
    x_prompt = np.asarray(inputs["x_prompt"], np.float32)
    B = x_prompt.shape[0]
    ncores = NCORES
    consts = host_consts()
    if "nc" not in _CACHE:
        _CACHE["nc"] = build_program()
    nc = _CACHE["nc"]
    g = lambda k: np.ascontiguousarray(np.asarray(inputs[k]))
    shared = {
        "norm_g": g("norm_g"), "ada_w": g("ada_w"), "ada_b": g("ada_b"),
        "attn_w_in": g("attn_w_in")[0], "q_norm": g("attn_q_norm"), "k_norm": g("attn_k_norm"),
        "attn_w_out": g("attn_w_out")[0], "lru_w_in": g("lru_w_in")[0], "conv_w": g("lru_conv_w")[0],
        "conv_b": g("lru_conv_b"), "w_a": g("lru_w_a")[0].reshape(1024, 256), "b_a": g("lru_b_a"),
        "w_x": g("lru_w_x")[0].reshape(1024, 256), "b_x": g("lru_b_x"), "lam": g("lru_lam"),
        "lru_w_out": g("lru_w_out")[0],
    }
    for k, v in consts.items():
        shared["c_" + k] = v
    c_prompt = g("c_prompt")
    c_sample = g("c_sample")
    x_sample = g("x_sample")
    cik = g("cache_idx_k")[0].reshape(5120 * 8, 1024)
    ck = g("cache_k")[0].reshape(5120 * 128, 256)
    cv = g("cache_v")[0].reshape(5120 * 128, 256)
    page_table = np.asarray(inputs["page_table"]).astype(np.int32)
    state_conv = g("state_conv")[0]
    state_h = g("state_h")[0]
    core_ids = list(range(ncores))
    if os.environ.get("K_ONECORE"):
        core_ids = [0]
    in_maps = []
    for c in core_ids:
        m = dict(shared)
        m["xp"] = np.ascontiguousarray(x_prompt[2 * c:2 * c + 2].reshape(NSEQ * SEQ, D))
        m["c6"] = np.ascontiguousarray(np.concatenate([c_prompt[2 * c:2 * c + 2], c_sample[4 * c:4 * c + 4]], axis=0))
        if STAGE < 4:
            in_maps.append(m)
            continue
        m["xs"] = np.ascontiguousarray(x_sample[4 * c:4 * c + 4].reshape(16, D))
        m["cik"] = cik
        m["ck"] = ck
        m["cv"] = cv
        ptc = page_table[4 * c:4 * c + 4]
        m["ptT"] = np.ascontiguousarray(ptc.T)
        m["pt128"] = np.ascontiguousarray(ptc[(np.arange(128) % 16) // 4])
        m["sconv"] = np.ascontiguousarray(state_conv[4 * c:4 * c + 4].reshape(12, D))
        m["sh"] = np.ascontiguousarray(state_h[4 * c:4 * c + 4])
        in_maps.append(m)
    res = run_bass_kernel_spmd(nc, in_maps, core_ids=core_ids, trace=bool(os.environ.get('K_TRACE')))
    if os.environ.get('K_TRACE'):
        print('EXEC_TIME_NS', res.exec_time_ns)
    R = res.results
    nco = len(core_ids)
    y_prompt = np.concatenate([r["y_p"].reshape(NSEQ, SEQ, D) for r in R], axis=0)
    k_prompt = np.concatenate([r["k_p"].reshape(NSEQ, SEQ, 2, 128) for r in R], axis=0)[None]
    v_prompt = np.concatenate([r["v_p"].reshape(NSEQ, SEQ, 2, 128) for r in R], axis=0)[None]
    ik_prompt = np.concatenate([r["ik_p"].reshape(NSEQ, SEQ, 64) for r in R], axis=0)[None]
    conv_prompt = np.concatenate([r["conv_p"].reshape(NSEQ, 3, D) for r in R], axis=0)[None]
    h_prompt = np.concatenate([r["h_p"].reshape(NSEQ, D) for r in R], axis=0)[None]
    if STAGE < 4:
        Bd = 4 * nco
        z = lambda *s_: np.zeros(s_, np.float32)
        return (y_prompt, z(Bd, 4, D), k_prompt, v_prompt, ik_prompt, z(1, Bd, 4, 2, 128), z(1, Bd, 4, 2, 128), z(1, Bd, 4, 64),
                conv_prompt, h_prompt, z(1, Bd, 3, D), z(1, Bd, D))
    y_sample = np.concatenate([r["y_s"].reshape(4, 4, D) for r in R], axis=0)
    k_sample = np.concatenate([r["k_s"].reshape(4, 4, 2, 128) for r in R], axis=0)[None]
    v_sample = np.concatenate([r["v_s"].reshape(4, 4, 2, 128) for r in R], axis=0)[None]
    ik_sample = np.concatenate([r["ik_s"].reshape(4, 4, 64) for r in R], axis=0)[None]
    conv_sample = np.concatenate([r["conv_s"].reshape(4, 3, D) for r in R], axis=0)[None]
    h_sample = np.concatenate([r["h_s"].reshape(4, D) for r in R], axis=0)[None]
    return (y_prompt, y_sample, k_prompt, v_prompt, ik_prompt, k_sample, v_sample, ik_sample,
            conv_prompt, h_prompt, conv_sample, h_sample)
```
